# Optimizing a Trainium2 kernel written in Bass

```python
import jax, jax.numpy as jnp
from jax import lax
import numpy as np

D_MODEL = 1024
BATCH = 4
SEQ = 4096
DEPTH = 1
DEC_BATCH = 128
DEC_SEQ = 1
PAST_LEN = 8192
PAGE_SIZE = 128

N_HEADS = 8
N_KV_HEADS = 2
HEAD_DIM = 64
ATTN_WIDTH = N_HEADS * HEAD_DIM
KV_WIDTH = N_KV_HEADS * HEAD_DIM
WINDOW = 128
BLOCK = 128
ROPE_THETA = 10000.0
ATTN_SCALE = HEAD_DIM ** -0.5
CONV_CH = D_MODEL - ATTN_WIDTH
CONV_K = 31
MIX_WIDTH = ATTN_WIDTH + CONV_CH
Q_END = ATTN_WIDTH
K_END = Q_END + KV_WIDTH
V_END = K_END + KV_WIDTH
IN_COLS = V_END + 2 * CONV_CH
D_FF = 2816
FFN_CONV_K = 3
PLE_DIM = 256
EPS = 1e-6
NEG_INF = -1e30

kernel_name = 'hymba_swa_sink_conformer_convffn_step'


def rms_norm(x, g):
    xf = x.astype(jnp.float32)
    y = xf * lax.rsqrt(jnp.mean(xf * xf, axis=-1, keepdims=True) + EPS)
    return (y * g.astype(jnp.float32)).astype(x.dtype)


def layer_norm(x, g, b):
    xf = x.astype(jnp.float32)
    xc = xf - jnp.mean(xf, axis=-1, keepdims=True)
    var = jnp.mean(xc * xc, axis=-1, keepdims=True)
    y = xc * lax.rsqrt(var + EPS) * g.astype(jnp.float32) + b.astype(jnp.float32)
    return y.astype(x.dtype)


def rope(x, pos):
    half = HEAD_DIM // 2
    inv_freq = ROPE_THETA ** (-jnp.arange(half, dtype=jnp.float32) / half)
    ang = pos.astype(jnp.float32)[:, None] * inv_freq[None, :]
    cos = jnp.cos(ang)[None, :, None, :]
    sin = jnp.sin(ang)[None, :, None, :]
    xf = x.astype(jnp.float32)
    x1, x2 = xf[..., :half], xf[..., half:]
    return jnp.concatenate([x1 * cos - x2 * sin, x2 * cos + x1 * sin], axis=-1).astype(x.dtype)


def causal_dwconv(x_ext, w, b):
    out = lax.conv_general_dilated(
        x_ext, w[:, None, :].astype(x_ext.dtype), window_strides=(1,), padding='VALID',
        dimension_numbers=('NWC', 'WIO', 'NWC'), feature_group_count=x_ext.shape[-1])
    return out + b.astype(out.dtype)


def sink_softmax_weights(s, mask, sink):
    s = jnp.where(mask, s, NEG_INF)
    m = jnp.maximum(jnp.max(s, axis=-1), sink)
    e = jnp.exp(s - m[..., None])
    denom = jnp.sum(e, axis=-1) + jnp.exp(sink - m)
    return e / denom[..., None]


def banded_swa(q, k, v, sinks):
    B, S = q.shape[0], q.shape[1]
    nb = S // BLOCK
    G = N_HEADS // N_KV_HEADS
    qb = q.reshape(B, nb, BLOCK, N_KV_HEADS, G, HEAD_DIM)
    kb = k.reshape(B, nb, BLOCK, N_KV_HEADS, HEAD_DIM)
    vb = v.reshape(B, nb, BLOCK, N_KV_HEADS, HEAD_DIM)
    shift = lambda t: jnp.concatenate([jnp.zeros_like(t[:, :1]), t[:, :-1]], axis=1)
    kk = jnp.concatenate([shift(kb), kb], axis=2)
    vv = jnp.concatenate([shift(vb), vb], axis=2)
    s = jnp.einsum('bnqkgd,bnjkd->bnkgqj', qb, kk,
                   preferred_element_type=jnp.float32) * ATTN_SCALE
    i = jnp.arange(BLOCK)[:, None]
    j = jnp.arange(2 * BLOCK)[None, :]
    d = i + BLOCK - j
    band = (d >= 0) & (d <= WINDOW)
    blk = jnp.arange(nb)[:, None, None]
    mask = band[None] & ((blk > 0) | (j[None] >= BLOCK))
    sink = sinks.astype(jnp.float32).reshape(N_KV_HEADS, G)[None, None, :, :, None]
    pr = sink_softmax_weights(s, mask[None, :, None, None], sink)
    o = jnp.einsum('bnkgqj,bnjkd->bnqkgd', pr, vv.astype(jnp.float32))
    return o.reshape(B, S, ATTN_WIDTH).astype(q.dtype)


def decode_swa(q, k, v, cache_k, cache_v, sinks):
    B, T = q.shape[0], q.shape[1]
    G = N_HEADS // N_KV_HEADS
    kk = jnp.concatenate([cache_k.astype(k.dtype), k], axis=1)
    vv = jnp.concatenate([cache_v.astype(v.dtype), v], axis=1)
    qh = q.reshape(B, T, N_KV_HEADS, G, HEAD_DIM)
    s = jnp.einsum('btkgd,bjkd->bkgtj', qh, kk,
                   preferred_element_type=jnp.float32) * ATTN_SCALE
    i = jnp.arange(T)[:, None]
    j = jnp.arange(WINDOW + T)[None, :]
    d = i + WINDOW - j
    mask = (d >= 0) & (d <= WINDOW)
    sink = sinks.astype(jnp.float32).reshape(N_KV_HEADS, G)[None, :, :, None]
    pr = sink_softmax_weights(s, mask, sink)
    o = jnp.einsum('bkgtj,bjkd->btkgd', pr, vv.astype(jnp.float32))
    return o.reshape(B, T, ATTN_WIDTH).astype(q.dtype), kk[:, -WINDOW:], vv[:, -WINDOW:]


def decoder_layer(x, p, pos, kv_cache, conv_hist, ffn_hist, lw):
    (norm_mix_g, w_in, q_norm_g, k_norm_g, attn_sinks, conv_w, conv_b, conv_ln_g,
     conv_ln_b, w_out, norm_ffn_g, w_ffn_in, ffn_conv_w, ffn_conv_b, w_ffn_out,
     norm_ple_g, w_ple_gate, w_ple_proj) = lw
    B, T = x.shape[0], x.shape[1]
    z = rms_norm(x, norm_mix_g) @ w_in
    q, k, v, cu = jnp.split(z, [Q_END, K_END, V_END], axis=-1)
    q = rope(rms_norm(q.reshape(B, T, N_HEADS, HEAD_DIM), q_norm_g), pos)
    k = rope(rms_norm(k.reshape(B, T, N_KV_HEADS, HEAD_DIM), k_norm_g), pos)
    v = v.reshape(B, T, N_KV_HEADS, HEAD_DIM)
    if kv_cache is None:
        attn_o = banded_swa(q, k, v, attn_sinks)
        new_k, new_v = k[:, -WINDOW:], v[:, -WINDOW:]
    else:
        attn_o, new_k, new_v = decode_swa(q, k, v, kv_cache[0], kv_cache[1], attn_sinks)
    a, g = jnp.split(cu, 2, axis=-1)
    glu = a * jax.nn.sigmoid(g)
    conv_ext = jnp.concatenate([conv_hist.astype(glu.dtype), glu], axis=1)
    conv_o = jax.nn.silu(layer_norm(causal_dwconv(conv_ext, conv_w, conv_b), conv_ln_g, conv_ln_b))
    h = x + jnp.concatenate([attn_o, conv_o], axis=-1) @ w_out
    fg, fu = jnp.split(rms_norm(h, norm_ffn_g) @ w_ffn_in, 2, axis=-1)
    ffn_ext = jnp.concatenate([ffn_hist.astype(fg.dtype), fg], axis=1)
    fc = causal_dwconv(ffn_ext, ffn_conv_w, ffn_conv_b)
    h = h + (jax.nn.gelu(fc, approximate=False) * fu) @ w_ffn_out
    gate = jax.nn.sigmoid(rms_norm(h, norm_ple_g) @ w_ple_gate)
    h = h + (p @ w_ple_proj) * gate
    return h, (new_k, new_v, conv_ext[:, -(CONV_K - 1):], ffn_ext[:, -(FFN_CONV_K - 1):])


def setup_inputs(seed: int = 0) -> dict:
    key = jax.random.key(seed)
    ks = jax.random.split(key, 32)
    f32 = jnp.float32
    nrm = lambda k, shape, scale: jax.random.normal(k, shape, f32) * scale
    gain = lambda k, n: 1.0 + 0.01 * jax.random.normal(k, (DEPTH, n), f32)
    return {
        'x_prompt': nrm(ks[0], (BATCH, SEQ, D_MODEL), 1.0),
        'x_sample': nrm(ks[1], (DEC_BATCH, DEC_SEQ, D_MODEL), 1.0),
        'state_attn_k': nrm(ks[2], (DEPTH, DEC_BATCH, WINDOW, N_KV_HEADS, HEAD_DIM), 1.0),
        'state_attn_v': nrm(ks[3], (DEPTH, DEC_BATCH, WINDOW, N_KV_HEADS, HEAD_DIM), 1.0),
        'state_conv': nrm(ks[4], (DEPTH, DEC_BATCH, CONV_K - 1, CONV_CH), 0.5),
        'state_ffn_conv': nrm(ks[5], (DEPTH, DEC_BATCH, FFN_CONV_K - 1, D_FF), 1.0),
        'p_prompt': nrm(ks[6], (DEPTH, BATCH, SEQ, PLE_DIM), 1.0),
        'p_sample': nrm(ks[7], (DEPTH, DEC_BATCH, DEC_SEQ, PLE_DIM), 1.0),
        'norm_mix_g': gain(ks[8], D_MODEL),
        'w_in': nrm(ks[9], (DEPTH, D_MODEL, IN_COLS), D_MODEL ** -0.5),
        'q_norm_g': gain(ks[10], HEAD_DIM),
        'k_norm_g': gain(ks[11], HEAD_DIM),
        'attn_sinks': nrm(ks[12], (DEPTH, N_HEADS), 0.5),
        'conv_w': nrm(ks[13], (DEPTH, CONV_K, CONV_CH), CONV_K ** -0.5),
        'conv_b': nrm(ks[14], (DEPTH, CONV_CH), 0.01),
        'conv_ln_g': gain(ks[15], CONV_CH),
        'conv_ln_b': nrm(ks[16], (DEPTH, CONV_CH), 0.01),
        'w_out': nrm(ks[17], (DEPTH, MIX_WIDTH, D_MODEL), MIX_WIDTH ** -0.5),
        'norm_ffn_g': gain(ks[18], D_MODEL),
        'w_ffn_in': nrm(ks[19], (DEPTH, D_MODEL, 2 * D_FF), D_MODEL ** -0.5),
        'ffn_conv_w': nrm(ks[20], (DEPTH, FFN_CONV_K, D_FF), FFN_CONV_K ** -0.5),
        'ffn_conv_b': nrm(ks[21], (DEPTH, D_FF), 0.01),
        'w_ffn_out': nrm(ks[22], (DEPTH, D_FF, D_MODEL), D_FF ** -0.5),
        'norm_ple_g': gain(ks[23], D_MODEL),
        'w_ple_gate': nrm(ks[24], (DEPTH, D_MODEL, D_MODEL), D_MODEL ** -0.5),
        'w_ple_proj': nrm(ks[25], (DEPTH, PLE_DIM, D_MODEL), PLE_DIM ** -0.5),
    }


def reference(x_prompt, x_sample, state_attn_k, state_attn_v, state_conv, state_ffn_conv,
              p_prompt, p_sample, norm_mix_g, w_in, q_norm_g, k_norm_g, attn_sinks,
              conv_w, conv_b, conv_ln_g, conv_ln_b, w_out, norm_ffn_g, w_ffn_in,
              ffn_conv_w, ffn_conv_b, w_ffn_out, norm_ple_g, w_ple_gate, w_ple_proj):
    bp, sp = x_prompt.shape[0], x_prompt.shape[1]
    ts = x_sample.shape[1]
    pos_prompt = jnp.arange(sp, dtype=jnp.int32)
    pos_sample = PAST_LEN + jnp.arange(ts, dtype=jnp.int32)
    conv_hist_p = jnp.zeros((bp, CONV_K - 1, CONV_CH), x_prompt.dtype)
    ffn_hist_p = jnp.zeros((bp, FFN_CONV_K - 1, D_FF), x_prompt.dtype)
    hp, hs = x_prompt, x_sample
    kp, vp, cp, fp = [], [], [], []
    ksl, vsl, csl, fsl = [], [], [], []
    for i in range(DEPTH):
        lw = (norm_mix_g[i], w_in[i], q_norm_g[i], k_norm_g[i], attn_sinks[i], conv_w[i],
              conv_b[i], conv_ln_g[i], conv_ln_b[i], w_out[i], norm_ffn_g[i], w_ffn_in[i],
              ffn_conv_w[i], ffn_conv_b[i], w_ffn_out[i], norm_ple_g[i], w_ple_gate[i],
              w_ple_proj[i])
        hp, st_p = decoder_layer(hp, p_prompt[i], pos_prompt, None, conv_hist_p, ffn_hist_p, lw)
        hs, st_s = decoder_layer(hs, p_sample[i], pos_sample,
                                 (state_attn_k[i], state_attn_v[i]),
                                 state_conv[i], state_ffn_conv[i], lw)
        kp.append(st_p[0]); vp.append(st_p[1]); cp.append(st_p[2]); fp.append(st_p[3])
        ksl.append(st_s[0]); vsl.append(st_s[1]); csl.append(st_s[2]); fsl.append(st_s[3])
    return (hp, hs,
            jnp.stack(kp), jnp.stack(vp), jnp.stack(cp), jnp.stack(fp),
            jnp.stack(ksl), jnp.stack(vsl), jnp.stack(csl), jnp.stack(fsl))
```

```python
import math
from contextlib import ExitStack

import numpy as np
import concourse.bass as bass
import concourse.mybir as mybir
from concourse.bass_utils import run_bass_kernel_spmd

F32 = mybir.dt.float32
BF16 = mybir.dt.bfloat16
I32 = mybir.dt.int32
AF = mybir.ActivationFunctionType
ALU = mybir.AluOpType

D_MODEL = 1024
SEQ = 4096
NB_CORE = 16
HD = 64
WINDOW = 128
CONV_CH = 512
CONV_K = 31
D_FF = 2816
NJ = D_FF // 128
PLE = 256
EPS = 1e-6
PAST_LEN = 8192
NS = 16
TWO_PI = 2.0 * math.pi
CW1 = 6.28125
CW2 = TWO_PI - CW1

C_GMIX, C_GFFN, C_GPLE = 0, 8, 16
C_GQ, C_GQR, C_GK, C_GKR, C_SGN = 24, 25, 26, 27, 28
C_CONVB, C_LNG, C_LNB = 33, 37, 41
C_FLAG, C_POSB = 45, 46
C_INVF = 29
C_FCB = 47
C_FCW = 69
C_CONVW = 135
C_SINKC = 135 + 4 * 31
NCOL = C_SINKC + 4

PASS_BLOCKS = [4, 4, 4, 4]
SLOT = 2816
NSLOT = 4


class Op:
    __slots__ = ("eng", "fn", "reads", "writes", "dma", "deps", "tok", "inc", "idx")

    def __init__(self, eng, fn, reads, writes, dma):
        self.eng = eng
        self.fn = fn
        self.reads = tuple(reads)
        self.writes = tuple(writes)
        self.dma = dma
        self.deps = ()
        self.tok = None
        self.inc = False
        self.idx = -1


class Prog:
    ENGS = ("pe", "act", "dve", "pool", "sp")

    def __init__(self, nc, dma_pool=8):
        self.nc = nc
        self.ops = []
        self.dma_pool = dma_pool

    def op(self, eng, fn, reads=(), writes=(), dma=False):
        o = Op(eng, fn, reads, writes, dma)
        o.idx = len(self.ops)
        self.ops.append(o)
        return o

    def pe(self, fn, reads=(), writes=()):
        return self.op("pe", fn, reads, writes)

    def act(self, fn, reads=(), writes=()):
        return self.op("act", fn, reads, writes)

    def dve(self, fn, reads=(), writes=()):
        return self.op("dve", fn, reads, writes)

    def pool(self, fn, reads=(), writes=()):
        return self.op("pool", fn, reads, writes)

    def dma(self, eng, out, in_, reads=(), writes=(), **kw):
        def fn(e, out=out, in_=in_, kw=kw):
            return e.dma_start(out=out, in_=in_, **kw)
        return self.op(eng, fn, reads, writes, dma=True)

    def finalize(self, stack):
        nc = self.nc
        ops = self.ops
        last_w = {}
        readers = {}

        def is_ps(k):
            return isinstance(k, tuple) and k[0] == "ps"
        for o in ops:
            deps = {}
            for k in o.reads:
                w = last_w.get(k)
                if w is not None:
                    deps[w.idx] = w
                if is_ps(k):
                    for r in readers.get(k, ()):
                        if r.eng != o.eng:
                            deps[r.idx] = r
            for k in o.writes:
                w = last_w.get(k)
                if w is not None:
                    deps[w.idx] = w
                for r in readers.get(k, ()):
                    deps[r.idx] = r
            deps.pop(o.idx, None)
            best = {}
            keep = []
            for d in deps.values():
                if d.dma:
                    keep.append(d)
                else:
                    b = best.get(d.eng)
                    if b is None or d.idx > b.idx:
                        best[d.eng] = d
            for e, d in best.items():
                if e == o.eng and not o.dma and e == "pe":
                    continue
                keep.append(d)
            o.deps = keep
            for d in keep:
                d.inc = True
            for k in o.reads:
                readers.setdefault(k, []).append(o)
            for k in o.writes:
                last_w[k] = o
                readers[k] = []
        self.eng_sem = {e: stack.enter_context(nc.semaphore("s_" + e)) for e in self.ENGS}
        self.pool_sems = {}
        cnt = {e: 0 for e in self.ENGS}
        dma_i = {e: 0 for e in self.ENGS}
        dma_hist = {e: [] for e in self.ENGS}
        pool_cnt = {}
        for o in ops:
            if o.dma:
                i = dma_i[o.eng]
                dma_i[o.eng] += 1
                key = (o.eng, i % self.dma_pool)
                if key not in self.pool_sems:
                    self.pool_sems[key] = stack.enter_context(nc.semaphore("d_%s_%d" % key))
                    pool_cnt[key] = 0
                pool_cnt[key] += 1
                o.tok = (key, 16 * pool_cnt[key])
                o.inc = True
                hist = dma_hist[o.eng]
                if len(hist) >= self.dma_pool:
                    o.deps = list(o.deps) + [hist[-self.dma_pool]]
                hist.append(o)
            elif o.inc:
                cnt[o.eng] += 1
                o.tok = (o.eng, cnt[o.eng])
        per_eng = {e: [o for o in ops if o.eng == e] for e in self.ENGS}

        def sem_of(key):
            return self.eng_sem[key] if isinstance(key, str) else self.pool_sems[key]

        def emit(ename, e):
            known = {}
            for o in per_eng[ename]:
                for d in sorted(o.deps, key=lambda d: d.idx):
                    key, val = d.tok
                    if known.get(key, 0) >= val:
                        continue
                    e.wait_ge(sem_of(key), val)
                    known[key] = val
                inst = o.fn(e)
                if o.inc:
                    key, val = o.tok
                    inst.then_inc(sem_of(key), 16 if o.dma else 1)
            if ename == "sp":
                for key, c in pool_cnt.items():
                    if known.get(key, 0) < 16 * c:
                        e.wait_ge(self.pool_sems[key], 16 * c)

        with nc.Block() as block:
            @block.tensor
            def _(e):
                emit("pe", e)

            @block.scalar
            def _(e):
                emit("act", e)

            @block.vector
            def _(e):
                emit("dve", e)

            @block.gpsimd
            def _(e):
                emit("pool", e)

            @block.sync
            def _(e):
                emit("sp", e)
        self.stats = {e: len(per_eng[e]) for e in self.ENGS}


class Rot:
    def __init__(self, alloc, name, shape, dt, n):
        self.t = [alloc("%s%d" % (name, i), shape, dt) for i in range(n)]
        self.name = name
        self.i = 0

    def get(self):
        k = self.i % len(self.t)
        self.i += 1
        return self.t[k], (self.name, k)


def K(name, a, b, *idx):
    return [(name,) + tuple(idx) + (z,) for z in range(a // 128, (b - 1) // 128 + 1)]


def split_even(a, b, maxn):
    tot = b - a
    nt = -(-tot // maxn)
    base = tot // nt
    rem = tot % nt
    out = []
    for i in range(nt):
        n = base + (1 if i < rem else 0)
        out.append((a, a + n))
        a += n
    return out


def split_tiles(a, b, step=512):
    out = []
    while a < b:
        n = min(step, b - a)
        out.append((a, a + n))
        a += n
    return out


def debug_srcs(L):
    out = []
    out.append((L["xnT"][:, 0, 0:768], K("xnT", 0, 768)))
    out.append((L["hT"][:, 0, 0:514], [k for kt in range(8) for k in K("hT", 0, 514, kt)]))
    out.append((L["qT"][:, 0, 0:768], K("qT", 0, 768, 0)))
    out.append((L["kT"][:, 0:768], K("kT", 0, 768)))
    out.append((L["gluT"][:, 0, 0:768], K("gluT", 0, 768, 0)))
    out.append((L["mixT"][:, 0, 0:514], K("mix", 0, 514, 0)))
    out.append((L["mixT"][:, 4, 0:514], K("mix", 0, 514, 4)))
    out.append((L["cosT"][:, 0:768], ["cosT"]))
    if L.get("with_samples"):
        out = out[:6]
        for kt in (0, 4):
            out.append((L["mixT"][:, kt, 514:530], K("mix", 514, 530, kt)))
    return out


def build_nc(with_samples=True, debug=False, stop=99, npass_run=99):
    nc = bass.Bass("TRN2", target_bir_lowering=False)

    def din(name, shape):
        return nc.dram_tensor(name, list(shape), F32, kind="ExternalInput").ap()

    def dout(name, shape):
        return nc.dram_tensor(name, list(shape), F32, kind="ExternalOutput").ap()

    xh = din("xh", (256 + 2048, D_MODEL))
    pp = din("pp", (2048, PLE))
    w_in = din("w_in", (D_MODEL, 18 * 128))
    w_v = din("w_v", (D_MODEL, 128))
    w_out = din("w_out", (D_MODEL, D_MODEL))
    w_ffi = din("w_ffi", (D_MODEL, 2 * D_FF))
    w_ffo = din("w_ffo", (D_FF, D_MODEL))
    w_ple = din("w_ple", (D_MODEL + PLE, D_MODEL))
    colsd = din("cols", (128, NCOL))
    sinkrow = din("sinkrow", (1, 512))
    mask0 = din("mask0", (128, 128))
    y = dout("y", (2048, D_MODEL))
    nk = dout("nk", (128, 128))
    nv = dout("nv", (128, 128))
    nconv = dout("nconv", (30, CONV_CH))
    nffn = dout("nffn", (2, D_FF))
    if with_samples:
        xs_d = din("xs", (NS, D_MODEL))
        ps_d = din("ps", (NS, PLE))
        ck_d = din("ck", (NS, 128, 128))
        cv_d = din("cv", (NS, 128, 128))
        sc_d = din("sc", (NS, 30, CONV_CH))
        sf_d = din("sf", (NS, 2, D_FF))
        cwt_d = din("cwt", (CONV_K, CONV_CH))
        cbt_d = din("cbt", (4, CONV_CH))
        ys = dout("ys", (NS, D_MODEL))
        nks = dout("nks", (NS, 128, 128))
        nvs = dout("nvs", (NS, 128, 128))
        nconvs = dout("nconvs", (NS, 30, CONV_CH))
        nffns = dout("nffns", (NS, 2, D_FF))

    MAXM = max(PASS_BLOCKS) * 128
    NSX = NS if with_samples else 0
    ZW = 256 + MAXM + NSX
    MW = 2 + MAXM + NSX
    NPASS = len(PASS_BLOCKS)

    with ExitStack() as st:
        def sb(name, shape, dt=F32):
            return st.enter_context(nc.sbuf_tensor("sb_" + name, list(shape), dt))

        P = Prog(nc)
        banks = [st.enter_context(nc.psum_tensor("ps%d" % i, [128, 512], F32)) for i in range(8)]
        banks_bf = [b.bitcast(BF16) for b in banks]
        ps_i = [0]

        held = set()
        held_by = {}

        def PS(hold=False):
            assert len(held) < 8, "all PSUM banks held %s" % (held_by,)
            while ps_i[0] % 8 in held:
                ps_i[0] += 1
            k = ps_i[0] % 8
            ps_i[0] += 1
            if hold:
                held.add(k)
                held_by[k] = True
            return banks[k], banks_bf[k], ("ps", k)

        def PSrel(*keys):
            for kk in keys:
                held.discard(kk[1])

        cols = sb("cols", (128, NCOL))
        cols2 = sb("cols2", (128, 8))
        ident_f = sb("ident_f", (128, 128))
        ident_b = sb("ident_b", (128, 128), BF16)
        ones_b = sb("ones_b", (128, 128), BF16)
        bd_b = sb("bd_b", (128, 128), BF16)
        onesm_f = sb("onesm_f", (128, 128))
        onesr_f = sb("onesr_f", (1, 128))
        sinkexp = sb("sinkexp", (1, 512))
        mk_cur = sb("mk_cur", (128, 128))
        mk_prev = sb("mk_prev", (128, 128))
        mk0 = sb("mk0", (128, 128))
        maskall = sb("maskall", (128, 512), BF16)
        maskall0 = sb("maskall0", (128, 512), BF16)
        iot_i = sb("iot_i", (128, 128), I32)
        iot_f = sb("iot_f", (128, 128))
        col_i = sb("col_i", (128, 2), I32)
        tmpc = sb("tmpc", (128, 128))

        P.dma("sp", cols[:], colsd, writes=["cols"])
        P.dma("sp", sinkexp[:], sinkrow, writes=["sinkexp"])
        P.dma("sp", mk0[:], mask0, writes=["mk0"])
        P.pool(lambda e: e.iota(iot_i[:], [[1, 128]], base=0, channel_multiplier=0), writes=["iot_i"])
        P.pool(lambda e: e.iota(col_i[:, 0:1], [[0, 1]], base=0, channel_multiplier=1), writes=["col_i"])
        P.dve(lambda e: e.tensor_copy(iot_f[:], iot_i[:]), reads=["iot_i"], writes=["iot_f"])
        P.dve(lambda e: e.tensor_copy(cols2[:, 3:4], col_i[:, 0:1]), reads=["col_i"], writes=["c2_3"])
        P.dve(lambda e: e.tensor_scalar(col_i[:, 1:2], col_i[:, 0:1], 31, None, op0=ALU.bitwise_and),
              reads=["col_i"], writes=["col_i1"])
        P.dve(lambda e: e.tensor_copy(cols2[:, 4:5], col_i[:, 1:2]), reads=["col_i1"], writes=["c2_4"])
        P.dve(lambda e: e.tensor_copy(cols2[:, 2:3], cols[:, C_INVF:C_INVF + 1]), reads=["cols"], writes=["c2_2"])
        P.dve(lambda e: e.tensor_tensor(cols2[:, 0:1], cols[:, C_GQR:C_GQR + 1], cols[:, C_SGN:C_SGN + 1], ALU.mult),
              reads=["cols"], writes=["c2_0"])
        P.dve(lambda e: e.tensor_tensor(cols2[:, 1:2], cols[:, C_GKR:C_GKR + 1], cols[:, C_SGN:C_SGN + 1], ALU.mult),
              reads=["cols"], writes=["c2_1"])
        P.dve(lambda e: e.tensor_scalar(ident_f[:], iot_f[:], cols2[:, 3:4], None, op0=ALU.is_equal),
              reads=["iot_f", "c2_3"], writes=["ident_f"])
        P.dve(lambda e: e.tensor_copy(ident_b[:], ident_f[:]), reads=["ident_f"], writes=["ident"])
        P.dve(lambda e: e.tensor_scalar(mk_cur[:], iot_f[:], cols2[:, 3:4], None, op0=ALU.is_ge),
              reads=["iot_f", "c2_3"], writes=["mk_cur"])
        P.dve(lambda e: e.tensor_scalar(mk_prev[:], iot_f[:], cols2[:, 3:4], None, op0=ALU.is_le),
              reads=["iot_f", "c2_3"], writes=["mk_prev"])
        for q4 in range(4):
            src = mk_cur if q4 % 2 == 0 else mk_prev
            src0 = mk_cur if q4 % 2 == 0 else mk0
            P.dve(lambda e, q4=q4, src=src: e.tensor_copy(maskall[:, q4 * 128:(q4 + 1) * 128], src[:]),
                  reads=["mk_cur", "mk_prev", "maskall"], writes=["maskall"])
            P.dve(lambda e, q4=q4, src0=src0: e.tensor_copy(maskall0[:, q4 * 128:(q4 + 1) * 128], src0[:]),
                  reads=["mk_cur", "mk0", "maskall0"], writes=["maskall0"])
        P.dve(lambda e: e.tensor_scalar(tmpc[:], iot_f[:], 64.0, None, op0=ALU.is_ge), reads=["iot_f"], writes=["tmpc"])
        P.dve(lambda e: e.tensor_scalar(cols2[:, 5:6], cols2[:, 3:4], 64.0, None, op0=ALU.is_ge), reads=["c2_3"], writes=["c2_5"])
        P.dve(lambda e: e.tensor_scalar(bd_b[:], tmpc[:], cols2[:, 5:6], None, op0=ALU.is_equal),
              reads=["tmpc", "c2_5"], writes=["bd_b"])
        if with_samples:
            P.dve(lambda e: e.tensor_scalar(bd_f[:], tmpc[:], cols2[:, 5:6], None, op0=ALU.is_equal),
                  reads=["tmpc", "c2_5"], writes=["bd_f"])
        P.pool(lambda e: e.memset(ones_b[:], 1.0), writes=["ones_b"])
        P.pool(lambda e: e.memset(onesm_f[:], 1.0 / CONV_CH), writes=["onesm_f"])
        P.pool(lambda e: e.memset(onesr_f[:], 1.0), writes=["onesr_f"])
        P.act(lambda e: e.activation(sinkexp[:], sinkexp[:], AF.Exp), reads=["sinkexp"], writes=["sinkexp"])
        sinkc = sb("sinkc", (128, 4))
        P.act(lambda e: e.activation(sinkc[:], cols[:, C_SINKC:C_SINKC + 4], AF.Exp), reads=["cols"], writes=["sinkc"])

        xnT = sb("xnT", (128, 8, ZW), BF16)
        qT = sb("qT", (128, 4, ZW), BF16)
        kT = sb("kT", (128, ZW), BF16)
        vS = sb("vS", (128, 2 + MAXM // 128, 128), BF16)
        gluT = sb("gluT", (128, 4, ZW), BF16)
        cosT = sb("cosT", (128, ZW))
        sinT = sb("sinT", (128, ZW))
        rt_i = sb("rt_i", (128, 512), I32)
        mixT = sb("mixT", (128, 8, MW), BF16)
        hT = sb("hT", (128, 8, MW))
        hnT = sb("hnT", (128, 8, MW), BF16)
        actT = sb("actT", (128, NJ, MW), BF16)
        pT = sb("pT", (128, 2, MW), BF16)
        cbuf = sb("cbuf", (128, 4, 512))
        ring = sb("ring", (128, NSLOT, SLOT), BF16)
        kf32 = sb("kf32", (128, 128))
        gl32 = sb("gl32", (128, 4, 32))
        fg32 = sb("fg32", (128, NJ, 2))
        hcarry = sb("hcarry", (128, 8, 2), BF16)
        cst = sb("cst", (32, 512))

        xtR = Rot(sb, "xt", (128, 1024), F32, 2)
        xbR = Rot(sb, "xb", (128, 1024), BF16, 2)
        ptR = Rot(sb, "ptk", (128, PLE), BF16, 2)
        ytR = Rot(sb, "ytk", (128, 1024), F32, 2)
        c1R = Rot(sb, "c1_", (128, 4), F32, 4)
        tAR = Rot(sb, "tA", (128, 512), F32, 3)
        tCR = Rot(sb, "tC", (128, 512), F32, 2)
        bfR = Rot(sb, "bfT", (128, 512), BF16, 4)
        dgR = Rot(sb, "dg", (128, 128), BF16, 8)
        outR = Rot(sb, "ot", (128, 128), F32, 2)
        sqR = Rot(sb, "sqr", (128, 512), BF16, 3)
        if with_samples:
            ks32 = sb("ks32", (128, NS))
            gs32 = sb("gs32", (128, 4, NS))
            vs_tok = sb("vs_tok", (NS, 128))
            vTs = sb("vTs", (128, NS))
            ckb = sb("ckb", (128, NS, 128), BF16)
            cvb = sb("cvb", (128, NS, 128), BF16)
            kTbR = Rot(sb, "kTb", (128, 128), BF16, 3)
            wrep = sb("wrep", (124, 512))
            selT = sb("selT", (124, 4, NS))
            bd_f = sb("bd_f", (128, 128))
            stT = sb("stT", (128, NJ, 2 * NS))
            fgs32 = sb("fgs32", (128, NJ, NS))
            crows = sb("crows", (65, 512))
            onesf16 = sb("onesf16", (65, NS))
            sm = {k_: sb("sm_" + k_, (128, 64)) for k_ in ("prod", "enew", "dtot", "ev", "otot")}
            EAB = [sb("EAB%d" % i, (128, 64), BF16) for i in range(2)]
            gs_tok = sb("gs_tok", (NS, 512))
            cs_col = sb("cs_col", (NS, 8))
            cs_col32 = sb("cs_col32", (128, 8))

        def col(i):
            return cols[:, i:i + 1]

        w_in_v = w_in.rearrange("(kt p) c -> p kt c", p=128)
        w_v_v = w_v.rearrange("(kt p) c -> p kt c", p=128)
        w_out_v = w_out.rearrange("(kt p) c -> p kt c", p=128)
        w_ffi_v = w_ffi.rearrange("(kt p) c -> p kt c", p=128)
        w_ffo_v = w_ffo.rearrange("(kt p) c -> p kt c", p=128)
        w_ple_v = w_ple.rearrange("(kt p) c -> p kt c", p=128)
        seq = []
        for pi in range(NPASS):
            for i in range(9):
                seq.append((w_in_v[:, :, i * 256:(i + 1) * 256], 8, 256))
            seq.append((w_v_v, 8, 128))
            for i in range(4):
                seq.append((w_out_v[:, :, i * 256:(i + 1) * 256], 8, 256))
            for j in range(NJ):
                seq.append((w_ffi_v[:, :, j * 256:(j + 1) * 256], 8, 256))
            for oc in range(8):
                seq.append((w_ffo_v[:, :, oc * 128:(oc + 1) * 128], NJ, 128))
            for i in range(4):
                seq.append((w_ple_v[:, :, i * 256:(i + 1) * 256], 10, 256))
        ws = {"issued": 0, "used": 0}

        def ws_issue_upto(n):
            while ws["issued"] < min(n, len(seq)):
                i = ws["issued"]
                src, kt, c = seq[i]
                slot = i % NSLOT
                dst = ring[:, slot, 0:kt * c].rearrange("p (a b) -> p a b", a=kt)
                P.dma("pool", dst, src, writes=[("w", slot)])
                ws["issued"] += 1

        def ws_next():
            i = ws["used"]
            ws["used"] += 1
            ws_issue_upto(i + NSLOT)
            src, kt, c = seq[i]
            slot = i % NSLOT
            return ring[:, slot, 0:kt * c].rearrange("p (a b) -> p a b", a=kt), ("w", slot)

        ws_issue_upto(NSLOT - 1)

        sinkb = sb("sinkb", (1, 512), BF16)
        P.act(lambda e: e.activation(sinkb[:], sinkexp[:], AF.Copy), reads=["sinkexp"], writes=["sinkb"])

        def rope_tables(pi, s0, M, last):
            W = 256 + M
            chunks = [(a, b, False) for (a, b) in split_tiles(0, W)]
            if last and with_samples:
                chunks.append((W, W + NS, True))
            for (a, b, samp) in chunks:
                n = b - a
                af, kaf = tAR.get()
                kfl, kkf = tCR.get()
                if samp:
                    P.dve(lambda e, af=af, n=n: e.tensor_scalar(af[:, 0:n], cols2[:, 2:3].to_broadcast([128, n]) if False else iot_f[:, 0:n], 0.0, float(PAST_LEN),
                                                                op0=ALU.mult, op1=ALU.add),
                          reads=["iot_f"], writes=[kaf])
                    P.dve(lambda e, af=af, n=n: e.tensor_scalar(af[:, 0:n], af[:, 0:n], cols2[:, 2:3], None, op0=ALU.mult),
                          reads=[kaf, "c2_2"], writes=[kaf])
                else:
                    P.pool(lambda e, a=a, n=n: e.iota(rt_i[:, 0:n], [[1, n]], base=s0 - 256 + a, channel_multiplier=0),
                           reads=["rt_i"], writes=["rt_i"])
                    P.dve(lambda e, af=af, n=n: e.tensor_copy(af[:, 0:n], rt_i[:, 0:n]), reads=["rt_i"], writes=[kaf])
                    P.dve(lambda e, af=af, n=n: e.tensor_scalar(af[:, 0:n], af[:, 0:n], col(C_POSB), cols2[:, 2:3],
                                                                op0=ALU.add, op1=ALU.mult),
                          reads=[kaf, "cols", "c2_2"], writes=[kaf])
                for tab, shift, key in ((sinT, 0.0, "sinT"), (cosT, math.pi / 2, "cosT")):
                    P.dve(lambda e, tab=tab, shift=shift, af=af, a=a, n=n: e.tensor_scalar(tab[:, a:a + n], af[:, 0:n], shift, None, op0=ALU.add),
                          reads=[kaf, key], writes=[key])
                    P.dve(lambda e, tab=tab, a=a, n=n: e.tensor_scalar(rt_i[:, 0:n], tab[:, a:a + n], 1.0 / TWO_PI, None, op0=ALU.mult),
                          reads=[key, "rt_i"], writes=["rt_i"])
                    P.dve(lambda e, kfl=kfl, n=n: e.tensor_copy(kfl[:, 0:n], rt_i[:, 0:n]), reads=["rt_i", kkf], writes=[kkf])
                    P.dve(lambda e, tab=tab, kfl=kfl, a=a, n=n: e.scalar_tensor_tensor(tab[:, a:a + n], kfl[:, 0:n], -CW1, tab[:, a:a + n],
                                                                                      op0=ALU.mult, op1=ALU.add),
                          reads=[kkf, key], writes=[key])
                    P.dve(lambda e, tab=tab, kfl=kfl, a=a, n=n: e.scalar_tensor_tensor(tab[:, a:a + n], kfl[:, 0:n], -CW2, tab[:, a:a + n],
                                                                                      op0=ALU.mult, op1=ALU.add),
                          reads=[kkf, key], writes=[key])
                    P.dve(lambda e, tab=tab, a=a, n=n: e.tensor_scalar(tab[:, a:a + n], tab[:, a:a + n], math.pi, -math.pi,
                                                                       op0=ALU.min, op1=ALU.max),
                          reads=[key], writes=[key])
                    P.act(lambda e, tab=tab, a=a, n=n: e.activation(tab[:, a:a + n], tab[:, a:a + n], AF.Sin), reads=[key], writes=[key])

        def hkeys(name, a, b, kts=range(8)):
            return [k for kt in kts for k in K(name, a, b, kt)]

        def stage0a_front(src_rows, n):
            xt, kx = xtR.get()
            P.dma("sp", xt[0:n, :], src_rows, writes=[kx])
            c1, kc = c1R.get()
            xb, kxb = xbR.get()
            P.act(lambda e: e.activation(xb[0:n, :], xt[0:n, :], AF.Square, accum_out=c1[0:n, 0:1]),
                  reads=[kx], writes=[kc, kxb])
            P.act(lambda e: e.activation(c1[0:n, 1:2], c1[0:n, 0:1], AF.Ln, scale=1.0 / D_MODEL, bias=EPS),
                  reads=[kc], writes=[kc])
            P.act(lambda e: e.activation(c1[0:n, 2:3], c1[0:n, 1:2], AF.Exp, scale=-0.5), reads=[kc], writes=[kc])
            P.dve(lambda e: e.tensor_scalar(xb[0:n, :], xt[0:n, :], c1[0:n, 2:3], None, op0=ALU.mult),
                  reads=[kx, kc, kxb], writes=[kxb])
            return xb, kxb

        def stage0a_back(xb, kxb, n, zc):
            ps, psb, kp = PS()

            def tr(e):
                r = None
                for kt in range(8):
                    r = e.transpose(psb[:, kt * 128:kt * 128 + n], xb[0:n, kt * 128:(kt + 1) * 128], ident_b[0:n, 0:n])
                return r
            P.pe(tr, reads=[kxb, "ident"], writes=[kp])
            for kt in range(8):
                P.act(lambda e, kt=kt: e.activation(xnT[:, kt, zc:zc + n], psb[:, kt * 128:kt * 128 + n], AF.Identity,
                                                    scale=col(C_GMIX + kt)),
                      reads=[kp, "cols"], writes=K("xnT", zc, zc + n))

        def stage0a(src_rows, n, zc):
            xb, kxb = stage0a_front(src_rows, n)
            stage0a_back(xb, kxb, n, zc)

        def stage0b(src_rows, n, mc, halo2=False):
            xt, kx = xtR.get()
            P.dma("sp", xt[0:n, :], src_rows, writes=[kx])
            for half in range(2):
                ps2, _, kp2 = PS()

                def trf(e, half=half, ps2=ps2):
                    r = None
                    for a4 in range(4):
                        kt = half * 4 + a4
                        r = e.transpose(ps2[:, a4 * 128:a4 * 128 + n], xt[0:n, kt * 128:(kt + 1) * 128], ident_f[0:n, 0:n])
                    return r
                P.pe(trf, reads=[kx, "ident_f"], writes=[kp2])
                v3 = ps2[:, 0:512].rearrange("p (a b) -> p a b", a=4)
                kts = range(half * 4, half * 4 + 4)
                if halo2:
                    P.dve(lambda e, half=half, v3=v3: e.tensor_copy(hT[:, half * 4:half * 4 + 4, 0:2], v3[:, :, 126:128]),
                          reads=[kp2], writes=hkeys("hT", 0, 2, kts))
                else:
                    P.dve(lambda e, half=half, v3=v3: e.tensor_copy(hT[:, half * 4:half * 4 + 4, mc:mc + n], v3[:, :, 0:n]),
                          reads=[kp2], writes=hkeys("hT", mc, mc + n, kts))

        import os
        RE = int(os.environ.get("MK_RE", "99"))

        def rope_epilogue(psq, psr, kq, kr, n, zc, gcol, grscol, dst, dkeys, f32dst=None):
            if RE < 1:
                return
            sq, ksq = bfR.get()
            P.act(lambda e: e.activation(sq[:, 0:n], psq[:, 0:n], AF.Square), reads=[kq], writes=[ksq])
            pss, _, kss = PS()
            P.pe(lambda e: e.matmul(pss[:, 0:n], bd_b[:], sq[:, 0:n], start=True, stop=True),
                 reads=[ksq, "bd_b"], writes=[kss])
            if RE < 2:
                return
            tl, ktl = tCR.get()
            P.act(lambda e: e.activation(tl[:, 0:n], pss[:, 0:n], AF.Ln, scale=1.0 / HD, bias=EPS), reads=[kss], writes=[ktl])
            if RE < 3:
                return
            P.act(lambda e: e.activation(pss[:, 0:n], tl[:, 0:n], AF.Exp, scale=-0.5), reads=[ktl, kss], writes=[kss])
            if RE < 4:
                return
            t1, k1 = tAR.get()
            P.dve(lambda e: e.scalar_tensor_tensor(t1[:, 0:n], psq[:, 0:n], gcol, cosT[:, zc:zc + n], op0=ALU.mult, op1=ALU.mult),
                  reads=[kq, "cosT", "cols"], writes=[k1])
            if RE < 5:
                return
            P.dve(lambda e: e.scalar_tensor_tensor(psr[:, 0:n], psr[:, 0:n], grscol, sinT[:, zc:zc + n], op0=ALU.mult, op1=ALU.mult),
                  reads=[kr, "sinT", "c2_0", "c2_1"], writes=[kr])
            if RE < 6:
                return
            P.dve(lambda e: e.tensor_tensor(t1[:, 0:n], t1[:, 0:n], psr[:, 0:n], ALU.add), reads=[k1, kr], writes=[k1])
            if RE < 7:
                return
            P.dve(lambda e: e.tensor_tensor(dst[:, zc:zc + n], t1[:, 0:n], pss[:, 0:n], ALU.mult),
                  reads=[k1, kss], writes=dkeys)
            if f32dst is not None:
                fd, lo, hi, fk = f32dst
                P.dve(lambda e: e.tensor_tensor(fd, t1[:, lo:hi], pss[:, lo:hi], ALU.mult),
                      reads=[k1, kss], writes=[fk])

        def mm2(slab, src, a, n, ps0, ps1):
            def f(e):
                r = None
                for kt in range(8):
                    r = e.matmul(ps0[:, 0:n], slab[:, kt, 0:128], src[:, kt, a:a + n], start=(kt == 0), stop=(kt == 7))
                for kt in range(8):
                    r = e.matmul(ps1[:, 0:n], slab[:, kt, 128:256], src[:, kt, a:a + n], start=(kt == 0), stop=(kt == 7))
                return r
            return f

        ZP = int(os.environ.get("MK_ZP", "99"))

        def z_stage(pi, ztiles, final, stile=None, fillers=()):
            fillers = list(fillers)
            tl_ = [(a, b, False) for (a, b) in ztiles] + ([(stile[0], stile[1], True)] if stile else [])
            nmain = len(ztiles)
            for c in range(4):
                slab, kw = ws_next()
                for (a, b, samp) in tl_:
                    n = b - a
                    psq, _, kq = PS()
                    psr, _, kr = PS()
                    P.pe(mm2(slab, xnT, a, n, psq, psr), reads=[kw] + K("xnT", a, b), writes=[kq, kr])
                    rope_epilogue(psq, psr, kq, kr, n, a, col(C_GQ), cols2[:, 0:1], qT[:, c, :], K("qT", a, b, c))
                if fillers:
                    fillers.pop(0)()
            slab, kw = ws_next()
            for ti, (a, b, samp) in enumerate(tl_):
                n = b - a
                psq, _, kq = PS()
                psr, _, kr = PS()
                P.pe(mm2(slab, xnT, a, n, psq, psr), reads=[kw] + K("xnT", a, b), writes=[kq, kr])
                f32d = None
                if samp:
                    f32d = (ks32[:, 0:NS], 0, NS, "ks32")
                elif final and ti == nmain - 1:
                    f32d = (kf32[:, 0:128], n - 128, n, "kf32")
                rope_epilogue(psq, psr, kq, kr, n, a, col(C_GK), cols2[:, 1:2], kT, K("kT", a, b), f32dst=f32d)
            for c in range(4):
                slab, kw = ws_next()
                for ti, (a, b, samp) in enumerate(tl_):
                    n = b - a
                    psa, _, ka = PS()
                    psg, _, kg = PS()
                    P.pe(mm2(slab, xnT, a, n, psa, psg), reads=[kw] + K("xnT", a, b), writes=[ka, kg])
                    sg, ksg = tAR.get()
                    P.act(lambda e, sg=sg, psg=psg, n=n: e.activation(sg[:, 0:n], psg[:, 0:n], AF.Sigmoid), reads=[kg], writes=[ksg])
                    P.dve(lambda e, sg=sg, psa=psa, n=n, a=a, c=c: e.tensor_tensor(gluT[:, c, a:a + n], psa[:, 0:n], sg[:, 0:n], ALU.mult),
                          reads=[ka, ksg], writes=K("gluT", a, b, c))
                    if samp:
                        P.dve(lambda e, sg=sg, psa=psa, n=n, c=c: e.tensor_tensor(gs32[:, c, 0:NS], psa[:, 0:n], sg[:, 0:n], ALU.mult),
                              reads=[ka, ksg], writes=[("gs32", c)])
                    elif final and ti == nmain - 1:
                        P.dve(lambda e, sg=sg, psa=psa, n=n, c=c: e.tensor_tensor(gl32[:, c, 0:32], psa[:, n - 32:n], sg[:, n - 32:n], ALU.mult),
                              reads=[ka, ksg], writes=[("gl32", c)])
                if fillers:
                    fillers.pop(0)()
            while fillers:
                fillers.pop(0)()
            slab, kw = ws_next()
            for ti, (a, b, samp) in enumerate(tl_):
                psv, _, kv = PS()
                if samp:
                    def mmvs(e, slab=slab, a=a, psv=psv):
                        r = None
                        for kt in range(8):
                            r = e.matmul(psv[0:NS, 0:128], xnT[:, kt, a:a + NS], slab[:, kt, 0:128], start=(kt == 0), stop=(kt == 7))
                        for kt in range(8):
                            r = e.matmul(psv[:, 128:128 + NS], slab[:, kt, 0:128], xnT[:, kt, a:a + NS], start=(kt == 0), stop=(kt == 7))
                        return r
                    P.pe(mmvs, reads=[kw] + K("xnT", a, b), writes=[kv])
                    P.act(lambda e, psv=psv: e.activation(vs_tok[:], psv[0:NS, 0:128], AF.Copy), reads=[kv], writes=["vs_tok"])
                    P.act(lambda e, psv=psv: e.activation(vTs[:], psv[:, 128:128 + NS], AF.Copy), reads=[kv], writes=["vTs"])
                    continue
                nblk = (b - a) // 128

                def mmv(e, slab=slab, a=a, nblk=nblk, psv=psv):
                    r = None
                    for bl in range(nblk):
                        for kt in range(8):
                            r = e.matmul(psv[:, bl * 128:(bl + 1) * 128], xnT[:, kt, a + bl * 128:a + (bl + 1) * 128],
                                         slab[:, kt, 0:128], start=(kt == 0), stop=(kt == 7))
                    return r
                P.pe(mmv, reads=[kw] + K("xnT", a, b), writes=[kv])
                P.act(lambda e, a=a, nblk=nblk, psv=psv: e.activation(
                    vS[:, a // 128:a // 128 + nblk, :], psv[:, 0:nblk * 128].rearrange("p (a b) -> p a b", a=nblk), AF.Copy),
                    reads=[kv], writes=K("vS", a, b))
                if final and ti == nmain - 1:
                    ot, ko = outR.get()
                    P.dve(lambda e, ot=ot, psv=psv, nblk=nblk: e.tensor_copy(ot[:], psv[:, (nblk - 1) * 128:nblk * 128]),
                          reads=[kv], writes=[ko])
                    P.dma("sp", nv, ot[:], reads=[ko])

        def attn_pass(blocks, nob=2):
            bk = [PS(hold=True) for _ in range(4 + 2 * nob)]
            sbank = [(bk[0], bk[1]), (bk[2], bk[3])]
            obank = [(bk[4 + 2 * i], bk[5 + 2 * i]) for i in range(nob)]
            units = [(bi, cp) for bi in range(len(blocks)) for cp in range(2)]
            state = {}

            def rkeys(zb):
                zc = zb * 128
                zp = zc - 128
                return [k for c in range(4) for k in K("qT", zc, zc + 128, c)] + K("kT", zp, zc + 128) + K("vS", zp, zc + 128)

            def S(ui):
                bi, cp = units[ui]
                zb, mc_dst, halo, first_main = blocks[bi]
                zc = zb * 128
                zp = zc - 128
                c0, c1 = 2 * cp, 2 * cp + 1
                (psA, _, kA), (psB, _, kB) = sbank[ui % 2]
                mk = maskall0 if first_main else maskall

                def sc(e):
                    r = None
                    for i, c in enumerate((c0, c1)):
                        o = i * 256
                        e.matmul(psA[:, o:o + 128], kT[0:64, zc:zc + 128], qT[0:64, c, zc:zc + 128], start=True, stop=True)
                        e.matmul(psB[:, o:o + 128], kT[64:128, zc:zc + 128], qT[64:128, c, zc:zc + 128], start=True, stop=True)
                        e.matmul(psA[:, o + 128:o + 256], kT[0:64, zp:zp + 128], qT[0:64, c, zc:zc + 128], start=True, stop=True)
                        r = e.matmul(psB[:, o + 128:o + 256], kT[64:128, zp:zp + 128], qT[64:128, c, zc:zc + 128], start=True, stop=True)
                    return r
                P.pe(sc, reads=rkeys(zb), writes=[kA, kB])
                ia, ib = 2 * (ui % 2), 2 * (ui % 2) + 1
                eA, keA = bfR.t[ia], ("bfT", ia)
                eB, keB = bfR.t[ib], ("bfT", ib)
                P.act(lambda e: e.activation(eA[:], psA[:], AF.Exp, scale=HD ** -0.5), reads=[kA], writes=[keA])
                P.act(lambda e: e.activation(eB[:], psB[:], AF.Exp, scale=HD ** -0.5), reads=[kB], writes=[keB])
                P.dve(lambda e: e.tensor_tensor(eA[:], eA[:], mk[:], ALU.mult), reads=[keA, "maskall", "maskall0"], writes=[keA])
                P.dve(lambda e: e.tensor_tensor(eB[:], eB[:], mk[:], ALU.mult), reads=[keB, "maskall", "maskall0"], writes=[keB])
                state[ui] = (eA, keA, eB, keB)

            def V(ui):
                bi, cp = units[ui]
                zb, mc_dst, halo, first_main = blocks[bi]
                c0, c1 = 2 * cp, 2 * cp + 1
                (pso, _, ko), (psd, _, kd) = obank[bi % nob]
                eA, keA, eB, keB = state.pop(ui)

                def pv(e):
                    r = None
                    for i, c in enumerate((c0, c1)):
                        o = i * 256
                        o0 = pso[0:64, c * 128:(c + 1) * 128]
                        o1 = pso[64:128, c * 128:(c + 1) * 128]
                        d0 = psd[0:64, c * 128:(c + 1) * 128]
                        d1 = psd[64:128, c * 128:(c + 1) * 128]
                        e.matmul(o0, vS[:, zb, 0:64], eA[:, o:o + 128], start=True, stop=False)
                        e.matmul(o0, vS[:, zb - 1, 0:64], eA[:, o + 128:o + 256], start=False, stop=True)
                        e.matmul(o1, vS[:, zb, 64:128], eB[:, o:o + 128], start=True, stop=False)
                        e.matmul(o1, vS[:, zb - 1, 64:128], eB[:, o + 128:o + 256], start=False, stop=True)
                        e.matmul(d0, ones_b[:, 0:64], eA[:, o:o + 128], start=True, stop=False)
                        e.matmul(d0, ones_b[:, 0:64], eA[:, o + 128:o + 256], start=False, stop=True)
                        e.matmul(d1, ones_b[:, 64:128], eB[:, o:o + 128], start=True, stop=False)
                        r = e.matmul(d1, ones_b[:, 64:128], eB[:, o + 128:o + 256], start=False, stop=True)
                    return r
                P.pe(pv, reads=[keA, keB, "ones_b"] + rkeys(zb), writes=[ko, kd])
                if cp == 1:
                    tl, ktl = tCR.get()
                    for c in range(4):
                        P.act(lambda e, c=c: e.activation(tl[:, c * 128:(c + 1) * 128], psd[:, c * 128:(c + 1) * 128], AF.Ln, bias=sinkc[:, c:c + 1]),
                              reads=[kd, "sinkc", ktl], writes=[ktl])
                    P.act(lambda e: e.activation(tl[:], tl[:], AF.Exp, scale=-1.0), reads=[ktl], writes=[ktl])
                    o3 = pso[:, 0:512].rearrange("p (a b) -> p a b", a=4)
                    r3 = tl[:, 0:512].rearrange("p (a b) -> p a b", a=4)
                    if not halo:
                        P.dve(lambda e: e.tensor_tensor(mixT[:, 0:4, mc_dst:mc_dst + 128], o3, r3, ALU.mult),
                              reads=[ko, ktl], writes=hkeys("mix", mc_dst, mc_dst + 128, range(4)))
                    else:
                        P.dve(lambda e: e.tensor_tensor(mixT[:, 0:4, 0:2], o3[:, :, 126:128], r3[:, :, 126:128], ALU.mult),
                              reads=[ko, ktl], writes=hkeys("mix", 0, 2, range(4)))

            nu = len(units)
            for ui in range(min(2, nu)):
                S(ui)
            for ui in range(nu):
                V(ui)
                if ui + 2 < nu:
                    S(ui + 2)
            PSrel(*[b_[2] for b_ in bk])

        def conv_stage(mtiles, fillers=()):
            fillers = list(fillers)
            for (a, b) in mtiles:
                n = b - a
                psm, _, km = PS(hold=True)
                pse, _, ke = PS(hold=True)
                for ch in range(4):
                    psc, _, kc = PS()
                    for k in range(CONV_K):
                        dg, kdg = dgR.get()
                        if k % 2 == 0 or os.environ.get('MK_D', '1') != '1':
                            P.pool(lambda e, dg=dg, ch=ch, k=k: e.tensor_scalar(
                                dg[:], ident_f[:], col(C_CONVW + ch * 31 + k), 1.0, op0=ALU.mult, op1=ALU.mult),
                                reads=["ident_f", "cols"], writes=[kdg])
                        else:
                            P.dve(lambda e, dg=dg, ch=ch, k=k: e.tensor_scalar(
                                dg[:], ident_f[:], col(C_CONVW + ch * 31 + k), None, op0=ALU.mult),
                                reads=["ident_f", "cols"], writes=[kdg])
                        z0 = 256 + (a - 2) - 30 + k
                        P.pe(lambda e, dg=dg, ch=ch, k=k, z0=z0, n=n, psc=psc: e.matmul(
                            psc[:, 0:n], dg[:], gluT[:, ch, z0:z0 + n], start=(k == 0), stop=(k == CONV_K - 1)),
                            reads=[kdg] + K("gluT", z0, z0 + n, ch), writes=[kc])
                    P.act(lambda e, ch=ch, n=n, psc=psc: e.activation(cbuf[:, ch, 0:n], psc[:, 0:n], AF.Identity, bias=col(C_CONVB + ch)),
                          reads=[kc, "cols"], writes=[("cbuf", ch)])
                    cq, kcq = sqR.get()
                    P.act(lambda e, ch=ch, n=n, psc=psc, cq=cq: e.activation(cq[:, 0:n], psc[:, 0:n], AF.Square, bias=col(C_CONVB + ch)),
                          reads=[kc, "cols"], writes=[kcq])
                    P.pe(lambda e, ch=ch, n=n, psm=psm: e.matmul(psm[:, 0:n], onesm_f[:], cbuf[:, ch, 0:n], start=(ch == 0), stop=(ch == 3)),
                         reads=[("cbuf", ch), "onesm_f"], writes=[km])
                    P.pe(lambda e, ch=ch, n=n, pse=pse, cq=cq: e.matmul(pse[:, 0:n], ones_b[:], cq[:, 0:n], start=(ch == 0), stop=(ch == 3)),
                         reads=[kcq, "ones_b"], writes=[ke])
                    if fillers:
                        fillers.pop(0)()
                m2, k2 = tCR.get()
                P.act(lambda e, n=n, m2=m2, psm=psm: e.activation(m2[:, 0:n], psm[:, 0:n], AF.Square), reads=[km], writes=[k2])
                P.dve(lambda e, n=n, m2=m2, pse=pse: e.scalar_tensor_tensor(m2[:, 0:n], pse[:, 0:n], 1.0 / CONV_CH, m2[:, 0:n],
                                                                          op0=ALU.mult, op1=ALU.subtract),
                      reads=[ke, k2], writes=[k2])
                P.act(lambda e, n=n, m2=m2: e.activation(m2[:, 0:n], m2[:, 0:n], AF.Ln, bias=EPS), reads=[k2], writes=[k2])
                P.act(lambda e, n=n, m2=m2, pse=pse: e.activation(pse[:, 0:n], m2[:, 0:n], AF.Exp, scale=-0.5), reads=[k2, ke], writes=[ke])
                for ch in range(4):
                    t1, k1 = tAR.get()
                    P.dve(lambda e, ch=ch, n=n, t1=t1, psm=psm: e.tensor_tensor(t1[:, 0:n], cbuf[:, ch, 0:n], psm[:, 0:n], ALU.subtract),
                          reads=[("cbuf", ch), km], writes=[k1])
                    P.dve(lambda e, n=n, t1=t1, pse=pse: e.tensor_tensor(t1[:, 0:n], t1[:, 0:n], pse[:, 0:n], ALU.mult),
                          reads=[k1, ke], writes=[k1])
                    P.act(lambda e, ch=ch, n=n, t1=t1, a=a: e.activation(mixT[:, 4 + ch, a:a + n], t1[:, 0:n], AF.Silu,
                                                                         scale=col(C_LNG + ch), bias=col(C_LNB + ch)),
                          reads=[k1, "cols"], writes=K("mix", a, b, 4 + ch))
                PSrel(km, ke)
            while fillers:
                fillers.pop(0)()

        class NormAcc:
            def __init__(self, mtiles):
                self.tiles = mtiles
                self.ps = [PS(hold=True) for _ in mtiles]
                self.pend = []

            def add(self, ti, kt):
                a, b = self.tiles[ti]
                n = b - a
                sq_, ksq_ = sqR.get()
                P.act(lambda e: e.activation(sq_[:, 0:n], hT[:, kt, a:a + n], AF.Square), reads=K("hT", a, b, kt), writes=[ksq_])
                self.pend.append((ti, kt, sq_, ksq_, n))

            def flush(self, keep=0):
                while len(self.pend) > keep:
                    ti, kt, sq_, ksq_, n = self.pend.pop(0)
                    pss, _, ks = self.ps[ti]
                    P.pe(lambda e, pss=pss, sq_=sq_, n=n, kt=kt: e.matmul(pss[:, 0:n], ones_b[:], sq_[:, 0:n], start=(kt == 0), stop=(kt == 7)),
                         reads=[ksq_, "ones_b"], writes=[ks])

            def finish(self, gbase, flag_halo=False):
                self.flush()
                for ti, (a, b) in enumerate(self.tiles):
                    n = b - a
                    pss, _, ks = self.ps[ti]
                    tl, ktl = tCR.get()
                    P.act(lambda e, n=n, tl=tl, pss=pss: e.activation(tl[:, 0:n], pss[:, 0:n], AF.Ln, scale=1.0 / D_MODEL, bias=EPS),
                          reads=[ks], writes=[ktl])
                    P.act(lambda e, n=n, tl=tl, pss=pss: e.activation(pss[:, 0:n], tl[:, 0:n], AF.Exp, scale=-0.5), reads=[ktl, ks], writes=[ks])
                    for kt in range(8):
                        P.dve(lambda e, kt=kt, a=a, n=n, pss=pss: e.scalar_tensor_tensor(
                            hnT[:, kt, a:a + n], hT[:, kt, a:a + n], col(gbase + kt), pss[:, 0:n], op0=ALU.mult, op1=ALU.mult),
                            reads=K("hT", a, b, kt) + [ks, "cols"], writes=K("hnT", a, b, kt))
                        if flag_halo and a == 0:
                            P.dve(lambda e, kt=kt: e.tensor_scalar(hnT[:, kt, 0:2], hnT[:, kt, 0:2], col(C_FLAG), None, op0=ALU.mult),
                                  reads=K("hnT", 0, 2, kt) + ["cols"], writes=K("hnT", 0, 2, kt))
                    PSrel(ks)

        def wout_stage(mtiles, na):
            for i in range(4):
                slab, kw = ws_next()
                for half in range(2):
                    oc = i * 2 + half
                    for ti, (a, b) in enumerate(mtiles):
                        n = b - a
                        pso, _, ko = PS()

                        def mm(e, slab=slab, half=half, a=a, n=n, pso=pso, k0=0):
                            r = None
                            for kt in range(k0, k0 + 4):
                                r = e.matmul(pso[:, 0:n], slab[:, kt, half * 128:(half + 1) * 128], mixT[:, kt, a:a + n],
                                             start=(kt == 0), stop=(kt == 7))
                            return r
                        if os.environ.get('MK_C', '1') == '1':
                            P.pe(mm, reads=[kw] + hkeys("mix", a, b, range(4)), writes=[ko])
                            P.pe(lambda e, mm=mm: mm(e, k0=4), reads=[kw] + hkeys("mix", a, b, range(4, 8)), writes=[ko])
                        else:
                            P.pe(lambda e, mm=mm: (mm(e, k0=0), mm(e, k0=4))[1], reads=[kw] + hkeys("mix", a, b), writes=[ko])
                        na.flush(keep=len(mtiles))
                        P.dve(lambda e, oc=oc, a=a, n=n, pso=pso: e.tensor_tensor(hT[:, oc, a:a + n], hT[:, oc, a:a + n], pso[:, 0:n], ALU.add),
                              reads=[ko] + K("hT", a, b, oc), writes=K("hT", a, b, oc))
                        na.add(ti, oc)

        def ffin_stage(mtiles, final, stile=None, fillers=()):
            fillers = list(fillers)
            for j in range(NJ):
                if fillers and j % 2 == 1:
                    fillers.pop(0)()
                slab, kw = ws_next()
                if stile is not None:
                    sa, sbb = stile
                    psg, _, kg = PS()
                    psu, _, ku = PS()
                    P.pe(mm2(slab, hnT, sa, NS, psg, psu), reads=[kw] + hkeys("hnT", sa, sbb), writes=[kg, ku])
                    t1, k1 = tAR.get()
                    st3 = stT[:, j, :].rearrange("p (b k) -> p b k", k=2)
                    P.act(lambda e, j=j, t1=t1, psg=psg: e.activation(t1[:, 0:NS], psg[:, 0:NS], AF.Identity,
                                                                     scale=col(C_FCW + j * 3 + 2), bias=col(C_FCB + j)),
                          reads=[kg, "cols"], writes=[k1])
                    P.act(lambda e, j=j, psg=psg: e.activation(fgs32[:, j, :], psg[:, 0:NS], AF.Copy), reads=[kg], writes=[("fgs32", j)])
                    P.dve(lambda e, j=j, t1=t1, st3=st3: e.scalar_tensor_tensor(t1[:, 0:NS], st3[:, :, 1], col(C_FCW + j * 3 + 1), t1[:, 0:NS],
                                                                              op0=ALU.mult, op1=ALU.add),
                          reads=[("stT", j // 4), k1, "cols"], writes=[k1])
                    P.dve(lambda e, j=j, t1=t1, st3=st3: e.scalar_tensor_tensor(t1[:, 0:NS], st3[:, :, 0], col(C_FCW + j * 3 + 0), t1[:, 0:NS],
                                                                              op0=ALU.mult, op1=ALU.add),
                          reads=[("stT", j // 4), k1, "cols"], writes=[k1])
                    P.act(lambda e, t1=t1: e.activation(t1[:, 0:NS], t1[:, 0:NS], AF.Gelu), reads=[k1], writes=[k1])
                    P.dve(lambda e, j=j, sa=sa, t1=t1, psu=psu: e.tensor_tensor(actT[:, j, sa:sa + NS], psu[:, 0:NS], t1[:, 0:NS], ALU.mult),
                          reads=[ku, k1], writes=K("actT", sa, sa + NS, j))
                for ti, (a, b) in enumerate(mtiles):
                    n = b - a
                    nn = n + 2
                    psg, _, kg = PS()
                    psu, _, ku = PS()
                    if j == 0:
                        for (pst_, kst_, off) in ((psg, kg, 0), (psu, ku, 128)):
                            for k0 in (0, 4):
                                def mmh(e, slab=slab, a=a, nn=nn, pst_=pst_, off=off, k0=k0):
                                    r = None
                                    for kt in range(k0, k0 + 4):
                                        r = e.matmul(pst_[:, 0:nn], slab[:, kt, off:off + 128], hnT[:, kt, a - 2:a - 2 + nn],
                                                     start=(kt == 0), stop=(kt == 7))
                                    return r
                                P.pe(mmh, reads=[kw] + hkeys("hnT", a - 2, b, range(k0, k0 + 4)), writes=[kst_])
                    else:
                        P.pe(mm2(slab, hnT, a - 2, nn, psg, psu), reads=[kw] + hkeys("hnT", a - 2, b), writes=[kg, ku])
                    t1, k1 = tAR.get()
                    P.act(lambda e, j=j, n=n, t1=t1, psg=psg: e.activation(t1[:, 0:n], psg[:, 2:2 + n], AF.Identity,
                                                                          scale=col(C_FCW + j * 3 + 2), bias=col(C_FCB + j)),
                          reads=[kg, "cols"], writes=[k1])
                    P.dve(lambda e, j=j, n=n, t1=t1, psg=psg: e.scalar_tensor_tensor(t1[:, 0:n], psg[:, 1:1 + n], col(C_FCW + j * 3 + 1), t1[:, 0:n],
                                                                                    op0=ALU.mult, op1=ALU.add),
                          reads=[kg, k1, "cols"], writes=[k1])
                    P.dve(lambda e, j=j, n=n, t1=t1, psg=psg: e.scalar_tensor_tensor(t1[:, 0:n], psg[:, 0:n], col(C_FCW + j * 3 + 0), t1[:, 0:n],
                                                                                    op0=ALU.mult, op1=ALU.add),
                          reads=[kg, k1, "cols"], writes=[k1])
                    P.act(lambda e, n=n, t1=t1: e.activation(t1[:, 0:n], t1[:, 0:n], AF.Gelu), reads=[k1], writes=[k1])
                    P.dve(lambda e, j=j, a=a, n=n, t1=t1, psu=psu: e.tensor_tensor(actT[:, j, a:a + n], psu[:, 2:2 + n], t1[:, 0:n], ALU.mult),
                          reads=[ku, k1], writes=K("actT", a, b, j))
                    if final and ti == len(mtiles) - 1:
                        P.act(lambda e, j=j, n=n, psg=psg: e.activation(fg32[:, j, 0:2], psg[:, n:n + 2], AF.Copy),
                              reads=[kg], writes=[("fg32", j)])
            while fillers:
                fillers.pop(0)()

        def ffout_stage(mtiles, na, fillers=()):
            fillers = list(fillers)
            for oc in range(8):
                slab, kw = ws_next()
                for ti, (a, b) in enumerate(mtiles):
                    n = b - a
                    pso, _, ko = PS()

                    def mm(e, slab=slab, a=a, n=n, pso=pso):
                        r = None
                        for kt in range(NJ):
                            r = e.matmul(pso[:, 0:n], slab[:, kt, 0:128], actT[:, kt, a:a + n], start=(kt == 0), stop=(kt == NJ - 1))
                        return r
                    P.pe(mm, reads=[kw] + hkeys("actT", a, b, range(NJ)), writes=[ko])
                    na.flush(keep=len(mtiles))
                    P.dve(lambda e, oc=oc, a=a, n=n, pso=pso: e.tensor_tensor(hT[:, oc, a:a + n], hT[:, oc, a:a + n], pso[:, 0:n], ALU.add),
                          reads=[ko] + K("hT", a, b, oc), writes=K("hT", a, b, oc))
                    na.add(ti, oc)
                if fillers:
                    fillers.pop(0)()
                    if fillers and oc >= 2:
                        fillers.pop(0)()
            while fillers:
                fillers.pop(0)()

        def ple_stage(mtiles):
            for i in range(4):
                slab, kw = ws_next()
                for half in range(2):
                    oc = i * 2 + half
                    for (a, b) in mtiles:
                        n = b - a
                        psg, _, kg = PS()
                        psp, _, kp = PS()

                        def mm(e, slab=slab, half=half, a=a, n=n, psg=psg, psp=psp):
                            r = None
                            for kt in range(8):
                                r = e.matmul(psg[:, 0:n], slab[:, kt, half * 128:(half + 1) * 128], hnT[:, kt, a:a + n],
                                             start=(kt == 0), stop=(kt == 7))
                            for kt in range(2):
                                r = e.matmul(psp[:, 0:n], slab[:, 8 + kt, half * 128:(half + 1) * 128], pT[:, kt, a:a + n],
                                             start=(kt == 0), stop=(kt == 1))
                            return r
                        P.pe(mm, reads=[kw] + hkeys("hnT", a, b) + K("pT", a, b), writes=[kg, kp])
                        t1, k1 = tAR.get()
                        P.act(lambda e, n=n, t1=t1, psg=psg: e.activation(t1[:, 0:n], psg[:, 0:n], AF.Sigmoid), reads=[kg], writes=[k1])
                        P.dve(lambda e, n=n, t1=t1, psp=psp: e.tensor_tensor(psp[:, 0:n], psp[:, 0:n], t1[:, 0:n], ALU.mult),
                              reads=[kp, k1], writes=[kp])
                        P.dve(lambda e, oc=oc, a=a, n=n, psp=psp: e.tensor_tensor(hT[:, oc, a:a + n], hT[:, oc, a:a + n], psp[:, 0:n], ALU.add),
                              reads=[kp] + K("hT", a, b, oc), writes=K("hT", a, b, oc))

        sst = {}

        def samples_prefetch():
            P.dma("pool", ckb[:], ck_d.rearrange("b k f -> k b f"), writes=["ckb"])
            P.dma("pool", cvb[:], cv_d.rearrange("b k f -> k b f"), writes=["cvb"])

        def attn_samples_p1(zs, g):
            rq = [k for c in range(4) for k in K("qT", zs, zs + NS, c)] + K("kT", zs, zs + NS)
            if g == 0:
                sst["A"] = PS(hold=True)
                sst["B"] = PS(hold=True)
                sst["t"] = [PS(hold=True), PS(hold=True)]
            psA, _, kA = sst["A"]
            psB, _, kB = sst["B"]
            for pr in range(2):
                bqs = (4 * g + 2 * pr, 4 * g + 2 * pr + 1)
                kbs = []
                for i, bq in enumerate(bqs):
                    pst, pstb, kt_ = sst["t"][i]
                    P.pe(lambda e, bq=bq, pstb=pstb: e.transpose(pstb[:, 0:128], ckb[:, bq, :], ident_b[:]),
                         reads=["ckb", "ident"], writes=[kt_])
                    kb_, kkb = kTbR.get()
                    P.act(lambda e, kb_=kb_, pstb=pstb: e.activation(kb_[:], pstb[:, 0:128], AF.Copy), reads=[kt_], writes=[kkb])
                    kbs.append((kb_, kkb))
                for (kb_, kkb), bq in zip(kbs, bqs):
                    def scs(e, bq=bq, kb_=kb_):
                        r = None
                        for c in range(4):
                            cc = c * NS + bq
                            e.matmul(psA[:, cc:cc + 1], kb_[0:64, :], qT[0:64, c, zs + bq:zs + bq + 1], start=True, stop=True)
                            r = e.matmul(psB[:, cc:cc + 1], kb_[64:128, :], qT[64:128, c, zs + bq:zs + bq + 1], start=True, stop=True)
                        return r
                    P.pe(scs, reads=[kkb] + rq, writes=[kA, kB])
            if g < 3:
                return
            P.act(lambda e: e.activation(EAB[0][:], psA[:, 0:64], AF.Exp, scale=HD ** -0.5), reads=[kA], writes=["EA"])
            P.act(lambda e: e.activation(EAB[1][:], psB[:, 0:64], AF.Exp, scale=HD ** -0.5), reads=[kB], writes=["EB"])
            PSrel(kA, kB, sst["t"][0][2], sst["t"][1][2])
            sst["o"] = PS(hold=True)
            sst["d"] = PS(hold=True)
            pso, _, ko = sst["o"]
            psd, _, kd = sst["d"]

            def pvs(e):
                r = None
                for bq in range(NS):
                    for c in range(4):
                        cc = c * NS + bq
                        e.matmul(pso[0:64, cc:cc + 1], cvb[:, bq, 0:64], EAB[0][:, cc:cc + 1], start=True, stop=True)
                        r = e.matmul(pso[64:128, cc:cc + 1], cvb[:, bq, 64:128], EAB[1][:, cc:cc + 1], start=True, stop=True)
                e.matmul(psd[0:64, 0:64], ones_b[:, 0:64], EAB[0][:, 0:64], start=True, stop=True)
                r = e.matmul(psd[64:128, 0:64], ones_b[:, 64:128], EAB[1][:, 0:64], start=True, stop=True)
                return r
            P.pe(pvs, reads=["EA", "EB", "cvb", "ones_b"], writes=[ko, kd])

        def attn_samples_p2(zs, ms):
            rq = [k for c in range(4) for k in K("qT", zs, zs + NS, c)] + K("kT", zs, zs + NS)
            pso, _, ko = sst["o"]
            psd, _, kd = sst["d"]
            for c in range(4):
                P.dve(lambda e, c=c: e.tensor_tensor(sm["prod"][:, c * NS:(c + 1) * NS], qT[:, c, zs:zs + NS], kT[:, zs:zs + NS], ALU.mult),
                      reads=rq + ["sm_prod"], writes=["sm_prod"])
            psn, _, kn = PS()
            P.pe(lambda e: e.matmul(psn[:, 0:64], bd_f[:], sm["prod"][:], start=True, stop=True), reads=["sm_prod", "bd_f"], writes=[kn])
            P.act(lambda e: e.activation(sm["enew"][:], psn[:, 0:64], AF.Exp, scale=HD ** -0.5), reads=[kn], writes=["sm_enew"])
            for c in range(4):
                P.dve(lambda e, c=c: e.scalar_tensor_tensor(sm["dtot"][:, c * NS:(c + 1) * NS], psd[:, c * NS:(c + 1) * NS], sinkc[:, c:c + 1],
                                                            sm["enew"][:, c * NS:(c + 1) * NS], op0=ALU.add, op1=ALU.add),
                      reads=[kd, "sm_enew", "sinkc", "sm_dtot"], writes=["sm_dtot"])
            P.act(lambda e: e.activation(sm["dtot"][:], sm["dtot"][:], AF.Ln), reads=["sm_dtot"], writes=["sm_dtot"])
            P.act(lambda e: e.activation(sm["dtot"][:], sm["dtot"][:], AF.Exp, scale=-1.0), reads=["sm_dtot"], writes=["sm_dtot"])
            for c in range(4):
                P.dve(lambda e, c=c: e.tensor_tensor(sm["ev"][:, c * NS:(c + 1) * NS], sm["enew"][:, c * NS:(c + 1) * NS], vTs[:], ALU.mult),
                      reads=["sm_enew", "vTs", "sm_ev"], writes=["sm_ev"])
            P.dve(lambda e: e.tensor_tensor(sm["otot"][:], pso[:, 0:64], sm["ev"][:], ALU.add), reads=[ko, "sm_ev"], writes=["sm_otot"])
            P.dve(lambda e: e.tensor_tensor(mixT[:, 0:4, ms:ms + NS], sm["otot"][:].rearrange("p (a b) -> p a b", a=4),
                                            sm["dtot"][:].rearrange("p (a b) -> p a b", a=4), ALU.mult),
                  reads=["sm_otot", "sm_dtot"], writes=hkeys("mix", ms, ms + NS, range(4)))
            PSrel(ko, kd)

        cst_ = {}

        if with_samples:
            sext = actT[:, 0:8, :].rearrange("p a b -> p (a b)").bitcast(F32)[0:124, 0:2048].rearrange("p (g c) -> p g c", g=4)
            SK = [k for j in range(8) for k in K("actT", 0, MW, j)]
            sdummy = sb("sdummy", (128, 1))

        def sext_claim():
            P.pool(lambda e: e.memset(sdummy[:], 0.0), writes=SK + [("sext", g) for g in range(4)])

        def sext_release():
            P.pool(lambda e: e.memset(sdummy[:], 0.0), reads=[("sext", g) for g in range(4)] + [("sextr", g, b4) for g in range(4) for b4 in range(4)], writes=SK)

        cst_ = {}

        def samples_prefetch2():
            sext_claim()
            for g in range(4):
                for bl_ in range(4):
                    bq = 4 * g + bl_
                    P.dma("sp", sext[bl_ * 30:bl_ * 30 + 30, g, :], sc_d[bq, :, :], reads=[("sext", g)], writes=[("sextr", g, bl_)])
            for bl_ in range(4):
                P.dma("sp", wrep[bl_ * 30:bl_ * 30 + 30, :], cwt_d[0:30, :], writes=[("wrep", bl_)])
            for i in range(3):
                P.dma("sp", crows[32 * i:32 * i + 1, :], cbt_d[i:i + 1, :], writes=[("crow", i)])
            P.dma("sp", cst[0:1, :], cwt_d[30:31, :], writes=["cst"])
            P.pool(lambda e: e.memset(onesf16[:], 1.0), writes=["onesf16"])
            P.dve(lambda e: e.tensor_scalar(cs_col32[:, 0:1], cols2[:, 3:4], 30.0, None, op0=ALU.is_ge), reads=["c2_3"], writes=["cc0"])
            P.dve(lambda e: e.tensor_scalar(cs_col32[:, 1:2], cols2[:, 3:4], 60.0, None, op0=ALU.is_ge), reads=["c2_3"], writes=["cc1"])
            P.dve(lambda e: e.tensor_scalar(cs_col32[:, 2:3], cols2[:, 3:4], 90.0, None, op0=ALU.is_ge), reads=["c2_3"], writes=["cc2"])
            P.dve(lambda e: e.tensor_tensor(cs_col32[:, 0:1], cs_col32[:, 0:1], cs_col32[:, 1:2], ALU.add), reads=["cc0", "cc1"], writes=["cc0"])
            P.dve(lambda e: e.tensor_tensor(cs_col32[:, 0:1], cs_col32[:, 0:1], cs_col32[:, 2:3], ALU.add), reads=["cc0", "cc2"], writes=["cc0"])
            for g in range(4):
                P.dve(lambda e, g=g: e.tensor_scalar(cs_col32[:, 3 + g:4 + g], cs_col32[:, 0:1], float(4 * g), None, op0=ALU.add),
                      reads=["cc0"], writes=[("ccg", g)])
                P.dve(lambda e, g=g: e.tensor_scalar(selT[0:120, g, :], iot_f[0:120, 0:NS], cs_col32[0:120, 3 + g:4 + g], None, op0=ALU.is_equal),
                      reads=["iot_f", ("ccg", g)], writes=[("selT", g)])

        def conv_samples_a(ms):
            pst, _, kpt_ = PS()

            def trg_(e):
                r = None
                for c in range(4):
                    r = e.transpose(pst[0:NS, c * 128:(c + 1) * 128], gs32[:, c, :], ident_f[:])
                return r
            P.pe(trg_, reads=[("gs32", c) for c in range(4)] + ["ident_f"], writes=[kpt_])
            P.dve(lambda e: e.tensor_copy(gs_tok[:], pst[0:NS, :]), reads=[kpt_], writes=["gs_tok"])
            P.dma("sp", nconvs[:, 29, :], gs_tok[:], reads=["gs_tok"])
            P.dma("sp", nconvs[:, 0:29, :], sc_d[:, 1:30, :])

        def conv_samples_b(ms):
            psc, _, kc = PS(hold=True)
            for g in range(4):
                P.dve(lambda e, g=g: e.tensor_tensor(sext[0:120, g, :], sext[0:120, g, :], wrep[0:120, :], ALU.mult),
                      reads=[("sext", g)] + [("sextr", g, b4) for b4 in range(4)] + [("wrep", b4) for b4 in range(4)], writes=[("sext", g)])
                P.pe(lambda e, g=g: e.matmul(psc[0:NS, :], selT[0:120, g, :], sext[0:120, g, :], start=(g == 0), stop=False),
                     reads=[("sext", g), ("selT", g)], writes=[kc])
            psw, _, kw_ = PS()
            P.pe(lambda e: e.matmul(psw[0:NS, :], onesf16[0:1, :], cst[0:1, :], start=True, stop=True), reads=["cst", "onesf16"], writes=[kw_])
            p30t, kp30 = tAR.get()
            P.dve(lambda e: e.tensor_tensor(p30t[0:NS, :], gs_tok[:], psw[0:NS, :], ALU.mult), reads=["gs_tok", kw_], writes=[kp30])
            P.pe(lambda e: e.matmul(psc[0:NS, :], ident_f[0:NS, 0:NS], p30t[0:NS, :], start=False, stop=False),
                 reads=[kp30, "ident_f"], writes=[kc])
            P.pe(lambda e: e.matmul(psc[0:NS, :], onesf16[0:1, :], crows[0:1, :], start=False, stop=True),
                 reads=[("crow", 0), "onesf16"], writes=[kc])
            psg_, _, kg_ = PS(hold=True)
            psb_, _, kb_ = PS(hold=True)
            P.pe(lambda e: e.matmul(psg_[0:NS, :], onesf16[32:33, :], crows[32:33, :], start=True, stop=True), reads=[("crow", 1), "onesf16"], writes=[kg_])
            P.pe(lambda e: e.matmul(psb_[0:NS, :], onesf16[64:65, :], crows[64:65, :], start=True, stop=True), reads=[("crow", 2), "onesf16"], writes=[kb_])
            cst_.update(psc=psc, kc=kc, psg_=psg_, kg_=kg_, psb_=psb_, kb_=kb_)

        def conv_samples_c(ms):
            cs_tok_t, kcs1 = tAR.get()
            cs_t2_t, kcs2 = tAR.get()
            cs_tok = cs_tok_t[0:NS, :]
            cs_t2 = cs_t2_t[0:NS, :]
            psc, kc, psg_, kg_, psb_, kb_ = cst_["psc"], cst_["kc"], cst_["psg_"], cst_["kg_"], cst_["psb_"], cst_["kb_"]
            P.act(lambda e: e.activation(cs_tok, psc[0:NS, :], AF.Identity, accum_out=cs_col[:, 0:1]), reads=[kc], writes=[kcs1, "csc0"])
            P.dve(lambda e: e.tensor_scalar(cs_col[:, 1:2], cs_col[:, 0:1], -1.0 / CONV_CH, None, op0=ALU.mult), reads=["csc0"], writes=["csc1"])
            P.act(lambda e: e.activation(cs_t2, cs_tok, AF.Square, bias=cs_col[:, 1:2], accum_out=cs_col[:, 2:3]),
                  reads=[kcs1, "csc1"], writes=[kcs2, "csc2"])
            P.act(lambda e: e.activation(cs_col[:, 3:4], cs_col[:, 2:3], AF.Ln, scale=1.0 / CONV_CH, bias=EPS), reads=["csc2"], writes=["csc3"])
            P.act(lambda e: e.activation(cs_col[:, 4:5], cs_col[:, 3:4], AF.Exp, scale=-0.5), reads=["csc3"], writes=["csc4"])
            P.dve(lambda e: e.tensor_scalar(cs_tok, cs_tok, cs_col[:, 1:2], cs_col[:, 4:5], op0=ALU.add, op1=ALU.mult),
                  reads=[kcs1, "csc1", "csc4", kcs2], writes=[kcs1])
            P.dve(lambda e: e.tensor_tensor(cs_tok, cs_tok, psg_[0:NS, :], ALU.mult), reads=[kcs1, kg_], writes=[kcs1])
            P.dve(lambda e: e.tensor_tensor(cs_tok, cs_tok, psb_[0:NS, :], ALU.add), reads=[kcs1, kb_], writes=[kcs1])
            P.act(lambda e: e.activation(cs_t2, cs_tok, AF.Silu), reads=[kcs1, kcs2], writes=[kcs2])
            pso_, _, ko_ = PS()

            def tro(e):
                r = None
                for c in range(4):
                    r = e.transpose(pso_[:, c * NS:(c + 1) * NS], cs_t2_t[0:NS, c * 128:(c + 1) * 128], ident_f[0:NS, 0:NS])
                return r
            P.pe(tro, reads=[kcs2, "ident_f"], writes=[ko_])
            P.dve(lambda e: e.tensor_copy(mixT[:, 4:8, ms:ms + NS], pso_[:, 0:4 * NS].rearrange("p (a b) -> p a b", a=4)),
                  reads=[ko_], writes=hkeys("mix", ms, ms + NS, range(4, 8)))
            PSrel(kc, kg_, kb_)

        def build_stT():
            sfv = sf_d.rearrange("b k f -> (b k) f")
            for g in range(6):
                j0 = g * 4
                j1 = min(NJ, j0 + 4)
                w_ = (j1 - j0) * 128
                slt, ksl = tAR.get()
                sl = slt[0:2 * NS, :]
                P.dma("sp", sl[:, 0:w_], sfv[:, j0 * 128:j1 * 128], writes=[ksl])
                pst, _, kpt_ = PS()

                def trs(e, j0=j0, j1=j1, sl=sl, pst=pst):
                    r = None
                    for j in range(j0, j1):
                        r = e.transpose(pst[:, (j - j0) * 32:(j - j0 + 1) * 32], sl[:, (j - j0) * 128:(j - j0 + 1) * 128], ident_f[0:32, 0:32])
                    return r
                P.pe(trs, reads=[ksl, "ident_f"], writes=[kpt_])
                P.dve(lambda e, j0=j0, j1=j1, pst=pst: e.tensor_copy(
                    stT[:, j0:j1, :], pst[:, 0:(j1 - j0) * 32].rearrange("p (a b) -> p a b", a=j1 - j0)),
                    reads=[kpt_], writes=[("stT", g)])
            P.dma("sp", nffns[:, 0, :], sf_d[:, 1, :])

        def sample_state_outputs():
            P.dma("sp", nks[:, 0:127, :], ck_d[:, 1:128, :])
            P.dma("sp", nvs[:, 0:127, :], cv_d[:, 1:128, :])
            P.dma("sp", nvs[:, 127, :], vs_tok[:], reads=["vs_tok"])
            psk_, _, kk_ = PS()
            P.pe(lambda e: e.transpose(psk_[0:NS, 0:128], ks32[:], ident_f[:]), reads=["ks32", "ident_f"], writes=[kk_])
            ot, ko = outR.get()
            P.dve(lambda e: e.tensor_copy(ot[0:NS, :], psk_[0:NS, 0:128]), reads=[kk_], writes=[ko])
            P.dma("sp", nks[:, 127, :], ot[0:NS, :], reads=[ko])
            for g in range(6):
                j0 = g * 4
                j1 = min(NJ, j0 + 4)
                psf, _, kpf = PS()

                def trg2(e, j0=j0, j1=j1, psf=psf):
                    r = None
                    for j in range(j0, j1):
                        r = e.transpose(psf[0:NS, (j - j0) * 128:(j - j0 + 1) * 128], fgs32[:, j, :], ident_f[:])
                    return r
                P.pe(trg2, reads=[("fgs32", j) for j in range(j0, j1)] + ["ident_f"], writes=[kpf])
                ft, kft = tAR.get()
                P.dve(lambda e, j0=j0, j1=j1, psf=psf, ft=ft: e.tensor_copy(ft[0:NS, 0:(j1 - j0) * 128], psf[0:NS, 0:(j1 - j0) * 128]),
                      reads=[kpf], writes=[kft])
                P.dma("sp", nffns[:, 1, j0 * 128:j1 * 128], ft[0:NS, 0:(j1 - j0) * 128], reads=[kft])

        def load_p_block(src, n, mc):
            pt, kpt = ptR.get()
            P.dma("pool", pt[0:n, :], src, writes=[kpt])
            ps, psb, kp = PS()

            def tr(e):
                e.transpose(psb[:, 0:n], pt[0:n, 0:128], ident_b[0:n, 0:n])
                return e.transpose(psb[:, 128:128 + n], pt[0:n, 128:256], ident_b[0:n, 0:n])
            P.pe(tr, reads=[kpt, "ident"], writes=[kp])
            P.act(lambda e: e.activation(pT[:, 0:2, mc:mc + n], psb[:, 0:256].rearrange("p (a b) -> p a b", a=2)[:, :, 0:n], AF.Copy),
                  reads=[kp], writes=K("pT", mc, mc + n))

        def out_block(dst, n, mc):
            yt, ky = ytR.get()
            for half in range(2):
                ps2, _, kp2 = PS()

                def trf(e, half=half, ps2=ps2):
                    r = None
                    for a4 in range(4):
                        kt = half * 4 + a4
                        r = e.transpose(ps2[0:n, a4 * 128:(a4 + 1) * 128], hT[:, kt, mc:mc + n], ident_f[:])
                    return r
                P.pe(trf, reads=hkeys("hT", mc, mc + n, range(half * 4, half * 4 + 4)) + ["ident_f"], writes=[kp2])
                if half == 0:
                    P.act(lambda e, ps2=ps2: e.activation(yt[0:n, 0:512], ps2[0:n, :], AF.Copy), reads=[kp2], writes=[ky])
                else:
                    P.dve(lambda e, ps2=ps2: e.tensor_copy(yt[0:n, 512:1024], ps2[0:n, :]), reads=[kp2, ky], writes=[ky])
            P.dma("sp", dst, yt[0:n, :], reads=[ky])

        pend_out = []

        def s0a_list(pi, pipelined=True):
            pipelined = pipelined and os.environ.get('MK_A', '1') == '1'
            nb = PASS_BLOCKS[pi]
            s0_ = sum(PASS_BLOCKS[:pi]) * 128
            M = nb * 128
            last = pi == NPASS - 1
            blocks = []
            if pi == 0:
                blocks += [(xh[0:128, :], 128, 0), (xh[128:256, :], 128, 128)]
            for bl in range(nb):
                r0 = 256 + s0_ + bl * 128
                blocks.append((xh[r0:r0 + 128, :], 128, 256 + bl * 128))
            if last and with_samples:
                blocks.append((xs_d, NS, 256 + M))
            fl = [lambda: rope_tables(pi, s0_, M, last)]
            if not pipelined:
                for (src, n, zc) in blocks:
                    fl.append(lambda src=src, n=n, zc=zc: stage0a(src, n, zc))
                return fl
            st_ = {}

            def step(i):
                if i < len(blocks):
                    src, n, zc = blocks[i]
                    st_[i] = stage0a_front(src, n)
                if i - 1 >= 0:
                    src, n, zc = blocks[i - 1]
                    xb, kxb = st_.pop(i - 1)
                    stage0a_back(xb, kxb, n, zc)
            for i in range(len(blocks) + 1):
                fl.append(lambda i=i: step(i))
            return fl

        def run_pass(pi, nb, s0):
            while ws["used"] < pi * 48:
                ws_next()
            M = nb * 128
            last = pi == NPASS - 1
            samp = last and with_samples
            zs, ms = 256 + M, 2 + M
            if pi == 0:
                for f in s0a_list(0, pipelined=True):
                    f()
            ztiles = split_tiles(0 if pi == 0 else 256, 256 + M)
            fl = list(pend_out)
            del pend_out[:]
            if samp:
                samples_prefetch()
                samples_prefetch2()
                while len(fl) < 4:
                    fl.append(lambda: None)
                for g in range(4):
                    fl.append(lambda g=g: attn_samples_p1(zs, g))
            z_stage(pi, ztiles, last, stile=(zs, zs + NS) if samp else None, fillers=fl)
            blocks = [(1, 0, True, False)] if pi == 0 else []
            for bl in range(nb):
                blocks.append((2 + bl, 2 + bl * 128, False, pi == 0 and bl == 0))
            attn_pass(blocks, nob=1 if samp else 2)
            m_lo = 0 if pi == 0 else 2
            mt_all = split_tiles(m_lo, 2 + M)
            mt_main = split_tiles(2, 2 + M)
            fl = []
            if pi == 0:
                fl.append(lambda: stage0b(xh[128:256, :], 128, None, halo2=True))
            for bl in range(nb):
                r0 = 256 + s0 + bl * 128
                fl.append(lambda r0=r0, bl=bl: stage0b(xh[r0:r0 + 128, :], 128, 2 + bl * 128))
                fl.append(lambda bl=bl: load_p_block(pp[s0 + bl * 128:s0 + (bl + 1) * 128, :], 128, 2 + bl * 128))
            if samp:
                flp = list(fl)
                fl = [lambda: (conv_samples_a(ms), attn_samples_p2(zs, ms), flp[0](), flp[1]()),
                      lambda: (conv_samples_b(ms), flp[2](), flp[3](), stage0b(xs_d, NS, ms)),
                      lambda: (conv_samples_c(ms), sext_release(), flp[4](), flp[5](), load_p_block(ps_d, NS, ms)),
                      lambda: (flp[6](), flp[7](), build_stT())]
            conv_stage(mt_all, fillers=fl)
            if samp:
                mt_all = mt_all + [(ms, ms + NS)]
                mt_main = mt_main + [(ms, ms + NS)]
            na = NormAcc(mt_all)
            wout_stage(mt_all, na)
            na.finish(C_GFFN, flag_halo=(pi == 0))
            if pi > 0:
                P.pool(lambda e: e.tensor_copy(hnT[:, :, 0:2], hcarry[:]), reads=["hcarry"] + hkeys("hnT", 0, 2),
                       writes=hkeys("hnT", 0, 2))
            if not last:
                P.pool(lambda e, M=M: e.tensor_copy(hcarry[:], hnT[:, :, M:M + 2]),
                       reads=hkeys("hnT", M, M + 2), writes=["hcarry"])
            if not last:
                P.pool(lambda e, M=M: e.tensor_copy(kT[:, 128:256], kT[:, 128 + M:256 + M]),
                       reads=K("kT", 128 + M, 256 + M), writes=K("kT", 128, 256))
                P.pool(lambda e, nb=nb: e.tensor_copy(vS[:, 1, :], vS[:, 1 + nb, :]),
                       reads=K("vS", 128 + M, 256 + M), writes=K("vS", 128, 256))
                P.pool(lambda e, M=M: e.tensor_copy(gluT[:, :, 224:256], gluT[:, :, 224 + M:256 + M]),
                       reads=hkeys("gluT", 224 + M, 256 + M, range(4)), writes=hkeys("gluT", 224, 256, range(4)))
            fl = s0a_list(pi + 1) if not last else []
            ffin_stage(split_even(2, 2 + M, 510), last, stile=(ms, ms + NS) if samp else None)
            na = NormAcc(mt_main)
            ffout_stage(mt_main, na, fillers=fl)
            na.finish(C_GPLE)
            ple_stage(mt_main)
            outs = [(lambda bl=bl: out_block(y[s0 + bl * 128:s0 + (bl + 1) * 128, :], 128, 2 + bl * 128)) for bl in range(nb)]
            if last:
                for f in outs:
                    f()
                if samp:
                    out_block(ys, NS, ms)
                    sample_state_outputs()
            else:
                pend_out.extend(outs)

        s0 = 0
        for pi, nb in enumerate(PASS_BLOCKS):
            if pi < npass_run:
                run_pass(pi, nb, s0)
            s0 += nb * 128

        if stop < 99 or npass_run < 99:
            if debug:
                dbg = nc.dram_tensor("dbg", [8, 128, 1024], F32, kind="ExternalOutput").ap()
                dt_ = sb("dbgt", (128, 1024))
                for i, (src, kk) in enumerate(debug_srcs(locals())):
                    P.dve(lambda e, src=src: e.tensor_copy(dt_[:, 0:src.shape[1]], src), reads=kk + [("dbgt",)], writes=[("dbgt",)])
                    P.dma("sp", dbg[i, :, 0:src.shape[1]], dt_[:, 0:src.shape[1]], reads=[("dbgt",)], writes=[("dbgt",)])
            P.finalize(st)
            nc._mk_stats = P.stats
            return nc

        psk, _, kpk = PS()
        P.pe(lambda e: e.transpose(psk[:, 0:128], kf32[:], ident_f[:]), reads=["kf32", "ident_f"], writes=[kpk])
        ot, ko = outR.get()
        P.dve(lambda e: e.tensor_copy(ot[:], psk[:, 0:128]), reads=[kpk], writes=[ko])
        P.dma("sp", nk, ot[:], reads=[ko])
        psc, _, kpc = PS()

        def trc(e):
            r = None
            for c in range(4):
                r = e.transpose(psc[0:32, c * 128:(c + 1) * 128], gl32[:, c, :], ident_f[:])
            return r
        P.pe(trc, reads=[("gl32", c) for c in range(4)] + ["ident_f"], writes=[kpc])
        P.dve(lambda e: e.tensor_copy(cst[:], psc[0:32, :]), reads=[kpc], writes=["cst"])
        P.dma("sp", nconv, cst[2:32, :], reads=["cst"])
        for g in range(6):
            j0 = g * 4
            j1 = min(NJ, j0 + 4)
            psf, _, kpf = PS()

            def trg(e, j0=j0, j1=j1, psf=psf):
                r = None
                for j in range(j0, j1):
                    r = e.transpose(psf[0:2, (j - j0) * 128:(j - j0 + 1) * 128], fg32[:, j, :], ident_f[:])
                return r
            P.pe(trg, reads=[("fg32", j) for j in range(j0, j1)] + ["ident_f"], writes=[kpf])
            ft, kft = tAR.get()
            P.dve(lambda e, j0=j0, j1=j1, psf=psf, ft=ft: e.tensor_copy(ft[0:2, 0:(j1 - j0) * 128], psf[0:2, 0:(j1 - j0) * 128]),
                  reads=[kpf], writes=[kft])
            P.dma("sp", nffn[:, j0 * 128:j1 * 128], ft[0:2, 0:(j1 - j0) * 128], reads=[kft])

        P.finalize(st)
        nc._mk_stats = P.stats
    return nc


def _prep_weights(inp):
    w_in = np.asarray(inp["w_in"][0])
    rot = np.concatenate([np.arange(32, 64), np.arange(0, 32)])
    cols_list = []
    for c in range(4):
        for hrot in (False, True):
            cc = []
            for h in (c, c + 4):
                base = h * HD
                cc.append(base + (rot if hrot else np.arange(HD)))
            cols_list.append(np.concatenate(cc))
    kbase = 512
    cols_list.append(np.concatenate([kbase + np.arange(64), kbase + 64 + np.arange(64)]))
    cols_list.append(np.concatenate([kbase + rot, kbase + 64 + rot]))
    for c in range(4):
        cols_list.append(768 + c * 128 + np.arange(128))
        cols_list.append(768 + 512 + c * 128 + np.arange(128))
    perm = np.concatenate(cols_list)
    w_in_p = np.ascontiguousarray(w_in[:, perm])
    w_v = np.ascontiguousarray(w_in[:, 640:768])
    w_out = np.asarray(inp["w_out"][0])
    rows = []
    for c in range(4):
        rows.append(c * HD + np.arange(HD))
        rows.append((c + 4) * HD + np.arange(HD))
    rows.append(512 + np.arange(512))
    w_out_p = np.ascontiguousarray(w_out[np.concatenate(rows), :])
    w_ffi = np.asarray(inp["w_ffn_in"][0])
    idx = []
    for j in range(NJ):
        idx.append(j * 128 + np.arange(128))
        idx.append(D_FF + j * 128 + np.arange(128))
    w_ffi_p = np.ascontiguousarray(w_ffi[:, np.concatenate(idx)])
    w_ffo = np.ascontiguousarray(np.asarray(inp["w_ffn_out"][0]))
    w_ple = np.ascontiguousarray(np.concatenate([np.asarray(inp["w_ple_gate"][0]), np.asarray(inp["w_ple_proj"][0])], axis=0))
    return w_in_p, w_v, w_out_p, w_ffi_p, w_ffo, w_ple


def _prep_cols(inp, flag, posb):
    c = np.zeros((128, NCOL), np.float32)
    p = np.arange(128)
    d = p % HD
    c[:, C_GMIX:C_GMIX + 8] = np.asarray(inp["norm_mix_g"][0]).reshape(8, 128).T
    c[:, C_GFFN:C_GFFN + 8] = np.asarray(inp["norm_ffn_g"][0]).reshape(8, 128).T
    c[:, C_GPLE:C_GPLE + 8] = np.asarray(inp["norm_ple_g"][0]).reshape(8, 128).T
    gq = np.asarray(inp["q_norm_g"][0])
    gk = np.asarray(inp["k_norm_g"][0])
    c[:, C_GQ] = gq[d]
    c[:, C_GQR] = gq[(d + 32) % HD]
    c[:, C_GK] = gk[d]
    c[:, C_GKR] = gk[(d + 32) % HD]
    c[:, C_SGN] = np.where(d < 32, -1.0, 1.0)
    c[:, C_CONVB:C_CONVB + 4] = np.asarray(inp["conv_b"][0]).reshape(4, 128).T
    c[:, C_LNG:C_LNG + 4] = np.asarray(inp["conv_ln_g"][0]).reshape(4, 128).T
    c[:, C_LNB:C_LNB + 4] = np.asarray(inp["conv_ln_b"][0]).reshape(4, 128).T
    c[:, C_FLAG] = flag
    sk = np.asarray(inp["attn_sinks"][0])
    for cc in range(4):
        c[0:64, C_SINKC + cc] = sk[cc]
        c[64:128, C_SINKC + cc] = sk[cc + 4]
    c[:, C_INVF] = (np.float32(10000.0) ** (-(np.arange(128) % 32).astype(np.float32) / np.float32(32.0))).astype(np.float32)
    c[:, C_POSB] = posb
    c[:, C_FCB:C_FCB + NJ] = np.asarray(inp["ffn_conv_b"][0]).reshape(NJ, 128).T
    fw = np.asarray(inp["ffn_conv_w"][0])
    c[:, C_FCW:C_FCW + NJ * 3] = fw.reshape(3, NJ, 128).transpose(2, 1, 0).reshape(128, NJ * 3)
    cw = np.asarray(inp["conv_w"][0])
    c[:, C_CONVW:C_CONVW + 4 * 31] = cw.reshape(31, 4, 128).transpose(2, 1, 0).reshape(128, 4 * 31)
    return c


_NC_CACHE = {}


def kernel(**inp):
    inp = {k: np.asarray(v) for k, v in inp.items()}
    with_samples = True
    key = ("nc", with_samples)
    if key not in _NC_CACHE:
        _NC_CACHE[key] = build_nc(with_samples=with_samples)
    nc = _NC_CACHE[key]
    w_in_p, w_v, w_out_p, w_ffi_p, w_ffo, w_ple = _prep_weights(inp)
    xp = inp["x_prompt"]
    ppr = inp["p_prompt"][0]
    sinks = inp["attn_sinks"][0]
    sinkrow = np.zeros((1, 512), np.float32)
    for c in range(4):
        sinkrow[0, c * 128:c * 128 + 64] = sinks[c]
        sinkrow[0, c * 128 + 64:c * 128 + 128] = sinks[c + 4]
    jj = np.arange(128)
    mprev = (jj[:, None] >= jj[None, :]).astype(np.float32)
    in_maps = []
    for core in range(8):
        b, half = core // 2, core % 2
        if half == 0:
            xhh = np.concatenate([np.zeros((256, D_MODEL), np.float32), xp[b, 0:2048]], axis=0)
            m0 = np.zeros((128, 128), np.float32)
        else:
            xhh = np.ascontiguousarray(xp[b, 2048 - 256:4096])
            m0 = mprev
        cols = _prep_cols(inp, float(half), float(half * 2048))
        m = {
            "xh": np.ascontiguousarray(xhh), "pp": np.ascontiguousarray(ppr[b, half * 2048:(half + 1) * 2048]),
            "w_in": w_in_p, "w_v": w_v, "w_out": w_out_p, "w_ffi": w_ffi_p, "w_ffo": w_ffo, "w_ple": w_ple,
            "cols": cols, "sinkrow": sinkrow, "mask0": m0,
        }
        if with_samples:
            sl = slice(core * NS, (core + 1) * NS)
            m["xs"] = np.ascontiguousarray(inp["x_sample"][sl, 0, :])
            m["ps"] = np.ascontiguousarray(inp["p_sample"][0, sl, 0, :])
            m["ck"] = np.ascontiguousarray(inp["state_attn_k"][0, sl].reshape(NS, 128, 128))
            m["cv"] = np.ascontiguousarray(inp["state_attn_v"][0, sl].reshape(NS, 128, 128))
            m["sc"] = np.ascontiguousarray(inp["state_conv"][0, sl])
            m["sf"] = np.ascontiguousarray(inp["state_ffn_conv"][0, sl])
            m["cwt"] = np.ascontiguousarray(inp["conv_w"][0])
            m["cbt"] = np.ascontiguousarray(np.stack([inp["conv_b"][0], inp["conv_ln_g"][0], inp["conv_ln_b"][0],
                                                      np.zeros(CONV_CH, np.float32)], axis=0))
        in_maps.append(m)
    res = run_bass_kernel_spmd(nc, in_maps, core_ids=list(range(8)))
    R = res.results
    y_prompt = np.zeros((4, SEQ, D_MODEL), np.float32)
    nk = np.zeros((1, 4, WINDOW, 2, HD), np.float32)
    nv = np.zeros((1, 4, WINDOW, 2, HD), np.float32)
    ncv = np.zeros((1, 4, CONV_K - 1, CONV_CH), np.float32)
    nff = np.zeros((1, 4, 2, D_FF), np.float32)
    for core in range(8):
        b, half = core // 2, core % 2
        y_prompt[b, half * 2048:(half + 1) * 2048] = R[core]["y"]
        if half == 1:
            nk[0, b] = R[core]["nk"].reshape(WINDOW, 2, HD)
            nv[0, b] = R[core]["nv"].reshape(WINDOW, 2, HD)
            ncv[0, b] = R[core]["nconv"]
            nff[0, b] = R[core]["nffn"]
    y_sample = np.zeros((128, 1, D_MODEL), np.float32)
    nks = np.zeros((1, 128, WINDOW, 2, HD), np.float32)
    nvs = np.zeros((1, 128, WINDOW, 2, HD), np.float32)
    ncs = np.zeros((1, 128, CONV_K - 1, CONV_CH), np.float32)
    nfs = np.zeros((1, 128, 2, D_FF), np.float32)
    if with_samples:
        for core in range(8):
            sl = slice(core * NS, (core + 1) * NS)
            y_sample[sl, 0] = R[core]["ys"]
            nks[0, sl] = R[core]["nks"].reshape(NS, WINDOW, 2, HD)
            nvs[0, sl] = R[core]["nvs"].reshape(NS, WINDOW, 2, HD)
            ncs[0, sl] = R[core]["nconvs"]
            nfs[0, sl] = R[core]["nffns"]
    return (y_prompt, y_sample, nk, nv, ncv, nff, nks, nvs, ncs, nfs)
```

```python
import math
from contextlib import ExitStack

import numpy as np
import concourse.bass as bass
import concourse.mybir as mybir
from concourse.bass_utils import run_bass_kernel_spmd

F32 = mybir.dt.float32
BF16 = mybir.dt.bfloat16
I32 = mybir.dt.int32
AF = mybir.ActivationFunctionType
ALU = mybir.AluOpType

D_MODEL = 1024
SEQ = 4096
NB_CORE = 16
HD = 64
WINDOW = 128
CONV_CH = 512
CONV_K = 31
D_FF = 2816
NJ = D_FF // 128
PLE = 256
EPS = 1e-6
PAST_LEN = 8192
NS = 16
TWO_PI = 2.0 * math.pi
CW1 = 6.28125
CW2 = TWO_PI - CW1

C_GMIX, C_GFFN, C_GPLE = 0, 8, 16
C_GQ, C_GQR, C_GK, C_GKR, C_SGN = 24, 25, 26, 27, 28
C_CONVB, C_LNG, C_LNB = 33, 37, 41
C_FLAG, C_POSB = 45, 46
C_INVF = 29
C_FCB = 47
C_FCW = 69
C_CONVW = 135
C_SINKC = 135 + 4 * 31
NCOL = C_SINKC + 4

PASS_BLOCKS = [4, 4, 4, 4]
SLOT = 2816
NSLOT = 4


class Op:
    __slots__ = ("eng", "fn", "reads", "writes", "dma", "deps", "tok", "inc", "idx")

    def __init__(self, eng, fn, reads, writes, dma):
        self.eng = eng
        self.fn = fn
        self.reads = tuple(reads)
        self.writes = tuple(writes)
        self.dma = dma
        self.deps = ()
        self.tok = None
        self.inc = False
        self.idx = -1


class Prog:
    ENGS = ("pe", "act", "dve", "pool", "sp")

    def __init__(self, nc, dma_pool=8):
        self.nc = nc
        self.ops = []
        self.dma_pool = dma_pool

    def op(self, eng, fn, reads=(), writes=(), dma=False):
        o = Op(eng, fn, reads, writes, dma)
        o.idx = len(self.ops)
        self.ops.append(o)
        return o

    def pe(self, fn, reads=(), writes=()):
        return self.op("pe", fn, reads, writes)

    def act(self, fn, reads=(), writes=()):
        return self.op("act", fn, reads, writes)

    def dve(self, fn, reads=(), writes=()):
        return self.op("dve", fn, reads, writes)

    def pool(self, fn, reads=(), writes=()):
        return self.op("pool", fn, reads, writes)

    def dma(self, eng, out, in_, reads=(), writes=(), **kw):
        def fn(e, out=out, in_=in_, kw=kw):
            return e.dma_start(out=out, in_=in_, **kw)
        return self.op(eng, fn, reads, writes, dma=True)

    def finalize(self, stack):
        nc = self.nc
        ops = self.ops
        last_w = {}
        readers = {}

        def is_ps(k):
            return isinstance(k, tuple) and k[0] == "ps"
        for o in ops:
            deps = {}
            for k in o.reads:
                w = last_w.get(k)
                if w is not None:
                    deps[w.idx] = w
                if is_ps(k):
                    for r in readers.get(k, ()):
                        if r.eng != o.eng:
                            deps[r.idx] = r
            for k in o.writes:
                w = last_w.get(k)
                if w is not None:
                    deps[w.idx] = w
                for r in readers.get(k, ()):
                    deps[r.idx] = r
            deps.pop(o.idx, None)
            best = {}
            keep = []
            for d in deps.values():
                if d.dma:
                    keep.append(d)
                else:
                    b = best.get(d.eng)
                    if b is None or d.idx > b.idx:
                        best[d.eng] = d
            for e, d in best.items():
                if e == o.eng and not o.dma and e == "pe":
                    continue
                keep.append(d)
            o.deps = keep
            for d in keep:
                d.inc = True
            for k in o.reads:
                readers.setdefault(k, []).append(o)
            for k in o.writes:
                last_w[k] = o
                readers[k] = []
        self.eng_sem = {e: stack.enter_context(nc.semaphore("s_" + e)) for e in self.ENGS}
        self.pool_sems = {}
        cnt = {e: 0 for e in self.ENGS}
        dma_i = {e: 0 for e in self.ENGS}
        dma_hist = {e: [] for e in self.ENGS}
        pool_cnt = {}
        for o in ops:
            if o.dma:
                i = dma_i[o.eng]
                dma_i[o.eng] += 1
                key = (o.eng, i % self.dma_pool)
                if key not in self.pool_sems:
                    self.pool_sems[key] = stack.enter_context(nc.semaphore("d_%s_%d" % key))
                    pool_cnt[key] = 0
                pool_cnt[key] += 1
                o.tok = (key, 16 * pool_cnt[key])
                o.inc = True
                hist = dma_hist[o.eng]
                if len(hist) >= self.dma_pool:
                    o.deps = list(o.deps) + [hist[-self.dma_pool]]
                hist.append(o)
            elif o.inc:
                cnt[o.eng] += 1
                o.tok = (o.eng, cnt[o.eng])
        per_eng = {e: [o for o in ops if o.eng == e] for e in self.ENGS}

        def sem_of(key):
            return self.eng_sem[key] if isinstance(key, str) else self.pool_sems[key]

        def emit(ename, e):
            known = {}
            for o in per_eng[ename]:
                for d in sorted(o.deps, key=lambda d: d.idx):
                    key, val = d.tok
                    if known.get(key, 0) >= val:
                        continue
                    e.wait_ge(sem_of(key), val)
                    known[key] = val
                inst = o.fn(e)
                if o.inc:
                    key, val = o.tok
                    inst.then_inc(sem_of(key), 16 if o.dma else 1)
            if ename == "sp":
                for key, c in pool_cnt.items():
                    if known.get(key, 0) < 16 * c:
                        e.wait_ge(self.pool_sems[key], 16 * c)

        with nc.Block() as block:
            @block.tensor
            def _(e):
                emit("pe", e)

            @block.scalar
            def _(e):
                emit("act", e)

            @block.vector
            def _(e):
                emit("dve", e)

            @block.gpsimd
            def _(e):
                emit("pool", e)

            @block.sync
            def _(e):
                emit("sp", e)
        self.stats = {e: len(per_eng[e]) for e in self.ENGS}


class Rot:
    def __init__(self, alloc, name, shape, dt, n):
        self.t = [alloc("%s%d" % (name, i), shape, dt) for i in range(n)]
        self.name = name
        self.i = 0

    def get(self):
        k = self.i % len(self.t)
        self.i += 1
        return self.t[k], (self.name, k)


def K(name, a, b, *idx):
    return [(name,) + tuple(idx) + (z,) for z in range(a // 128, (b - 1) // 128 + 1)]


def split_even(a, b, maxn):
    tot = b - a
    nt = -(-tot // maxn)
    base = tot // nt
    rem = tot % nt
    out = []
    for i in range(nt):
        n = base + (1 if i < rem else 0)
        out.append((a, a + n))
        a += n
    return out


def split_tiles(a, b, step=512):
    out = []
    while a < b:
        n = min(step, b - a)
        out.append((a, a + n))
        a += n
    return out


def debug_srcs(L):
    out = []
    out.append((L["xnT"][:, 0, 0:768], K("xnT", 0, 768)))
    out.append((L["hT"][:, 0, 0:514], [k for kt in range(8) for k in K("hT", 0, 514, kt)]))
    out.append((L["qT"][:, 0, 0:768], K("qT", 0, 768, 0)))
    out.append((L["kT"][:, 0:768], K("kT", 0, 768)))
    out.append((L["gluT"][:, 0, 0:768], K("gluT", 0, 768, 0)))
    out.append((L["mixT"][:, 0, 0:514], K("mix", 0, 514, 0)))
    out.append((L["mixT"][:, 4, 0:514], K("mix", 0, 514, 4)))
    out.append((L["cosT"][:, 0:768], ["cosT"]))
    if L.get("with_samples"):
        out = out[:6]
        for kt in (0, 4):
            out.append((L["mixT"][:, kt, 514:530], K("mix", 514, 530, kt)))
    return out


def build_nc(with_samples=True, debug=False, stop=99, npass_run=99):
    nc = bass.Bass("TRN2", target_bir_lowering=False)

    def din(name, shape):
        return nc.dram_tensor(name, list(shape), F32, kind="ExternalInput").ap()

    def dout(name, shape):
        return nc.dram_tensor(name, list(shape), F32, kind="ExternalOutput").ap()

    xh = din("xh", (256 + 2048, D_MODEL))
    pp = din("pp", (2048, PLE))
    w_in = din("w_in", (D_MODEL, 18 * 128))
    w_v = din("w_v", (D_MODEL, 128))
    w_out = din("w_out", (D_MODEL, D_MODEL))
    w_ffi = din("w_ffi", (D_MODEL, 2 * D_FF))
    w_ffo = din("w_ffo", (D_FF, D_MODEL))
    w_ple = din("w_ple", (D_MODEL + PLE, D_MODEL))
    colsd = din("cols", (128, NCOL))
    sinkrow = din("sinkrow", (1, 512))
    mask0 = din("mask0", (128, 128))
    y = dout("y", (2048, D_MODEL))
    nk = dout("nk", (128, 128))
    nv = dout("nv", (128, 128))
    nconv = dout("nconv", (30, CONV_CH))
    nffn = dout("nffn", (2, D_FF))
    if with_samples:
        xs_d = din("xs", (NS, D_MODEL))
        ps_d = din("ps", (NS, PLE))
        ck_d = din("ck", (NS, 128, 128))
        cv_d = din("cv", (NS, 128, 128))
        sc_d = din("sc", (NS, 30, CONV_CH))
        sf_d = din("sf", (NS, 2, D_FF))
        cwt_d = din("cwt", (CONV_K, CONV_CH))
        cbt_d = din("cbt", (4, CONV_CH))
        ys = dout("ys", (NS, D_MODEL))
        nks = dout("nks", (NS, 128, 128))
        nvs = dout("nvs", (NS, 128, 128))
        nconvs = dout("nconvs", (NS, 30, CONV_CH))
        nffns = dout("nffns", (NS, 2, D_FF))

    MAXM = max(PASS_BLOCKS) * 128
    NSX = NS if with_samples else 0
    ZW = 256 + MAXM + NSX
    MW = 2 + MAXM + NSX
    NPASS = len(PASS_BLOCKS)

    with ExitStack() as st:
        def sb(name, shape, dt=F32):
            return st.enter_context(nc.sbuf_tensor("sb_" + name, list(shape), dt))

        P = Prog(nc)
        banks = [st.enter_context(nc.psum_tensor("ps%d" % i, [128, 512], F32)) for i in range(8)]
        banks_bf = [b.bitcast(BF16) for b in banks]
        ps_i = [0]

        held = set()
        held_by = {}

        def PS(hold=False):
            assert len(held) < 8, "all PSUM banks held %s" % (held_by,)
            while ps_i[0] % 8 in held:
                ps_i[0] += 1
            k = ps_i[0] % 8
            ps_i[0] += 1
            if hold:
                held.add(k)
                held_by[k] = True
            return banks[k], banks_bf[k], ("ps", k)

        def PSrel(*keys):
            for kk in keys:
                held.discard(kk[1])

        cols = sb("cols", (128, NCOL))
        cols2 = sb("cols2", (128, 8))
        ident_f = sb("ident_f", (128, 128))
        ident_b = sb("ident_b", (128, 128), BF16)
        ones_b = sb("ones_b", (128, 128), BF16)
        bd_b = sb("bd_b", (128, 128), BF16)
        onesm_f = sb("onesm_f", (128, 128))
        onesr_f = sb("onesr_f", (1, 128))
        sinkexp = sb("sinkexp", (1, 512))
        mk_cur = sb("mk_cur", (128, 128))
        mk_prev = sb("mk_prev", (128, 128))
        mk0 = sb("mk0", (128, 128))
        maskall = sb("maskall", (128, 512), BF16)
        maskall0 = sb("maskall0", (128, 512), BF16)
        iot_i = sb("iot_i", (128, 128), I32)
        iot_f = sb("iot_f", (128, 128))
        col_i = sb("col_i", (128, 2), I32)
        tmpc = sb("tmpc", (128, 128))

        P.dma("sp", cols[:], colsd, writes=["cols"])
        P.dma("sp", sinkexp[:], sinkrow, writes=["sinkexp"])
        P.dma("sp", mk0[:], mask0, writes=["mk0"])
        P.pool(lambda e: e.iota(iot_i[:], [[1, 128]], base=0, channel_multiplier=0), writes=["iot_i"])
        P.pool(lambda e: e.iota(col_i[:, 0:1], [[0, 1]], base=0, channel_multiplier=1), writes=["col_i"])
        P.dve(lambda e: e.tensor_copy(iot_f[:], iot_i[:]), reads=["iot_i"], writes=["iot_f"])
        P.dve(lambda e: e.tensor_copy(cols2[:, 3:4], col_i[:, 0:1]), reads=["col_i"], writes=["c2_3"])
        P.dve(lambda e: e.tensor_scalar(col_i[:, 1:2], col_i[:, 0:1], 31, None, op0=ALU.bitwise_and),
              reads=["col_i"], writes=["col_i1"])
        P.dve(lambda e: e.tensor_copy(cols2[:, 4:5], col_i[:, 1:2]), reads=["col_i1"], writes=["c2_4"])
        P.dve(lambda e: e.tensor_copy(cols2[:, 2:3], cols[:, C_INVF:C_INVF + 1]), reads=["cols"], writes=["c2_2"])
        P.dve(lambda e: e.tensor_tensor(cols2[:, 0:1], cols[:, C_GQR:C_GQR + 1], cols[:, C_SGN:C_SGN + 1], ALU.mult),
              reads=["cols"], writes=["c2_0"])
        P.dve(lambda e: e.tensor_tensor(cols2[:, 1:2], cols[:, C_GKR:C_GKR + 1], cols[:, C_SGN:C_SGN + 1], ALU.mult),
              reads=["cols"], writes=["c2_1"])
        P.dve(lambda e: e.tensor_scalar(ident_f[:], iot_f[:], cols2[:, 3:4], None, op0=ALU.is_equal),
              reads=["iot_f", "c2_3"], writes=["ident_f"])
        P.dve(lambda e: e.tensor_copy(ident_b[:], ident_f[:]), reads=["ident_f"], writes=["ident"])
        P.dve(lambda e: e.tensor_scalar(mk_cur[:], iot_f[:], cols2[:, 3:4], None, op0=ALU.is_ge),
              reads=["iot_f", "c2_3"], writes=["mk_cur"])
        P.dve(lambda e: e.tensor_scalar(mk_prev[:], iot_f[:], cols2[:, 3:4], None, op0=ALU.is_le),
              reads=["iot_f", "c2_3"], writes=["mk_prev"])
        for q4 in range(4):
            src = mk_cur if q4 % 2 == 0 else mk_prev
            src0 = mk_cur if q4 % 2 == 0 else mk0
            P.dve(lambda e, q4=q4, src=src: e.tensor_copy(maskall[:, q4 * 128:(q4 + 1) * 128], src[:]),
                  reads=["mk_cur", "mk_prev", "maskall"], writes=["maskall"])
            P.dve(lambda e, q4=q4, src0=src0: e.tensor_copy(maskall0[:, q4 * 128:(q4 + 1) * 128], src0[:]),
                  reads=["mk_cur", "mk0", "maskall0"], writes=["maskall0"])
        P.dve(lambda e: e.tensor_scalar(tmpc[:], iot_f[:], 64.0, None, op0=ALU.is_ge), reads=["iot_f"], writes=["tmpc"])
        P.dve(lambda e: e.tensor_scalar(cols2[:, 5:6], cols2[:, 3:4], 64.0, None, op0=ALU.is_ge), reads=["c2_3"], writes=["c2_5"])
        P.dve(lambda e: e.tensor_scalar(bd_b[:], tmpc[:], cols2[:, 5:6], None, op0=ALU.is_equal),
              reads=["tmpc", "c2_5"], writes=["bd_b"])
        if with_samples:
            P.dve(lambda e: e.tensor_scalar(bd_f[:], tmpc[:], cols2[:, 5:6], None, op0=ALU.is_equal),
                  reads=["tmpc", "c2_5"], writes=["bd_f"])
        P.pool(lambda e: e.memset(ones_b[:], 1.0), writes=["ones_b"])
        P.pool(lambda e: e.memset(onesm_f[:], 1.0 / CONV_CH), writes=["onesm_f"])
        P.pool(lambda e: e.memset(onesr_f[:], 1.0), writes=["onesr_f"])
        P.act(lambda e: e.activation(sinkexp[:], sinkexp[:], AF.Exp), reads=["sinkexp"], writes=["sinkexp"])
        sinkc = sb("sinkc", (128, 4))
        P.act(lambda e: e.activation(sinkc[:], cols[:, C_SINKC:C_SINKC + 4], AF.Exp), reads=["cols"], writes=["sinkc"])

        xnT = sb("xnT", (128, 8, ZW), BF16)
        qT = sb("qT", (128, 4, ZW), BF16)
        kT = sb("kT", (128, ZW), BF16)
        vS = sb("vS", (128, 2 + MAXM // 128, 128), BF16)
        gluT = sb("gluT", (128, 4, ZW), BF16)
        cosT = sb("cosT", (128, ZW))
        sinT = sb("sinT", (128, ZW))
        rt_i = sb("rt_i", (128, 512), I32)
        mixT = sb("mixT", (128, 8, MW), BF16)
        hT = sb("hT", (128, 8, MW))
        hnT = sb("hnT", (128, 8, MW), BF16)
        actT = sb("actT", (128, NJ, MW), BF16)
        pT = sb("pT", (128, 2, MW), BF16)
        cbuf = sb("cbuf", (128, 4, 512))
        ring = sb("ring", (128, NSLOT, SLOT), BF16)
        kf32 = sb("kf32", (128, 128))
        gl32 = sb("gl32", (128, 4, 32))
        fg32 = sb("fg32", (128, NJ, 2))
        hcarry = sb("hcarry", (128, 8, 2), BF16)
        cst = sb("cst", (32, 512))

        xtR = Rot(sb, "xt", (128, 1024), F32, 2)
        xbR = Rot(sb, "xb", (128, 1024), BF16, 2)
        ptR = Rot(sb, "ptk", (128, PLE), BF16, 2)
        ytR = Rot(sb, "ytk", (128, 1024), F32, 2)
        c1R = Rot(sb, "c1_", (128, 4), F32, 4)
        tAR = Rot(sb, "tA", (128, 512), F32, 3)
        tCR = Rot(sb, "tC", (128, 512), F32, 2)
        bfR = Rot(sb, "bfT", (128, 512), BF16, 4)
        dgR = Rot(sb, "dg", (128, 128), BF16, 8)
        outR = Rot(sb, "ot", (128, 128), F32, 2)
        sqR = Rot(sb, "sqr", (128, 512), BF16, 3)
        if with_samples:
            ks32 = sb("ks32", (128, NS))
            gs32 = sb("gs32", (128, 4, NS))
            vs_tok = sb("vs_tok", (NS, 128))
            vTs = sb("vTs", (128, NS))
            ckb = sb("ckb", (128, NS, 128), BF16)
            cvb = sb("cvb", (128, NS, 128), BF16)
            kTbR = Rot(sb, "kTb", (128, 128), BF16, 3)
            wrep = sb("wrep", (124, 512))
            selT = sb("selT", (124, 4, NS))
            bd_f = sb("bd_f", (128, 128))
            stT = sb("stT", (128, NJ, 2 * NS))
            fgs32 = sb("fgs32", (128, NJ, NS))
            crows = sb("crows", (65, 512))
            onesf16 = sb("onesf16", (65, NS))
            sm = {k_: sb("sm_" + k_, (128, 64)) for k_ in ("prod", "enew", "dtot", "ev", "otot")}
            EAB = [sb("EAB%d" % i, (128, 64), BF16) for i in range(2)]
            gs_tok = sb("gs_tok", (NS, 512))
            cs_col = sb("cs_col", (NS, 8))
            cs_col32 = sb("cs_col32", (128, 8))

        def col(i):
            return cols[:, i:i + 1]

        w_in_v = w_in.rearrange("(kt p) c -> p kt c", p=128)
        w_v_v = w_v.rearrange("(kt p) c -> p kt c", p=128)
        w_out_v = w_out.rearrange("(kt p) c -> p kt c", p=128)
        w_ffi_v = w_ffi.rearrange("(kt p) c -> p kt c", p=128)
        w_ffo_v = w_ffo.rearrange("(kt p) c -> p kt c", p=128)
        w_ple_v = w_ple.rearrange("(kt p) c -> p kt c", p=128)
        seq = []
        for pi in range(NPASS):
            for i in range(9):
                seq.append((w_in_v[:, :, i * 256:(i + 1) * 256], 8, 256))
            seq.append((w_v_v, 8, 128))
            for i in range(4):
                seq.append((w_out_v[:, :, i * 256:(i + 1) * 256], 8, 256))
            for j in range(NJ):
                seq.append((w_ffi_v[:, :, j * 256:(j + 1) * 256], 8, 256))
            for oc in range(8):
                seq.append((w_ffo_v[:, :, oc * 128:(oc + 1) * 128], NJ, 128))
            for i in range(4):
                seq.append((w_ple_v[:, :, i * 256:(i + 1) * 256], 10, 256))
        ws = {"issued": 0, "used": 0}

        def ws_issue_upto(n):
            while ws["issued"] < min(n, len(seq)):
                i = ws["issued"]
                src, kt, c = seq[i]
                slot = i % NSLOT
                dst = ring[:, slot, 0:kt * c].rearrange("p (a b) -> p a b", a=kt)
                P.dma("pool", dst, src, writes=[("w", slot)])
                ws["issued"] += 1

        def ws_next():
            i = ws["used"]
            ws["used"] += 1
            ws_issue_upto(i + NSLOT)
            src, kt, c = seq[i]
            slot = i % NSLOT
            return ring[:, slot, 0:kt * c].rearrange("p (a b) -> p a b", a=kt), ("w", slot)

        ws_issue_upto(NSLOT - 1)

        sinkb = sb("sinkb", (1, 512), BF16)
        P.act(lambda e: e.activation(sinkb[:], sinkexp[:], AF.Copy), reads=["sinkexp"], writes=["sinkb"])

        def rope_tables(pi, s0, M, last):
            W = 256 + M
            chunks = [(a, b, False) for (a, b) in split_tiles(0, W)]
            if last and with_samples:
                chunks.append((W, W + NS, True))
            for (a, b, samp) in chunks:
                n = b - a
                af, kaf = tAR.get()
                kfl, kkf = tCR.get()
                if samp:
                    P.dve(lambda e, af=af, n=n: e.tensor_scalar(af[:, 0:n], cols2[:, 2:3].to_broadcast([128, n]) if False else iot_f[:, 0:n], 0.0, float(PAST_LEN),
                                                                op0=ALU.mult, op1=ALU.add),
                          reads=["iot_f"], writes=[kaf])
                    P.dve(lambda e, af=af, n=n: e.tensor_scalar(af[:, 0:n], af[:, 0:n], cols2[:, 2:3], None, op0=ALU.mult),
                          reads=[kaf, "c2_2"], writes=[kaf])
                else:
                    P.pool(lambda e, a=a, n=n: e.iota(rt_i[:, 0:n], [[1, n]], base=s0 - 256 + a, channel_multiplier=0),
                           reads=["rt_i"], writes=["rt_i"])
                    P.dve(lambda e, af=af, n=n: e.tensor_copy(af[:, 0:n], rt_i[:, 0:n]), reads=["rt_i"], writes=[kaf])
                    P.dve(lambda e, af=af, n=n: e.tensor_scalar(af[:, 0:n], af[:, 0:n], col(C_POSB), cols2[:, 2:3],
                                                                op0=ALU.add, op1=ALU.mult),
                          reads=[kaf, "cols", "c2_2"], writes=[kaf])
                for tab, shift, key in ((sinT, 0.0, "sinT"), (cosT, math.pi / 2, "cosT")):
                    P.dve(lambda e, tab=tab, shift=shift, af=af, a=a, n=n: e.tensor_scalar(tab[:, a:a + n], af[:, 0:n], shift, None, op0=ALU.add),
                          reads=[kaf, key], writes=[key])
                    P.dve(lambda e, tab=tab, a=a, n=n: e.tensor_scalar(rt_i[:, 0:n], tab[:, a:a + n], 1.0 / TWO_PI, None, op0=ALU.mult),
                          reads=[key, "rt_i"], writes=["rt_i"])
                    P.dve(lambda e, kfl=kfl, n=n: e.tensor_copy(kfl[:, 0:n], rt_i[:, 0:n]), reads=["rt_i", kkf], writes=[kkf])
                    P.dve(lambda e, tab=tab, kfl=kfl, a=a, n=n: e.scalar_tensor_tensor(tab[:, a:a + n], kfl[:, 0:n], -CW1, tab[:, a:a + n],
                                                                                      op0=ALU.mult, op1=ALU.add),
                          reads=[kkf, key], writes=[key])
                    P.dve(lambda e, tab=tab, kfl=kfl, a=a, n=n: e.scalar_tensor_tensor(tab[:, a:a + n], kfl[:, 0:n], -CW2, tab[:, a:a + n],
                                                                                      op0=ALU.mult, op1=ALU.add),
                          reads=[kkf, key], writes=[key])
                    P.dve(lambda e, tab=tab, a=a, n=n: e.tensor_scalar(tab[:, a:a + n], tab[:, a:a + n], math.pi, -math.pi,
                                                                       op0=ALU.min, op1=ALU.max),
                          reads=[key], writes=[key])
                    P.act(lambda e, tab=tab, a=a, n=n: e.activation(tab[:, a:a + n], tab[:, a:a + n], AF.Sin), reads=[key], writes=[key])

        def hkeys(name, a, b, kts=range(8)):
            return [k for kt in kts for k in K(name, a, b, kt)]

        def stage0a_front(src_rows, n):
            xt, kx = xtR.get()
            P.dma("sp", xt[0:n, :], src_rows, writes=[kx])
            c1, kc = c1R.get()
            xb, kxb = xbR.get()
            P.act(lambda e: e.activation(xb[0:n, :], xt[0:n, :], AF.Square, accum_out=c1[0:n, 0:1]),
                  reads=[kx], writes=[kc, kxb])
            P.act(lambda e: e.activation(c1[0:n, 1:2], c1[0:n, 0:1], AF.Ln, scale=1.0 / D_MODEL, bias=EPS),
                  reads=[kc], writes=[kc])
            P.act(lambda e: e.activation(c1[0:n, 2:3], c1[0:n, 1:2], AF.Exp, scale=-0.5), reads=[kc], writes=[kc])
            P.dve(lambda e: e.tensor_scalar(xb[0:n, :], xt[0:n, :], c1[0:n, 2:3], None, op0=ALU.mult),
                  reads=[kx, kc, kxb], writes=[kxb])
            return xb, kxb

        def stage0a_back(xb, kxb, n, zc):
            ps, psb, kp = PS()

            def tr(e):
                r = None
                for kt in range(8):
                    r = e.transpose(psb[:, kt * 128:kt * 128 + n], xb[0:n, kt * 128:(kt + 1) * 128], ident_b[0:n, 0:n])
                return r
            P.pe(tr, reads=[kxb, "ident"], writes=[kp])
            for kt in range(8):
                P.act(lambda e, kt=kt: e.activation(xnT[:, kt, zc:zc + n], psb[:, kt * 128:kt * 128 + n], AF.Identity,
                                                    scale=col(C_GMIX + kt)),
                      reads=[kp, "cols"], writes=K("xnT", zc, zc + n))

        def stage0a(src_rows, n, zc):
            xb, kxb = stage0a_front(src_rows, n)
            stage0a_back(xb, kxb, n, zc)

        def stage0b(src_rows, n, mc, halo2=False):
            xt, kx = xtR.get()
            P.dma("sp", xt[0:n, :], src_rows, writes=[kx])
            for half in range(2):
                ps2, _, kp2 = PS()

                def trf(e, half=half, ps2=ps2):
                    r = None
                    for a4 in range(4):
                        kt = half * 4 + a4
                        r = e.transpose(ps2[:, a4 * 128:a4 * 128 + n], xt[0:n, kt * 128:(kt + 1) * 128], ident_f[0:n, 0:n])
                    return r
                P.pe(trf, reads=[kx, "ident_f"], writes=[kp2])
                v3 = ps2[:, 0:512].rearrange("p (a b) -> p a b", a=4)
                kts = range(half * 4, half * 4 + 4)
                if halo2:
                    P.dve(lambda e, half=half, v3=v3: e.tensor_copy(hT[:, half * 4:half * 4 + 4, 0:2], v3[:, :, 126:128]),
                          reads=[kp2], writes=hkeys("hT", 0, 2, kts))
                else:
                    P.dve(lambda e, half=half, v3=v3: e.tensor_copy(hT[:, half * 4:half * 4 + 4, mc:mc + n], v3[:, :, 0:n]),
                          reads=[kp2], writes=hkeys("hT", mc, mc + n, kts))

        import os
        RE = int(os.environ.get("MK_RE", "99"))

        def rope_epilogue(psq, psr, kq, kr, n, zc, gcol, grscol, dst, dkeys, f32dst=None):
            if RE < 1:
                return
            sq, ksq = bfR.get()
            P.act(lambda e: e.activation(sq[:, 0:n], psq[:, 0:n], AF.Square), reads=[kq], writes=[ksq])
            pss, _, kss = PS()
            P.pe(lambda e: e.matmul(pss[:, 0:n], bd_b[:], sq[:, 0:n], start=True, stop=True),
                 reads=[ksq, "bd_b"], writes=[kss])
            if RE < 2:
                return
            tl, ktl = tCR.get()
            P.act(lambda e: e.activation(tl[:, 0:n], pss[:, 0:n], AF.Ln, scale=1.0 / HD, bias=EPS), reads=[kss], writes=[ktl])
            if RE < 3:
                return
            P.act(lambda e: e.activation(pss[:, 0:n], tl[:, 0:n], AF.Exp, scale=-0.5), reads=[ktl, kss], writes=[kss])
            if RE < 4:
                return
            t1, k1 = tAR.get()
            P.dve(lambda e: e.scalar_tensor_tensor(t1[:, 0:n], psq[:, 0:n], gcol, cosT[:, zc:zc + n], op0=ALU.mult, op1=ALU.mult),
                  reads=[kq, "cosT", "cols"], writes=[k1])
            if RE < 5:
                return
            P.dve(lambda e: e.scalar_tensor_tensor(psr[:, 0:n], psr[:, 0:n], grscol, sinT[:, zc:zc + n], op0=ALU.mult, op1=ALU.mult),
                  reads=[kr, "sinT", "c2_0", "c2_1"], writes=[kr])
            if RE < 6:
                return
            P.dve(lambda e: e.tensor_tensor(t1[:, 0:n], t1[:, 0:n], psr[:, 0:n], ALU.add), reads=[k1, kr], writes=[k1])
            if RE < 7:
                return
            P.dve(lambda e: e.tensor_tensor(dst[:, zc:zc + n], t1[:, 0:n], pss[:, 0:n], ALU.mult),
                  reads=[k1, kss], writes=dkeys)
            if f32dst is not None:
                fd, lo, hi, fk = f32dst
                P.dve(lambda e: e.tensor_tensor(fd, t1[:, lo:hi], pss[:, lo:hi], ALU.mult),
                      reads=[k1, kss], writes=[fk])

        def mm2(slab, src, a, n, ps0, ps1):
            def f(e):
                r = None
                for kt in range(8):
                    r = e.matmul(ps0[:, 0:n], slab[:, kt, 0:128], src[:, kt, a:a + n], start=(kt == 0), stop=(kt == 7))
                for kt in range(8):
                    r = e.matmul(ps1[:, 0:n], slab[:, kt, 128:256], src[:, kt, a:a + n], start=(kt == 0), stop=(kt == 7))
                return r
            return f

        ZP = int(os.environ.get("MK_ZP", "99"))

        def z_stage(pi, ztiles, final, stile=None, fillers=()):
            fillers = list(fillers)
            tl_ = [(a, b, False) for (a, b) in ztiles] + ([(stile[0], stile[1], True)] if stile else [])
            nmain = len(ztiles)
            for c in range(4):
                slab, kw = ws_next()
                for (a, b, samp) in tl_:
                    n = b - a
                    psq, _, kq = PS()
                    psr, _, kr = PS()
                    P.pe(mm2(slab, xnT, a, n, psq, psr), reads=[kw] + K("xnT", a, b), writes=[kq, kr])
                    rope_epilogue(psq, psr, kq, kr, n, a, col(C_GQ), cols2[:, 0:1], qT[:, c, :], K("qT", a, b, c))
                if fillers:
                    fillers.pop(0)()
            slab, kw = ws_next()
            for ti, (a, b, samp) in enumerate(tl_):
                n = b - a
                psq, _, kq = PS()
                psr, _, kr = PS()
                P.pe(mm2(slab, xnT, a, n, psq, psr), reads=[kw] + K("xnT", a, b), writes=[kq, kr])
                f32d = None
                if samp:
                    f32d = (ks32[:, 0:NS], 0, NS, "ks32")
                elif final and ti == nmain - 1:
                    f32d = (kf32[:, 0:128], n - 128, n, "kf32")
                rope_epilogue(psq, psr, kq, kr, n, a, col(C_GK), cols2[:, 1:2], kT, K("kT", a, b), f32dst=f32d)
            for c in range(4):
                slab, kw = ws_next()
                for ti, (a, b, samp) in enumerate(tl_):
                    n = b - a
                    psa, _, ka = PS()
                    psg, _, kg = PS()
                    P.pe(mm2(slab, xnT, a, n, psa, psg), reads=[kw] + K("xnT", a, b), writes=[ka, kg])
                    sg, ksg = tAR.get()
                    P.act(lambda e, sg=sg, psg=psg, n=n: e.activation(sg[:, 0:n], psg[:, 0:n], AF.Sigmoid), reads=[kg], writes=[ksg])
                    P.dve(lambda e, sg=sg, psa=psa, n=n, a=a, c=c: e.tensor_tensor(gluT[:, c, a:a + n], psa[:, 0:n], sg[:, 0:n], ALU.mult),
                          reads=[ka, ksg], writes=K("gluT", a, b, c))
                    if samp:
                        P.dve(lambda e, sg=sg, psa=psa, n=n, c=c: e.tensor_tensor(gs32[:, c, 0:NS], psa[:, 0:n], sg[:, 0:n], ALU.mult),
                              reads=[ka, ksg], writes=[("gs32", c)])
                    elif final and ti == nmain - 1:
                        P.dve(lambda e, sg=sg, psa=psa, n=n, c=c: e.tensor_tensor(gl32[:, c, 0:32], psa[:, n - 32:n], sg[:, n - 32:n], ALU.mult),
                              reads=[ka, ksg], writes=[("gl32", c)])
                if fillers:
                    fillers.pop(0)()
            while fillers:
                fillers.pop(0)()
            slab, kw = ws_next()
            for ti, (a, b, samp) in enumerate(tl_):
                psv, _, kv = PS()
                if samp:
                    def mmvs(e, slab=slab, a=a, psv=psv):
                        r = None
                        for kt in range(8):
                            r = e.matmul(psv[0:NS, 0:128], xnT[:, kt, a:a + NS], slab[:, kt, 0:128], start=(kt == 0), stop=(kt == 7))
                        for kt in range(8):
                            r = e.matmul(psv[:, 128:128 + NS], slab[:, kt, 0:128], xnT[:, kt, a:a + NS], start=(kt == 0), stop=(kt == 7))
                        return r
                    P.pe(mmvs, reads=[kw] + K("xnT", a, b), writes=[kv])
                    P.act(lambda e, psv=psv: e.activation(vs_tok[:], psv[0:NS, 0:128], AF.Copy), reads=[kv], writes=["vs_tok"])
                    P.act(lambda e, psv=psv: e.activation(vTs[:], psv[:, 128:128 + NS], AF.Copy), reads=[kv], writes=["vTs"])
                    continue
                nblk = (b - a) // 128

                def mmv(e, slab=slab, a=a, nblk=nblk, psv=psv):
                    r = None
                    for bl in range(nblk):
                        for kt in range(8):
                            r = e.matmul(psv[:, bl * 128:(bl + 1) * 128], xnT[:, kt, a + bl * 128:a + (bl + 1) * 128],
                                         slab[:, kt, 0:128], start=(kt == 0), stop=(kt == 7))
                    return r
                P.pe(mmv, reads=[kw] + K("xnT", a, b), writes=[kv])
                P.act(lambda e, a=a, nblk=nblk, psv=psv: e.activation(
                    vS[:, a // 128:a // 128 + nblk, :], psv[:, 0:nblk * 128].rearrange("p (a b) -> p a b", a=nblk), AF.Copy),
                    reads=[kv], writes=K("vS", a, b))
                if final and ti == nmain - 1:
                    ot, ko = outR.get()
                    P.dve(lambda e, ot=ot, psv=psv, nblk=nblk: e.tensor_copy(ot[:], psv[:, (nblk - 1) * 128:nblk * 128]),
                          reads=[kv], writes=[ko])
                    P.dma("sp", nv, ot[:], reads=[ko])

        def attn_pass(blocks, nob=2):
            bk = [PS(hold=True) for _ in range(4 + 2 * nob)]
            sbank = [(bk[0], bk[1]), (bk[2], bk[3])]
            obank = [(bk[4 + 2 * i], bk[5 + 2 * i]) for i in range(nob)]
            units = [(bi, cp) for bi in range(len(blocks)) for cp in range(2)]
            state = {}

            def rkeys(zb):
                zc = zb * 128
                zp = zc - 128
                return [k for c in range(4) for k in K("qT", zc, zc + 128, c)] + K("kT", zp, zc + 128) + K("vS", zp, zc + 128)

            def S(ui):
                bi, cp = units[ui]
                zb, mc_dst, halo, first_main = blocks[bi]
                zc = zb * 128
                zp = zc - 128
                c0, c1 = 2 * cp, 2 * cp + 1
                (psA, _, kA), (psB, _, kB) = sbank[ui % 2]
                mk = maskall0 if first_main else maskall

                def sc(e):
                    r = None
                    for i, c in enumerate((c0, c1)):
                        o = i * 256
                        e.matmul(psA[:, o:o + 128], kT[0:64, zc:zc + 128], qT[0:64, c, zc:zc + 128], start=True, stop=True)
                        e.matmul(psB[:, o:o + 128], kT[64:128, zc:zc + 128], qT[64:128, c, zc:zc + 128], start=True, stop=True)
                        e.matmul(psA[:, o + 128:o + 256], kT[0:64, zp:zp + 128], qT[0:64, c, zc:zc + 128], start=True, stop=True)
                        r = e.matmul(psB[:, o + 128:o + 256], kT[64:128, zp:zp + 128], qT[64:128, c, zc:zc + 128], start=True, stop=True)
                    return r
                P.pe(sc, reads=rkeys(zb), writes=[kA, kB])
                ia, ib = 2 * (ui % 2), 2 * (ui % 2) + 1
                eA, keA = bfR.t[ia], ("bfT", ia)
                eB, keB = bfR.t[ib], ("bfT", ib)
                P.act(lambda e: e.activation(eA[:], psA[:], AF.Exp, scale=HD ** -0.5), reads=[kA], writes=[keA])
                P.act(lambda e: e.activation(eB[:], psB[:], AF.Exp, scale=HD ** -0.5), reads=[kB], writes=[keB])
                P.dve(lambda e: e.tensor_tensor(eA[:], eA[:], mk[:], ALU.mult), reads=[keA, "maskall", "maskall0"], writes=[keA])
                P.dve(lambda e: e.tensor_tensor(eB[:], eB[:], mk[:], ALU.mult), reads=[keB, "maskall", "maskall0"], writes=[keB])
                state[ui] = (eA, keA, eB, keB)

            def V(ui):
                bi, cp = units[ui]
                zb, mc_dst, halo, first_main = blocks[bi]
                c0, c1 = 2 * cp, 2 * cp + 1
                (pso, _, ko), (psd, _, kd) = obank[bi % nob]
                eA, keA, eB, keB = state.pop(ui)

                def pv(e):
                    r = None
                    for i, c in enumerate((c0, c1)):
                        o = i * 256
                        o0 = pso[0:64, c * 128:(c + 1) * 128]
                        o1 = pso[64:128, c * 128:(c + 1) * 128]
                        d0 = psd[0:64, c * 128:(c + 1) * 128]
                        d1 = psd[64:128, c * 128:(c + 1) * 128]
                        e.matmul(o0, vS[:, zb, 0:64], eA[:, o:o + 128], start=True, stop=False)
                        e.matmul(o0, vS[:, zb - 1, 0:64], eA[:, o + 128:o + 256], start=False, stop=True)
                        e.matmul(o1, vS[:, zb, 64:128], eB[:, o:o + 128], start=True, stop=False)
                        e.matmul(o1, vS[:, zb - 1, 64:128], eB[:, o + 128:o + 256], start=False, stop=True)
                        e.matmul(d0, ones_b[:, 0:64], eA[:, o:o + 128], start=True, stop=False)
                        e.matmul(d0, ones_b[:, 0:64], eA[:, o + 128:o + 256], start=False, stop=True)
                        e.matmul(d1, ones_b[:, 64:128], eB[:, o:o + 128], start=True, stop=False)
                        r = e.matmul(d1, ones_b[:, 64:128], eB[:, o + 128:o + 256], start=False, stop=True)
                    return r
                P.pe(pv, reads=[keA, keB, "ones_b"] + rkeys(zb), writes=[ko, kd])
                if cp == 1:
                    tl, ktl = tCR.get()
                    for c in range(4):
                        P.act(lambda e, c=c: e.activation(tl[:, c * 128:(c + 1) * 128], psd[:, c * 128:(c + 1) * 128], AF.Ln, bias=sinkc[:, c:c + 1]),
                              reads=[kd, "sinkc", ktl], writes=[ktl])
                    P.act(lambda e: e.activation(tl[:], tl[:], AF.Exp, scale=-1.0), reads=[ktl], writes=[ktl])
                    o3 = pso[:, 0:512].rearrange("p (a b) -> p a b", a=4)
                    r3 = tl[:, 0:512].rearrange("p (a b) -> p a b", a=4)
                    if not halo:
                        P.dve(lambda e: e.tensor_tensor(mixT[:, 0:4, mc_dst:mc_dst + 128], o3, r3, ALU.mult),
                              reads=[ko, ktl], writes=hkeys("mix", mc_dst, mc_dst + 128, range(4)))
                    else:
                        P.dve(lambda e: e.tensor_tensor(mixT[:, 0:4, 0:2], o3[:, :, 126:128], r3[:, :, 126:128], ALU.mult),
                              reads=[ko, ktl], writes=hkeys("mix", 0, 2, range(4)))

            nu = len(units)
            for ui in range(min(2, nu)):
                S(ui)
            for ui in range(nu):
                V(ui)
                if ui + 2 < nu:
                    S(ui + 2)
            PSrel(*[b_[2] for b_ in bk])

        def conv_stage(mtiles, fillers=()):
            fillers = list(fillers)
            for (a, b) in mtiles:
                n = b - a
                psm, _, km = PS(hold=True)
                pse, _, ke = PS(hold=True)
                for ch in range(4):
                    psc, _, kc = PS()
                    for k in range(CONV_K):
                        dg, kdg = dgR.get()
                        if k % 2 == 0 or os.environ.get('MK_D', '1') != '1':
                            P.pool(lambda e, dg=dg, ch=ch, k=k: e.tensor_scalar(
                                dg[:], ident_f[:], col(C_CONVW + ch * 31 + k), 1.0, op0=ALU.mult, op1=ALU.mult),
                                reads=["ident_f", "cols"], writes=[kdg])
                        else:
                            P.dve(lambda e, dg=dg, ch=ch, k=k: e.tensor_scalar(
                                dg[:], ident_f[:], col(C_CONVW + ch * 31 + k), None, op0=ALU.mult),
                                reads=["ident_f", "cols"], writes=[kdg])
                        z0 = 256 + (a - 2) - 30 + k
                        P.pe(lambda e, dg=dg, ch=ch, k=k, z0=z0, n=n, psc=psc: e.matmul(
                            psc[:, 0:n], dg[:], gluT[:, ch, z0:z0 + n], start=(k == 0), stop=(k == CONV_K - 1)),
                            reads=[kdg] + K("gluT", z0, z0 + n, ch), writes=[kc])
                    P.act(lambda e, ch=ch, n=n, psc=psc: e.activation(cbuf[:, ch, 0:n], psc[:, 0:n], AF.Identity, bias=col(C_CONVB + ch)),
                          reads=[kc, "cols"], writes=[("cbuf", ch)])
                    cq, kcq = sqR.get()
                    P.act(lambda e, ch=ch, n=n, psc=psc, cq=cq: e.activation(cq[:, 0:n], psc[:, 0:n], AF.Square, bias=col(C_CONVB + ch)),
                          reads=[kc, "cols"], writes=[kcq])
                    P.pe(lambda e, ch=ch, n=n, psm=psm: e.matmul(psm[:, 0:n], onesm_f[:], cbuf[:, ch, 0:n], start=(ch == 0), stop=(ch == 3)),
                         reads=[("cbuf", ch), "onesm_f"], writes=[km])
                    P.pe(lambda e, ch=ch, n=n, pse=pse, cq=cq: e.matmul(pse[:, 0:n], ones_b[:], cq[:, 0:n], start=(ch == 0), stop=(ch == 3)),
                         reads=[kcq, "ones_b"], writes=[ke])
                    if fillers:
                        fillers.pop(0)()
                m2, k2 = tCR.get()
                P.act(lambda e, n=n, m2=m2, psm=psm: e.activation(m2[:, 0:n], psm[:, 0:n], AF.Square), reads=[km], writes=[k2])
                P.dve(lambda e, n=n, m2=m2, pse=pse: e.scalar_tensor_tensor(m2[:, 0:n], pse[:, 0:n], 1.0 / CONV_CH, m2[:, 0:n],
                                                                          op0=ALU.mult, op1=ALU.subtract),
                      reads=[ke, k2], writes=[k2])
                P.act(lambda e, n=n, m2=m2: e.activation(m2[:, 0:n], m2[:, 0:n], AF.Ln, bias=EPS), reads=[k2], writes=[k2])
                P.act(lambda e, n=n, m2=m2, pse=pse: e.activation(pse[:, 0:n], m2[:, 0:n], AF.Exp, scale=-0.5), reads=[k2, ke], writes=[ke])
                for ch in range(4):
                    t1, k1 = tAR.get()
                    P.dve(lambda e, ch=ch, n=n, t1=t1, psm=psm: e.tensor_tensor(t1[:, 0:n], cbuf[:, ch, 0:n], psm[:, 0:n], ALU.subtract),
                          reads=[("cbuf", ch), km], writes=[k1])
                    P.dve(lambda e, n=n, t1=t1, pse=pse: e.tensor_tensor(t1[:, 0:n], t1[:, 0:n], pse[:, 0:n], ALU.mult),
                          reads=[k1, ke], writes=[k1])
                    P.act(lambda e, ch=ch, n=n, t1=t1, a=a: e.activation(mixT[:, 4 + ch, a:a + n], t1[:, 0:n], AF.Silu,
                                                                         scale=col(C_LNG + ch), bias=col(C_LNB + ch)),
                          reads=[k1, "cols"], writes=K("mix", a, b, 4 + ch))
                PSrel(km, ke)
            while fillers:
                fillers.pop(0)()

        class NormAcc:
            def __init__(self, mtiles):
                self.tiles = mtiles
                self.ps = [PS(hold=True) for _ in mtiles]
                self.pend = []

            def add(self, ti, kt):
                a, b = self.tiles[ti]
                n = b - a
                sq_, ksq_ = sqR.get()
                P.act(lambda e: e.activation(sq_[:, 0:n], hT[:, kt, a:a + n], AF.Square), reads=K("hT", a, b, kt), writes=[ksq_])
                self.pend.append((ti, kt, sq_, ksq_, n))

            def flush(self, keep=0):
                while len(self.pend) > keep:
                    ti, kt, sq_, ksq_, n = self.pend.pop(0)
                    pss, _, ks = self.ps[ti]
                    P.pe(lambda e, pss=pss, sq_=sq_, n=n, kt=kt: e.matmul(pss[:, 0:n], ones_b[:], sq_[:, 0:n], start=(kt == 0), stop=(kt == 7)),
                         reads=[ksq_, "ones_b"], writes=[ks])

            def finish(self, gbase, flag_halo=False):
                self.flush()
                for ti, (a, b) in enumerate(self.tiles):
                    n = b - a
                    pss, _, ks = self.ps[ti]
                    tl, ktl = tCR.get()
                    P.act(lambda e, n=n, tl=tl, pss=pss: e.activation(tl[:, 0:n], pss[:, 0:n], AF.Ln, scale=1.0 / D_MODEL, bias=EPS),
                          reads=[ks], writes=[ktl])
                    P.act(lambda e, n=n, tl=tl, pss=pss: e.activation(pss[:, 0:n], tl[:, 0:n], AF.Exp, scale=-0.5), reads=[ktl, ks], writes=[ks])
                    for kt in range(8):
                        P.dve(lambda e, kt=kt, a=a, n=n, pss=pss: e.scalar_tensor_tensor(
                            hnT[:, kt, a:a + n], hT[:, kt, a:a + n], col(gbase + kt), pss[:, 0:n], op0=ALU.mult, op1=ALU.mult),
                            reads=K("hT", a, b, kt) + [ks, "cols"], writes=K("hnT", a, b, kt))
                        if flag_halo and a == 0:
                            P.dve(lambda e, kt=kt: e.tensor_scalar(hnT[:, kt, 0:2], hnT[:, kt, 0:2], col(C_FLAG), None, op0=ALU.mult),
                                  reads=K("hnT", 0, 2, kt) + ["cols"], writes=K("hnT", 0, 2, kt))
                    PSrel(ks)

        def wout_stage(mtiles, na):
            for i in range(4):
                slab, kw = ws_next()
                for half in range(2):
                    oc = i * 2 + half
                    for ti, (a, b) in enumerate(mtiles):
                        n = b - a
                        pso, _, ko = PS()

                        def mm(e, slab=slab, half=half, a=a, n=n, pso=pso, k0=0):
                            r = None
                            for kt in range(k0, k0 + 4):
                                r = e.matmul(pso[:, 0:n], slab[:, kt, half * 128:(half + 1) * 128], mixT[:, kt, a:a + n],
                                             start=(kt == 0), stop=(kt == 7))
                            return r
                        if os.environ.get('MK_C', '1') == '1':
                            P.pe(mm, reads=[kw] + hkeys("mix", a, b, range(4)), writes=[ko])
                            P.pe(lambda e, mm=mm: mm(e, k0=4), reads=[kw] + hkeys("mix", a, b, range(4, 8)), writes=[ko])
                        else:
                            P.pe(lambda e, mm=mm: (mm(e, k0=0), mm(e, k0=4))[1], reads=[kw] + hkeys("mix", a, b), writes=[ko])
                        na.flush(keep=len(mtiles))
                        P.dve(lambda e, oc=oc, a=a, n=n, pso=pso: e.tensor_tensor(hT[:, oc, a:a + n], hT[:, oc, a:a + n], pso[:, 0:n], ALU.add),
                              reads=[ko] + K("hT", a, b, oc), writes=K("hT", a, b, oc))
                        na.add(ti, oc)

        def ffin_stage(mtiles, final, stile=None, fillers=()):
            fillers = list(fillers)
            for j in range(NJ):
                if fillers and j % 2 == 1:
                    fillers.pop(0)()
                slab, kw = ws_next()
                if stile is not None:
                    sa, sbb = stile
                    psg, _, kg = PS()
                    psu, _, ku = PS()
                    P.pe(mm2(slab, hnT, sa, NS, psg, psu), reads=[kw] + hkeys("hnT", sa, sbb), writes=[kg, ku])
                    t1, k1 = tAR.get()
                    st3 = stT[:, j, :].rearrange("p (b k) -> p b k", k=2)
                    P.act(lambda e, j=j, t1=t1, psg=psg: e.activation(t1[:, 0:NS], psg[:, 0:NS], AF.Identity,
                                                                     scale=col(C_FCW + j * 3 + 2), bias=col(C_FCB + j)),
                          reads=[kg, "cols"], writes=[k1])
                    P.act(lambda e, j=j, psg=psg: e.activation(fgs32[:, j, :], psg[:, 0:NS], AF.Copy), reads=[kg], writes=[("fgs32", j)])
                    P.dve(lambda e, j=j, t1=t1, st3=st3: e.scalar_tensor_tensor(t1[:, 0:NS], st3[:, :, 1], col(C_FCW + j * 3 + 1), t1[:, 0:NS],
                                                                              op0=ALU.mult, op1=ALU.add),
                          reads=[("stT", j // 4), k1, "cols"], writes=[k1])
                    P.dve(lambda e, j=j, t1=t1, st3=st3: e.scalar_tensor_tensor(t1[:, 0:NS], st3[:, :, 0], col(C_FCW + j * 3 + 0), t1[:, 0:NS],
                                                                              op0=ALU.mult, op1=ALU.add),
                          reads=[("stT", j // 4), k1, "cols"], writes=[k1])
                    P.act(lambda e, t1=t1: e.activation(t1[:, 0:NS], t1[:, 0:NS], AF.Gelu), reads=[k1], writes=[k1])
                    P.dve(lambda e, j=j, sa=sa, t1=t1, psu=psu: e.tensor_tensor(actT[:, j, sa:sa + NS], psu[:, 0:NS], t1[:, 0:NS], ALU.mult),
                          reads=[ku, k1], writes=K("actT", sa, sa + NS, j))
                for ti, (a, b) in enumerate(mtiles):
                    n = b - a
                    nn = n + 2
                    psg, _, kg = PS()
                    psu, _, ku = PS()
                    if j == 0:
                        for (pst_, kst_, off) in ((psg, kg, 0), (psu, ku, 128)):
                            for k0 in (0, 4):
                                def mmh(e, slab=slab, a=a, nn=nn, pst_=pst_, off=off, k0=k0):
                                    r = None
                                    for kt in range(k0, k0 + 4):
                                        r = e.matmul(pst_[:, 0:nn], slab[:, kt, off:off + 128], hnT[:, kt, a - 2:a - 2 + nn],
                                                     start=(kt == 0), stop=(kt == 7))
                                    return r
                                P.pe(mmh, reads=[kw] + hkeys("hnT", a - 2, b, range(k0, k0 + 4)), writes=[kst_])
                    else:
                        P.pe(mm2(slab, hnT, a - 2, nn, psg, psu), reads=[kw] + hkeys("hnT", a - 2, b), writes=[kg, ku])
                    t1, k1 = tAR.get()
                    P.act(lambda e, j=j, n=n, t1=t1, psg=psg: e.activation(t1[:, 0:n], psg[:, 2:2 + n], AF.Identity,
                                                                          scale=col(C_FCW + j * 3 + 2), bias=col(C_FCB + j)),
                          reads=[kg, "cols"], writes=[k1])
                    P.dve(lambda e, j=j, n=n, t1=t1, psg=psg: e.scalar_tensor_tensor(t1[:, 0:n], psg[:, 1:1 + n], col(C_FCW + j * 3 + 1), t1[:, 0:n],
                                                                                    op0=ALU.mult, op1=ALU.add),
                          reads=[kg, k1, "cols"], writes=[k1])
                    P.dve(lambda e, j=j, n=n, t1=t1, psg=psg: e.scalar_tensor_tensor(t1[:, 0:n], psg[:, 0:n], col(C_FCW + j * 3 + 0), t1[:, 0:n],
                                                                                    op0=ALU.mult, op1=ALU.add),
                          reads=[kg, k1, "cols"], writes=[k1])
                    P.act(lambda e, n=n, t1=t1: e.activation(t1[:, 0:n], t1[:, 0:n], AF.Gelu), reads=[k1], writes=[k1])
                    P.dve(lambda e, j=j, a=a, n=n, t1=t1, psu=psu: e.tensor_tensor(actT[:, j, a:a + n], psu[:, 2:2 + n], t1[:, 0:n], ALU.mult),
                          reads=[ku, k1], writes=K("actT", a, b, j))
                    if final and ti == len(mtiles) - 1:
                        P.act(lambda e, j=j, n=n, psg=psg: e.activation(fg32[:, j, 0:2], psg[:, n:n + 2], AF.Copy),
                              reads=[kg], writes=[("fg32", j)])
            while fillers:
                fillers.pop(0)()

        def ffout_stage(mtiles, na, fillers=()):
            fillers = list(fillers)
            for oc in range(8):
                slab, kw = ws_next()
                for ti, (a, b) in enumerate(mtiles):
                    n = b - a
                    pso, _, ko = PS()

                    def mm(e, slab=slab, a=a, n=n, pso=pso):
                        r = None
                        for kt in range(NJ):
                            r = e.matmul(pso[:, 0:n], slab[:, kt, 0:128], actT[:, kt, a:a + n], start=(kt == 0), stop=(kt == NJ - 1))
                        return r
                    P.pe(mm, reads=[kw] + hkeys("actT", a, b, range(NJ)), writes=[ko])
                    na.flush(keep=len(mtiles))
                    P.dve(lambda e, oc=oc, a=a, n=n, pso=pso: e.tensor_tensor(hT[:, oc, a:a + n], hT[:, oc, a:a + n], pso[:, 0:n], ALU.add),
                          reads=[ko] + K("hT", a, b, oc), writes=K("hT", a, b, oc))
                    na.add(ti, oc)
                if fillers:
                    fillers.pop(0)()
                    if fillers and oc >= 2:
                        fillers.pop(0)()
            while fillers:
                fillers.pop(0)()

        def ple_stage(mtiles):
            for i in range(4):
                slab, kw = ws_next()
                for half in range(2):
                    oc = i * 2 + half
                    for (a, b) in mtiles:
                        n = b - a
                        psg, _, kg = PS()
                        psp, _, kp = PS()

                        def mm(e, slab=slab, half=half, a=a, n=n, psg=psg, psp=psp):
                            r = None
                            for kt in range(8):
                                r = e.matmul(psg[:, 0:n], slab[:, kt, half * 128:(half + 1) * 128], hnT[:, kt, a:a + n],
                                             start=(kt == 0), stop=(kt == 7))
                            for kt in range(2):
                                r = e.matmul(psp[:, 0:n], slab[:, 8 + kt, half * 128:(half + 1) * 128], pT[:, kt, a:a + n],
                                             start=(kt == 0), stop=(kt == 1))
                            return r
                        if oc == 0:
                            def mmp(e, slab=slab, half=half, a=a, n=n, psp=psp):
                                r = None
                                for kt in range(2):
                                    r = e.matmul(psp[:, 0:n], slab[:, 8 + kt, half * 128:(half + 1) * 128], pT[:, kt, a:a + n],
                                                 start=(kt == 0), stop=(kt == 1))
                                return r
                            P.pe(mmp, reads=[kw] + K("pT", a, b), writes=[kp])
                            for k0 in (0, 4):
                                def mmg(e, slab=slab, half=half, a=a, n=n, psg=psg, k0=k0):
                                    r = None
                                    for kt in range(k0, k0 + 4):
                                        r = e.matmul(psg[:, 0:n], slab[:, kt, half * 128:(half + 1) * 128], hnT[:, kt, a:a + n],
                                                     start=(kt == 0), stop=(kt == 7))
                                    return r
                                P.pe(mmg, reads=[kw] + hkeys("hnT", a, b, range(k0, k0 + 4)), writes=[kg])
                        else:
                            P.pe(mm, reads=[kw] + hkeys("hnT", a, b) + K("pT", a, b), writes=[kg, kp])
                        t1, k1 = tAR.get()
                        P.act(lambda e, n=n, t1=t1, psg=psg: e.activation(t1[:, 0:n], psg[:, 0:n], AF.Sigmoid), reads=[kg], writes=[k1])
                        P.dve(lambda e, n=n, t1=t1, psp=psp: e.tensor_tensor(psp[:, 0:n], psp[:, 0:n], t1[:, 0:n], ALU.mult),
                              reads=[kp, k1], writes=[kp])
                        P.dve(lambda e, oc=oc, a=a, n=n, psp=psp: e.tensor_tensor(hT[:, oc, a:a + n], hT[:, oc, a:a + n], psp[:, 0:n], ALU.add),
                              reads=[kp] + K("hT", a, b, oc), writes=K("hT", a, b, oc))

        sst = {}

        def samples_prefetch():
            P.dma("pool", ckb[:], ck_d.rearrange("b k f -> k b f"), writes=["ckb"])
            P.dma("pool", cvb[:], cv_d.rearrange("b k f -> k b f"), writes=["cvb"])

        def attn_samples_p1(zs, g):
            rq = [k for c in range(4) for k in K("qT", zs, zs + NS, c)] + K("kT", zs, zs + NS)
            if g == 0:
                sst["A"] = PS(hold=True)
                sst["B"] = PS(hold=True)
                sst["t"] = [PS(hold=True), PS(hold=True)]
            psA, _, kA = sst["A"]
            psB, _, kB = sst["B"]
            for pr in range(2):
                bqs = (4 * g + 2 * pr, 4 * g + 2 * pr + 1)
                kbs = []
                for i, bq in enumerate(bqs):
                    pst, pstb, kt_ = sst["t"][i]
                    P.pe(lambda e, bq=bq, pstb=pstb: e.transpose(pstb[:, 0:128], ckb[:, bq, :], ident_b[:]),
                         reads=["ckb", "ident"], writes=[kt_])
                    kb_, kkb = kTbR.get()
                    P.act(lambda e, kb_=kb_, pstb=pstb: e.activation(kb_[:], pstb[:, 0:128], AF.Copy), reads=[kt_], writes=[kkb])
                    kbs.append((kb_, kkb))
                for (kb_, kkb), bq in zip(kbs, bqs):
                    def scs(e, bq=bq, kb_=kb_):
                        r = None
                        for c in range(4):
                            cc = c * NS + bq
                            e.matmul(psA[:, cc:cc + 1], kb_[0:64, :], qT[0:64, c, zs + bq:zs + bq + 1], start=True, stop=True)
                            r = e.matmul(psB[:, cc:cc + 1], kb_[64:128, :], qT[64:128, c, zs + bq:zs + bq + 1], start=True, stop=True)
                        return r
                    P.pe(scs, reads=[kkb] + rq, writes=[kA, kB])
            if g < 3:
                return
            P.act(lambda e: e.activation(EAB[0][:], psA[:, 0:64], AF.Exp, scale=HD ** -0.5), reads=[kA], writes=["EA"])
            P.act(lambda e: e.activation(EAB[1][:], psB[:, 0:64], AF.Exp, scale=HD ** -0.5), reads=[kB], writes=["EB"])
            PSrel(kA, kB, sst["t"][0][2], sst["t"][1][2])
            sst["o"] = PS(hold=True)
            sst["d"] = PS(hold=True)
            pso, _, ko = sst["o"]
            psd, _, kd = sst["d"]

            def pvs(e):
                r = None
                for bq in range(NS):
                    for c in range(4):
                        cc = c * NS + bq
                        e.matmul(pso[0:64, cc:cc + 1], cvb[:, bq, 0:64], EAB[0][:, cc:cc + 1], start=True, stop=True)
                        r = e.matmul(pso[64:128, cc:cc + 1], cvb[:, bq, 64:128], EAB[1][:, cc:cc + 1], start=True, stop=True)
                e.matmul(psd[0:64, 0:64], ones_b[:, 0:64], EAB[0][:, 0:64], start=True, stop=True)
                r = e.matmul(psd[64:128, 0:64], ones_b[:, 64:128], EAB[1][:, 0:64], start=True, stop=True)
                return r
            P.pe(pvs, reads=["EA", "EB", "cvb", "ones_b"], writes=[ko, kd])

        def attn_samples_p2(zs, ms):
            rq = [k for c in range(4) for k in K("qT", zs, zs + NS, c)] + K("kT", zs, zs + NS)
            pso, _, ko = sst["o"]
            psd, _, kd = sst["d"]
            for c in range(4):
                P.dve(lambda e, c=c: e.tensor_tensor(sm["prod"][:, c * NS:(c + 1) * NS], qT[:, c, zs:zs + NS], kT[:, zs:zs + NS], ALU.mult),
                      reads=rq + ["sm_prod"], writes=["sm_prod"])
            psn, _, kn = PS()
            P.pe(lambda e: e.matmul(psn[:, 0:64], bd_f[:], sm["prod"][:], start=True, stop=True), reads=["sm_prod", "bd_f"], writes=[kn])
            P.act(lambda e: e.activation(sm["enew"][:], psn[:, 0:64], AF.Exp, scale=HD ** -0.5), reads=[kn], writes=["sm_enew"])
            for c in range(4):
                P.dve(lambda e, c=c: e.scalar_tensor_tensor(sm["dtot"][:, c * NS:(c + 1) * NS], psd[:, c * NS:(c + 1) * NS], sinkc[:, c:c + 1],
                                                            sm["enew"][:, c * NS:(c + 1) * NS], op0=ALU.add, op1=ALU.add),
                      reads=[kd, "sm_enew", "sinkc", "sm_dtot"], writes=["sm_dtot"])
            P.act(lambda e: e.activation(sm["dtot"][:], sm["dtot"][:], AF.Ln), reads=["sm_dtot"], writes=["sm_dtot"])
            P.act(lambda e: e.activation(sm["dtot"][:], sm["dtot"][:], AF.Exp, scale=-1.0), reads=["sm_dtot"], writes=["sm_dtot"])
            for c in range(4):
                P.dve(lambda e, c=c: e.tensor_tensor(sm["ev"][:, c * NS:(c + 1) * NS], sm["enew"][:, c * NS:(c + 1) * NS], vTs[:], ALU.mult),
                      reads=["sm_enew", "vTs", "sm_ev"], writes=["sm_ev"])
            P.dve(lambda e: e.tensor_tensor(sm["otot"][:], pso[:, 0:64], sm["ev"][:], ALU.add), reads=[ko, "sm_ev"], writes=["sm_otot"])
            P.dve(lambda e: e.tensor_tensor(mixT[:, 0:4, ms:ms + NS], sm["otot"][:].rearrange("p (a b) -> p a b", a=4),
                                            sm["dtot"][:].rearrange("p (a b) -> p a b", a=4), ALU.mult),
                  reads=["sm_otot", "sm_dtot"], writes=hkeys("mix", ms, ms + NS, range(4)))
            PSrel(ko, kd)

        cst_ = {}

        if with_samples:
            sext = actT[:, 0:8, :].rearrange("p a b -> p (a b)").bitcast(F32)[0:124, 0:2048].rearrange("p (g c) -> p g c", g=4)
            SK = [k for j in range(8) for k in K("actT", 0, MW, j)]
            sdummy = sb("sdummy", (128, 1))

        def sext_claim():
            P.pool(lambda e: e.memset(sdummy[:], 0.0), writes=SK + [("sext", g) for g in range(4)])

        def sext_release():
            P.pool(lambda e: e.memset(sdummy[:], 0.0), reads=[("sext", g) for g in range(4)] + [("sextr", g, b4) for g in range(4) for b4 in range(4)], writes=SK)

        cst_ = {}

        def samples_prefetch2():
            sext_claim()
            for g in range(4):
                for bl_ in range(4):
                    bq = 4 * g + bl_
                    P.dma("sp", sext[bl_ * 30:bl_ * 30 + 30, g, :], sc_d[bq, :, :], reads=[("sext", g)], writes=[("sextr", g, bl_)])
            for bl_ in range(4):
                P.dma("sp", wrep[bl_ * 30:bl_ * 30 + 30, :], cwt_d[0:30, :], writes=[("wrep", bl_)])
            for i in range(3):
                P.dma("sp", crows[32 * i:32 * i + 1, :], cbt_d[i:i + 1, :], writes=[("crow", i)])
            P.dma("sp", cst[0:1, :], cwt_d[30:31, :], writes=["cst"])
            P.pool(lambda e: e.memset(onesf16[:], 1.0), writes=["onesf16"])
            P.dve(lambda e: e.tensor_scalar(cs_col32[:, 0:1], cols2[:, 3:4], 30.0, None, op0=ALU.is_ge), reads=["c2_3"], writes=["cc0"])
            P.dve(lambda e: e.tensor_scalar(cs_col32[:, 1:2], cols2[:, 3:4], 60.0, None, op0=ALU.is_ge), reads=["c2_3"], writes=["cc1"])
            P.dve(lambda e: e.tensor_scalar(cs_col32[:, 2:3], cols2[:, 3:4], 90.0, None, op0=ALU.is_ge), reads=["c2_3"], writes=["cc2"])
            P.dve(lambda e: e.tensor_tensor(cs_col32[:, 0:1], cs_col32[:, 0:1], cs_col32[:, 1:2], ALU.add), reads=["cc0", "cc1"], writes=["cc0"])
            P.dve(lambda e: e.tensor_tensor(cs_col32[:, 0:1], cs_col32[:, 0:1], cs_col32[:, 2:3], ALU.add), reads=["cc0", "cc2"], writes=["cc0"])
            for g in range(4):
                P.dve(lambda e, g=g: e.tensor_scalar(cs_col32[:, 3 + g:4 + g], cs_col32[:, 0:1], float(4 * g), None, op0=ALU.add),
                      reads=["cc0"], writes=[("ccg", g)])
                P.dve(lambda e, g=g: e.tensor_scalar(selT[0:120, g, :], iot_f[0:120, 0:NS], cs_col32[0:120, 3 + g:4 + g], None, op0=ALU.is_equal),
                      reads=["iot_f", ("ccg", g)], writes=[("selT", g)])

        def conv_samples_a(ms):
            pst, _, kpt_ = PS()

            def trg_(e):
                r = None
                for c in range(4):
                    r = e.transpose(pst[0:NS, c * 128:(c + 1) * 128], gs32[:, c, :], ident_f[:])
                return r
            P.pe(trg_, reads=[("gs32", c) for c in range(4)] + ["ident_f"], writes=[kpt_])
            P.dve(lambda e: e.tensor_copy(gs_tok[:], pst[0:NS, :]), reads=[kpt_], writes=["gs_tok"])
            P.dma("sp", nconvs[:, 29, :], gs_tok[:], reads=["gs_tok"])
            P.dma("sp", nconvs[:, 0:29, :], sc_d[:, 1:30, :])

        def conv_samples_b(ms):
            psc, _, kc = PS(hold=True)
            for g in range(4):
                P.dve(lambda e, g=g: e.tensor_tensor(sext[0:120, g, :], sext[0:120, g, :], wrep[0:120, :], ALU.mult),
                      reads=[("sext", g)] + [("sextr", g, b4) for b4 in range(4)] + [("wrep", b4) for b4 in range(4)], writes=[("sext", g)])
                P.pe(lambda e, g=g: e.matmul(psc[0:NS, :], selT[0:120, g, :], sext[0:120, g, :], start=(g == 0), stop=False),
                     reads=[("sext", g), ("selT", g)], writes=[kc])
            psw, _, kw_ = PS()
            P.pe(lambda e: e.matmul(psw[0:NS, :], onesf16[0:1, :], cst[0:1, :], start=True, stop=True), reads=["cst", "onesf16"], writes=[kw_])
            p30t, kp30 = tAR.get()
            P.dve(lambda e: e.tensor_tensor(p30t[0:NS, :], gs_tok[:], psw[0:NS, :], ALU.mult), reads=["gs_tok", kw_], writes=[kp30])
            P.pe(lambda e: e.matmul(psc[0:NS, :], ident_f[0:NS, 0:NS], p30t[0:NS, :], start=False, stop=False),
                 reads=[kp30, "ident_f"], writes=[kc])
            P.pe(lambda e: e.matmul(psc[0:NS, :], onesf16[0:1, :], crows[0:1, :], start=False, stop=True),
                 reads=[("crow", 0), "onesf16"], writes=[kc])
            psg_, _, kg_ = PS(hold=True)
            psb_, _, kb_ = PS(hold=True)
            P.pe(lambda e: e.matmul(psg_[0:NS, :], onesf16[32:33, :], crows[32:33, :], start=True, stop=True), reads=[("crow", 1), "onesf16"], writes=[kg_])
            P.pe(lambda e: e.matmul(psb_[0:NS, :], onesf16[64:65, :], crows[64:65, :], start=True, stop=True), reads=[("crow", 2), "onesf16"], writes=[kb_])
            cst_.update(psc=psc, kc=kc, psg_=psg_, kg_=kg_, psb_=psb_, kb_=kb_)

        def conv_samples_c(ms):
            cs_tok_t, kcs1 = tAR.get()
            cs_t2_t, kcs2 = tAR.get()
            cs_tok = cs_tok_t[0:NS, :]
            cs_t2 = cs_t2_t[0:NS, :]
            psc, kc, psg_, kg_, psb_, kb_ = cst_["psc"], cst_["kc"], cst_["psg_"], cst_["kg_"], cst_["psb_"], cst_["kb_"]
            P.act(lambda e: e.activation(cs_tok, psc[0:NS, :], AF.Identity, accum_out=cs_col[:, 0:1]), reads=[kc], writes=[kcs1, "csc0"])
            P.dve(lambda e: e.tensor_scalar(cs_col[:, 1:2], cs_col[:, 0:1], -1.0 / CONV_CH, None, op0=ALU.mult), reads=["csc0"], writes=["csc1"])
            P.act(lambda e: e.activation(cs_t2, cs_tok, AF.Square, bias=cs_col[:, 1:2], accum_out=cs_col[:, 2:3]),
                  reads=[kcs1, "csc1"], writes=[kcs2, "csc2"])
            P.act(lambda e: e.activation(cs_col[:, 3:4], cs_col[:, 2:3], AF.Ln, scale=1.0 / CONV_CH, bias=EPS), reads=["csc2"], writes=["csc3"])
            P.act(lambda e: e.activation(cs_col[:, 4:5], cs_col[:, 3:4], AF.Exp, scale=-0.5), reads=["csc3"], writes=["csc4"])
            P.dve(lambda e: e.tensor_scalar(cs_tok, cs_tok, cs_col[:, 1:2], cs_col[:, 4:5], op0=ALU.add, op1=ALU.mult),
                  reads=[kcs1, "csc1", "csc4", kcs2], writes=[kcs1])
            P.dve(lambda e: e.tensor_tensor(cs_tok, cs_tok, psg_[0:NS, :], ALU.mult), reads=[kcs1, kg_], writes=[kcs1])
            P.dve(lambda e: e.tensor_tensor(cs_tok, cs_tok, psb_[0:NS, :], ALU.add), reads=[kcs1, kb_], writes=[kcs1])
            P.act(lambda e: e.activation(cs_t2, cs_tok, AF.Silu), reads=[kcs1, kcs2], writes=[kcs2])
            pso_, _, ko_ = PS()

            def tro(e):
                r = None
                for c in range(4):
                    r = e.transpose(pso_[:, c * NS:(c + 1) * NS], cs_t2_t[0:NS, c * 128:(c + 1) * 128], ident_f[0:NS, 0:NS])
                return r
            P.pe(tro, reads=[kcs2, "ident_f"], writes=[ko_])
            P.dve(lambda e: e.tensor_copy(mixT[:, 4:8, ms:ms + NS], pso_[:, 0:4 * NS].rearrange("p (a b) -> p a b", a=4)),
                  reads=[ko_], writes=hkeys("mix", ms, ms + NS, range(4, 8)))
            PSrel(kc, kg_, kb_)

        def build_stT():
            sfv = sf_d.rearrange("b k f -> (b k) f")
            for g in range(6):
                j0 = g * 4
                j1 = min(NJ, j0 + 4)
                w_ = (j1 - j0) * 128
                slt, ksl = tAR.get()
                sl = slt[0:2 * NS, :]
                P.dma("sp", sl[:, 0:w_], sfv[:, j0 * 128:j1 * 128], writes=[ksl])
                pst, _, kpt_ = PS()

                def trs(e, j0=j0, j1=j1, sl=sl, pst=pst):
                    r = None
                    for j in range(j0, j1):
                        r = e.transpose(pst[:, (j - j0) * 32:(j - j0 + 1) * 32], sl[:, (j - j0) * 128:(j - j0 + 1) * 128], ident_f[0:32, 0:32])
                    return r
                P.pe(trs, reads=[ksl, "ident_f"], writes=[kpt_])
                P.dve(lambda e, j0=j0, j1=j1, pst=pst: e.tensor_copy(
                    stT[:, j0:j1, :], pst[:, 0:(j1 - j0) * 32].rearrange("p (a b) -> p a b", a=j1 - j0)),
                    reads=[kpt_], writes=[("stT", g)])
            P.dma("sp", nffns[:, 0, :], sf_d[:, 1, :])

        def sample_state_outputs():
            P.dma("sp", nks[:, 0:127, :], ck_d[:, 1:128, :])
            P.dma("sp", nvs[:, 0:127, :], cv_d[:, 1:128, :])
            P.dma("sp", nvs[:, 127, :], vs_tok[:], reads=["vs_tok"])
            psk_, _, kk_ = PS()
            P.pe(lambda e: e.transpose(psk_[0:NS, 0:128], ks32[:], ident_f[:]), reads=["ks32", "ident_f"], writes=[kk_])
            ot, ko = outR.get()
            P.dve(lambda e: e.tensor_copy(ot[0:NS, :], psk_[0:NS, 0:128]), reads=[kk_], writes=[ko])
            P.dma("sp", nks[:, 127, :], ot[0:NS, :], reads=[ko])
            for g in range(6):
                j0 = g * 4
                j1 = min(NJ, j0 + 4)
                psf, _, kpf = PS()

                def trg2(e, j0=j0, j1=j1, psf=psf):
                    r = None
                    for j in range(j0, j1):
                        r = e.transpose(psf[0:NS, (j - j0) * 128:(j - j0 + 1) * 128], fgs32[:, j, :], ident_f[:])
                    return r
                P.pe(trg2, reads=[("fgs32", j) for j in range(j0, j1)] + ["ident_f"], writes=[kpf])
                ft, kft = tAR.get()
                P.dve(lambda e, j0=j0, j1=j1, psf=psf, ft=ft: e.tensor_copy(ft[0:NS, 0:(j1 - j0) * 128], psf[0:NS, 0:(j1 - j0) * 128]),
                      reads=[kpf], writes=[kft])
                P.dma("sp", nffns[:, 1, j0 * 128:j1 * 128], ft[0:NS, 0:(j1 - j0) * 128], reads=[kft])

        def load_p_block(src, n, mc):
            pt, kpt = ptR.get()
            P.dma("pool", pt[0:n, :], src, writes=[kpt])
            ps, psb, kp = PS()

            def tr(e):
                e.transpose(psb[:, 0:n], pt[0:n, 0:128], ident_b[0:n, 0:n])
                return e.transpose(psb[:, 128:128 + n], pt[0:n, 128:256], ident_b[0:n, 0:n])
            P.pe(tr, reads=[kpt, "ident"], writes=[kp])
            P.act(lambda e: e.activation(pT[:, 0:2, mc:mc + n], psb[:, 0:256].rearrange("p (a b) -> p a b", a=2)[:, :, 0:n], AF.Copy),
                  reads=[kp], writes=K("pT", mc, mc + n))

        def out_block(dst, n, mc):
            yt, ky = ytR.get()
            for half in range(2):
                ps2, _, kp2 = PS()

                def trf(e, half=half, ps2=ps2):
                    r = None
                    for a4 in range(4):
                        kt = half * 4 + a4
                        r = e.transpose(ps2[0:n, a4 * 128:(a4 + 1) * 128], hT[:, kt, mc:mc + n], ident_f[:])
                    return r
                P.pe(trf, reads=hkeys("hT", mc, mc + n, range(half * 4, half * 4 + 4)) + ["ident_f"], writes=[kp2])
                if half == 0:
                    P.act(lambda e, ps2=ps2: e.activation(yt[0:n, 0:512], ps2[0:n, :], AF.Copy), reads=[kp2], writes=[ky])
                else:
                    P.dve(lambda e, ps2=ps2: e.tensor_copy(yt[0:n, 512:1024], ps2[0:n, :]), reads=[kp2, ky], writes=[ky])
            P.dma("sp", dst, yt[0:n, :], reads=[ky])

        pend_out = []

        def s0a_list(pi, pipelined=True):
            pipelined = pipelined and os.environ.get('MK_A', '1') == '1'
            nb = PASS_BLOCKS[pi]
            s0_ = sum(PASS_BLOCKS[:pi]) * 128
            M = nb * 128
            last = pi == NPASS - 1
            blocks = []
            if pi == 0:
                blocks += [(xh[0:128, :], 128, 0), (xh[128:256, :], 128, 128)]
            for bl in range(nb):
                r0 = 256 + s0_ + bl * 128
                blocks.append((xh[r0:r0 + 128, :], 128, 256 + bl * 128))
            if last and with_samples:
                blocks.append((xs_d, NS, 256 + M))
            fl = []
            if not pipelined:
                fl.append(lambda: rope_tables(pi, s0_, M, last))
                for (src, n, zc) in blocks:
                    fl.append(lambda src=src, n=n, zc=zc: stage0a(src, n, zc))
                return fl
            st_ = {}

            def step(i):
                if i < len(blocks):
                    src, n, zc = blocks[i]
                    st_[i] = stage0a_front(src, n)
                if i - 1 >= 0:
                    src, n, zc = blocks[i - 1]
                    xb, kxb = st_.pop(i - 1)
                    stage0a_back(xb, kxb, n, zc)
            for i in range(len(blocks) + 1):
                fl.append(lambda i=i: step(i))
            fl.append(lambda: rope_tables(pi, s0_, M, last))
            return fl

        def run_pass(pi, nb, s0):
            while ws["used"] < pi * 48:
                ws_next()
            M = nb * 128
            last = pi == NPASS - 1
            samp = last and with_samples
            zs, ms = 256 + M, 2 + M
            if pi == 0:
                for f in s0a_list(0, pipelined=True):
                    f()
            ztiles = split_tiles(0 if pi == 0 else 256, 256 + M)
            fl = list(pend_out)
            del pend_out[:]
            if samp:
                samples_prefetch()
                samples_prefetch2()
                while len(fl) < 4:
                    fl.append(lambda: None)
                for g in range(4):
                    fl.append(lambda g=g: attn_samples_p1(zs, g))
            z_stage(pi, ztiles, last, stile=(zs, zs + NS) if samp else None, fillers=fl)
            blocks = [(1, 0, True, False)] if pi == 0 else []
            for bl in range(nb):
                blocks.append((2 + bl, 2 + bl * 128, False, pi == 0 and bl == 0))
            attn_pass(blocks, nob=1 if samp else 2)
            m_lo = 0 if pi == 0 else 2
            mt_all = split_tiles(m_lo, 2 + M)
            mt_main = split_tiles(2, 2 + M)
            fl = []
            if pi == 0:
                fl.append(lambda: stage0b(xh[128:256, :], 128, None, halo2=True))
            for bl in range(nb):
                r0 = 256 + s0 + bl * 128
                fl.append(lambda r0=r0, bl=bl: stage0b(xh[r0:r0 + 128, :], 128, 2 + bl * 128))
                fl.append(lambda bl=bl: load_p_block(pp[s0 + bl * 128:s0 + (bl + 1) * 128, :], 128, 2 + bl * 128))
            if samp:
                flp = list(fl)
                fl = [lambda: (conv_samples_a(ms), attn_samples_p2(zs, ms), flp[0](), flp[1]()),
                      lambda: (conv_samples_b(ms), flp[2](), flp[3](), stage0b(xs_d, NS, ms)),
                      lambda: (conv_samples_c(ms), sext_release(), flp[4](), flp[5](), load_p_block(ps_d, NS, ms)),
                      lambda: (flp[6](), flp[7](), build_stT())]
            conv_stage(mt_all, fillers=fl)
            if samp:
                mt_all = mt_all + [(ms, ms + NS)]
                mt_main = mt_main + [(ms, ms + NS)]
            na = NormAcc(mt_all)
            wout_stage(mt_all, na)
            na.finish(C_GFFN, flag_halo=(pi == 0))
            if pi > 0:
                P.pool(lambda e: e.tensor_copy(hnT[:, :, 0:2], hcarry[:]), reads=["hcarry"] + hkeys("hnT", 0, 2),
                       writes=hkeys("hnT", 0, 2))
            if not last:
                P.pool(lambda e, M=M: e.tensor_copy(hcarry[:], hnT[:, :, M:M + 2]),
                       reads=hkeys("hnT", M, M + 2), writes=["hcarry"])
            if not last:
                P.pool(lambda e, M=M: e.tensor_copy(kT[:, 128:256], kT[:, 128 + M:256 + M]),
                       reads=K("kT", 128 + M, 256 + M), writes=K("kT", 128, 256))
                P.pool(lambda e, nb=nb: e.tensor_copy(vS[:, 1, :], vS[:, 1 + nb, :]),
                       reads=K("vS", 128 + M, 256 + M), writes=K("vS", 128, 256))
                P.pool(lambda e, M=M: e.tensor_copy(gluT[:, :, 224:256], gluT[:, :, 224 + M:256 + M]),
                       reads=hkeys("gluT", 224 + M, 256 + M, range(4)), writes=hkeys("gluT", 224, 256, range(4)))
            fl = s0a_list(pi + 1) if not last else ([sample_state_outputs] if samp else [])
            ffin_stage(split_even(2, 2 + M, 510), last, stile=(ms, ms + NS) if samp else None)
            na = NormAcc(mt_main)
            ffout_stage(mt_main, na, fillers=fl)
            na.finish(C_GPLE)
            ple_stage(mt_main)
            outs = [(lambda bl=bl: out_block(y[s0 + bl * 128:s0 + (bl + 1) * 128, :], 128, 2 + bl * 128)) for bl in range(nb)]
            if last:
                for f in outs:
                    f()
                if samp:
                    out_block(ys, NS, ms)
            else:
                pend_out.extend(outs)

        s0 = 0
        for pi, nb in enumerate(PASS_BLOCKS):
            if pi < npass_run:
                run_pass(pi, nb, s0)
            s0 += nb * 128

        if stop < 99 or npass_run < 99:
            if debug:
                dbg = nc.dram_tensor("dbg", [8, 128, 1024], F32, kind="ExternalOutput").ap()
                dt_ = sb("dbgt", (128, 1024))
                for i, (src, kk) in enumerate(debug_srcs(locals())):
                    P.dve(lambda e, src=src: e.tensor_copy(dt_[:, 0:src.shape[1]], src), reads=kk + [("dbgt",)], writes=[("dbgt",)])
                    P.dma("sp", dbg[i, :, 0:src.shape[1]], dt_[:, 0:src.shape[1]], reads=[("dbgt",)], writes=[("dbgt",)])
            P.finalize(st)
            nc._mk_stats = P.stats
            return nc

        psk, _, kpk = PS()
        P.pe(lambda e: e.transpose(psk[:, 0:128], kf32[:], ident_f[:]), reads=["kf32", "ident_f"], writes=[kpk])
        ot, ko = outR.get()
        P.dve(lambda e: e.tensor_copy(ot[:], psk[:, 0:128]), reads=[kpk], writes=[ko])
        P.dma("sp", nk, ot[:], reads=[ko])
        psc, _, kpc = PS()

        def trc(e):
            r = None
            for c in range(4):
                r = e.transpose(psc[0:32, c * 128:(c + 1) * 128], gl32[:, c, :], ident_f[:])
            return r
        P.pe(trc, reads=[("gl32", c) for c in range(4)] + ["ident_f"], writes=[kpc])
        P.dve(lambda e: e.tensor_copy(cst[:], psc[0:32, :]), reads=[kpc], writes=["cst"])
        P.dma("sp", nconv, cst[2:32, :], reads=["cst"])
        for g in range(6):
            j0 = g * 4
            j1 = min(NJ, j0 + 4)
            psf, _, kpf = PS()

            def trg(e, j0=j0, j1=j1, psf=psf):
                r = None
                for j in range(j0, j1):
                    r = e.transpose(psf[0:2, (j - j0) * 128:(j - j0 + 1) * 128], fg32[:, j, :], ident_f[:])
                return r
            P.pe(trg, reads=[("fg32", j) for j in range(j0, j1)] + ["ident_f"], writes=[kpf])
            ft, kft = tAR.get()
            P.dve(lambda e, j0=j0, j1=j1, psf=psf, ft=ft: e.tensor_copy(ft[0:2, 0:(j1 - j0) * 128], psf[0:2, 0:(j1 - j0) * 128]),
                  reads=[kpf], writes=[kft])
            P.dma("sp", nffn[:, j0 * 128:j1 * 128], ft[0:2, 0:(j1 - j0) * 128], reads=[kft])

        P.finalize(st)
        nc._mk_stats = P.stats
    return nc


def _prep_weights(inp):
    w_in = np.asarray(inp["w_in"][0])
    rot = np.concatenate([np.arange(32, 64), np.arange(0, 32)])
    cols_list = []
    for c in range(4):
        for hrot in (False, True):
            cc = []
            for h in (c, c + 4):
                base = h * HD
                cc.append(base + (rot if hrot else np.arange(HD)))
            cols_list.append(np.concatenate(cc))
    kbase = 512
    cols_list.append(np.concatenate([kbase + np.arange(64), kbase + 64 + np.arange(64)]))
    cols_list.append(np.concatenate([kbase + rot, kbase + 64 + rot]))
    for c in range(4):
        cols_list.append(768 + c * 128 + np.arange(128))
        cols_list.append(768 + 512 + c * 128 + np.arange(128))
    perm = np.concatenate(cols_list)
    w_in_p = np.ascontiguousarray(w_in[:, perm])
    w_v = np.ascontiguousarray(w_in[:, 640:768])
    w_out = np.asarray(inp["w_out"][0])
    rows = []
    for c in range(4):
        rows.append(c * HD + np.arange(HD))
        rows.append((c + 4) * HD + np.arange(HD))
    rows.append(512 + np.arange(512))
    w_out_p = np.ascontiguousarray(w_out[np.concatenate(rows), :])
    w_ffi = np.asarray(inp["w_ffn_in"][0])
    idx = []
    for j in range(NJ):
        idx.append(j * 128 + np.arange(128))
        idx.append(D_FF + j * 128 + np.arange(128))
    w_ffi_p = np.ascontiguousarray(w_ffi[:, np.concatenate(idx)])
    w_ffo = np.ascontiguousarray(np.asarray(inp["w_ffn_out"][0]))
    w_ple = np.ascontiguousarray(np.concatenate([np.asarray(inp["w_ple_gate"][0]), np.asarray(inp["w_ple_proj"][0])], axis=0))
    return w_in_p, w_v, w_out_p, w_ffi_p, w_ffo, w_ple


def _prep_cols(inp, flag, posb):
    c = np.zeros((128, NCOL), np.float32)
    p = np.arange(128)
    d = p % HD
    c[:, C_GMIX:C_GMIX + 8] = np.asarray(inp["norm_mix_g"][0]).reshape(8, 128).T
    c[:, C_GFFN:C_GFFN + 8] = np.asarray(inp["norm_ffn_g"][0]).reshape(8, 128).T
    c[:, C_GPLE:C_GPLE + 8] = np.asarray(inp["norm_ple_g"][0]).reshape(8, 128).T
    gq = np.asarray(inp["q_norm_g"][0])
    gk = np.asarray(inp["k_norm_g"][0])
    c[:, C_GQ] = gq[d]
    c[:, C_GQR] = gq[(d + 32) % HD]
    c[:, C_GK] = gk[d]
    c[:, C_GKR] = gk[(d + 32) % HD]
    c[:, C_SGN] = np.where(d < 32, -1.0, 1.0)
    c[:, C_CONVB:C_CONVB + 4] = np.asarray(inp["conv_b"][0]).reshape(4, 128).T
    c[:, C_LNG:C_LNG + 4] = np.asarray(inp["conv_ln_g"][0]).reshape(4, 128).T
    c[:, C_LNB:C_LNB + 4] = np.asarray(inp["conv_ln_b"][0]).reshape(4, 128).T
    c[:, C_FLAG] = flag
    sk = np.asarray(inp["attn_sinks"][0])
    for cc in range(4):
        c[0:64, C_SINKC + cc] = sk[cc]
        c[64:128, C_SINKC + cc] = sk[cc + 4]
    c[:, C_INVF] = (np.float32(10000.0) ** (-(np.arange(128) % 32).astype(np.float32) / np.float32(32.0))).astype(np.float32)
    c[:, C_POSB] = posb
    c[:, C_FCB:C_FCB + NJ] = np.asarray(inp["ffn_conv_b"][0]).reshape(NJ, 128).T
    fw = np.asarray(inp["ffn_conv_w"][0])
    c[:, C_FCW:C_FCW + NJ * 3] = fw.reshape(3, NJ, 128).transpose(2, 1, 0).reshape(128, NJ * 3)
    cw = np.asarray(inp["conv_w"][0])
    c[:, C_CONVW:C_CONVW + 4 * 31] = cw.reshape(31, 4, 128).transpose(2, 1, 0).reshape(128, 4 * 31)
    return c


_NC_CACHE = {}


def kernel(**inp):
    inp = {k: np.asarray(v) for k, v in inp.items()}
    with_samples = True
    key = ("nc", with_samples)
    if key not in _NC_CACHE:
        _NC_CACHE[key] = build_nc(with_samples=with_samples)
    nc = _NC_CACHE[key]
    w_in_p, w_v, w_out_p, w_ffi_p, w_ffo, w_ple = _prep_weights(inp)
    xp = inp["x_prompt"]
    ppr = inp["p_prompt"][0]
    sinks = inp["attn_sinks"][0]
    sinkrow = np.zeros((1, 512), np.float32)
    for c in range(4):
        sinkrow[0, c * 128:c * 128 + 64] = sinks[c]
        sinkrow[0, c * 128 + 64:c * 128 + 128] = sinks[c + 4]
    jj = np.arange(128)
    mprev = (jj[:, None] >= jj[None, :]).astype(np.float32)
    in_maps = []
    for core in range(8):
        b, half = core // 2, core % 2
        if half == 0:
            xhh = np.concatenate([np.zeros((256, D_MODEL), np.float32), xp[b, 0:2048]], axis=0)
            m0 = np.zeros((128, 128), np.float32)
        else:
            xhh = np.ascontiguousarray(xp[b, 2048 - 256:4096])
            m0 = mprev
        cols = _prep_cols(inp, float(half), float(half * 2048))
        m = {
            "xh": np.ascontiguousarray(xhh), "pp": np.ascontiguousarray(ppr[b, half * 2048:(half + 1) * 2048]),
            "w_in": w_in_p, "w_v": w_v, "w_out": w_out_p, "w_ffi": w_ffi_p, "w_ffo": w_ffo, "w_ple": w_ple,
            "cols": cols, "sinkrow": sinkrow, "mask0": m0,
        }
        if with_samples:
            sl = slice(core * NS, (core + 1) * NS)
            m["xs"] = np.ascontiguousarray(inp["x_sample"][sl, 0, :])
            m["ps"] = np.ascontiguousarray(inp["p_sample"][0, sl, 0, :])
            m["ck"] = np.ascontiguousarray(inp["state_attn_k"][0, sl].reshape(NS, 128, 128))
            m["cv"] = np.ascontiguousarray(inp["state_attn_v"][0, sl].reshape(NS, 128, 128))
            m["sc"] = np.ascontiguousarray(inp["state_conv"][0, sl])
            m["sf"] = np.ascontiguousarray(inp["state_ffn_conv"][0, sl])
            m["cwt"] = np.ascontiguousarray(inp["conv_w"][0])
            m["cbt"] = np.ascontiguousarray(np.stack([inp["conv_b"][0], inp["conv_ln_g"][0], inp["conv_ln_b"][0],
                                                      np.zeros(CONV_CH, np.float32)], axis=0))
        in_maps.append(m)
    res = run_bass_kernel_spmd(nc, in_maps, core_ids=list(range(8)))
    R = res.results
    y_prompt = np.zeros((4, SEQ, D_MODEL), np.float32)
    nk = np.zeros((1, 4, WINDOW, 2, HD), np.float32)
    nv = np.zeros((1, 4, WINDOW, 2, HD), np.float32)
    ncv = np.zeros((1, 4, CONV_K - 1, CONV_CH), np.float32)
    nff = np.zeros((1, 4, 2, D_FF), np.float32)
    for core in range(8):
        b, half = core // 2, core % 2
        y_prompt[b, half * 2048:(half + 1) * 2048] = R[core]["y"]
        if half == 1:
            nk[0, b] = R[core]["nk"].reshape(WINDOW, 2, HD)
            nv[0, b] = R[core]["nv"].reshape(WINDOW, 2, HD)
            ncv[0, b] = R[core]["nconv"]
            nff[0, b] = R[core]["nffn"]
    y_sample = np.zeros((128, 1, D_MODEL), np.float32)
    nks = np.zeros((1, 128, WINDOW, 2, HD), np.float32)
    nvs = np.zeros((1, 128, WINDOW, 2, HD), np.float32)
    ncs = np.zeros((1, 128, CONV_K - 1, CONV_CH), np.float32)
    nfs = np.zeros((1, 128, 2, D_FF), np.float32)
    if with_samples:
        for core in range(8):
            sl = slice(core * NS, (core + 1) * NS)
            y_sample[sl, 0] = R[core]["ys"]
            nks[0, sl] = R[core]["nks"].reshape(NS, WINDOW, 2, HD)
            nvs[0, sl] = R[core]["nvs"].reshape(NS, WINDOW, 2, HD)
            ncs[0, sl] = R[core]["nconvs"]
            nfs[0, sl] = R[core]["nffns"]
    return (y_prompt, y_sample, nk, nv, ncv, nff, nks, nvs, ncs, nfs)
```
